# Optimizing a Trainium2 kernel written in Bass

```python
import functools
import numpy as np
import jax, jax.numpy as jnp
from jax import lax

D_MODEL = 2048
BATCH = 4
SEQ = 2048
DEPTH = 1
DEC_BATCH = 32
DEC_SEQ = 8
PAST_LEN = 8192
PAGE_SIZE = 128

GDN_QK_HEADS = 16
GDN_V_HEADS = 32
GDN_DK = 128
GDN_DV = 128
CONV_W = 4
GDN_CHUNK = 64
ATT_HEADS = 16
ATT_KV_HEADS = 2
ATT_DH = 128
IDX_HEADS = 16
IDX_DH = 128
TOPK_MAX = 256
Q_BLOCK = 128
NORM_EPS = 1e-6
L2_EPS = 1e-6

A_QK = GDN_QK_HEADS * GDN_DK
A_V = GDN_V_HEADS * GDN_DV
A_CONV_CH = 2 * A_QK + A_V
B_Q = ATT_HEADS * ATT_DH
B_KV = ATT_KV_HEADS * ATT_DH
IDX_Q = IDX_HEADS * IDX_DH
SPLIT_SIZES = (A_CONV_CH, A_V, GDN_V_HEADS, GDN_V_HEADS, B_Q, B_KV, B_KV, B_Q, IDX_Q, IDX_DH, IDX_HEADS, D_MODEL, D_MODEL)
D_IN = sum(SPLIT_SIZES)

kernel_name = 'hybrid_gdn_dsa_decoder_step'


def rms_norm(x, g):
    xf = x.astype(jnp.float32)
    y = xf * lax.rsqrt(jnp.mean(xf * xf, axis=-1, keepdims=True) + NORM_EPS)
    return (y * g.astype(jnp.float32)).astype(x.dtype)


def l2_normalize(x):
    xf = x.astype(jnp.float32)
    return xf * lax.rsqrt(jnp.sum(xf * xf, axis=-1, keepdims=True) + L2_EPS)


def split_columns(t):
    offsets = np.cumsum(SPLIT_SIZES)[:-1].tolist()
    return jnp.split(t, offsets, axis=-1)


def causal_short_conv(x, buf, w):
    L = x.shape[1]
    xp = jnp.concatenate([buf.astype(x.dtype), x], axis=1)
    y = sum(xp[:, i:i + L] * w[i] for i in range(CONV_W))
    return jax.nn.silu(y), xp[:, xp.shape[1] - (CONV_W - 1):]


def gated_delta_rule(q, k, v, g, beta, s0):
    B, L, H, DK = q.shape
    DV = v.shape[-1]
    C = min(GDN_CHUNK, L)
    n = -(-L // C)
    pad = n * C - L

    def blocks(t):
        t = jnp.moveaxis(t, 2, 1)
        t = jnp.pad(t, [(0, 0), (0, 0), (0, pad)] + [(0, 0)] * (t.ndim - 3))
        return t.reshape(B, H, n, C, *t.shape[3:])

    q, k, v, g, beta = (blocks(t) for t in (q, k, v, g, beta))
    G = jnp.cumsum(g, axis=-1)
    incl = jnp.tril(jnp.ones((C, C), bool))
    strict = jnp.tril(jnp.ones((C, C), bool), -1)
    decay = jnp.exp(jnp.where(incl, G[..., :, None] - G[..., None, :], -jnp.inf))
    kb = k * beta[..., None]
    a_strict = jnp.where(strict, jnp.einsum('bhnid,bhnjd->bhnij', kb, k) * decay, 0.0)
    rhs = jnp.concatenate([v * beta[..., None], kb * jnp.exp(G)[..., None]], axis=-1)
    sol = lax.linalg.triangular_solve(a_strict, rhs, left_side=True, lower=True, unit_diagonal=True)
    u_intra, w_state = sol[..., :DV], sol[..., DV:]
    qk = jnp.einsum('bhnid,bhnjd->bhnij', q, k) * decay
    q_dec = q * jnp.exp(G)[..., None]
    k_tail = k * jnp.exp(G[..., -1:] - G)[..., None]
    g_tail = jnp.exp(G[..., -1])

    def chunk_step(s, xs):
        u_c, w_c, qk_c, qd_c, kt_c, gt_c = xs
        u = u_c - jnp.einsum('bhck,bhkv->bhcv', w_c, s)
        o = jnp.einsum('bhck,bhkv->bhcv', qd_c, s) + jnp.einsum('bhij,bhjv->bhiv', qk_c, u)
        s = s * gt_c[..., None, None] + jnp.einsum('bhck,bhcv->bhkv', kt_c, u)
        return s, o

    xs = tuple(jnp.moveaxis(t, 2, 0) for t in (u_intra, w_state, qk, q_dec, k_tail, g_tail))
    s, o = lax.scan(chunk_step, s0, xs)
    o = jnp.moveaxis(o, 0, 2).reshape(B, H, n * C, DV)[:, :, :L]
    return jnp.moveaxis(o, 1, 2), s


def gdn_branch(a_qkv, a_z, a_b, a_a, conv_buf, s0, conv_w, a_log, dt_bias, gdn_norm_g):
    B, L, _ = a_qkv.shape
    qkv, conv_new = causal_short_conv(a_qkv, conv_buf, conv_w)
    q, k, v = jnp.split(qkv, [A_QK, 2 * A_QK], axis=-1)
    rep = GDN_V_HEADS // GDN_QK_HEADS
    q = jnp.repeat(l2_normalize(q.reshape(B, L, GDN_QK_HEADS, GDN_DK)), rep, axis=2) * (GDN_DK ** -0.5)
    k = jnp.repeat(l2_normalize(k.reshape(B, L, GDN_QK_HEADS, GDN_DK)), rep, axis=2)
    v = v.reshape(B, L, GDN_V_HEADS, GDN_DV).astype(jnp.float32)
    beta = jax.nn.sigmoid(a_b.astype(jnp.float32))
    g = -jnp.exp(a_log.astype(jnp.float32)) * jax.nn.softplus(a_a.astype(jnp.float32) + dt_bias.astype(jnp.float32))
    o, s = gated_delta_rule(q, k, v, g, beta, s0.astype(jnp.float32))
    z = a_z.reshape(B, L, GDN_V_HEADS, GDN_DV).astype(jnp.float32)
    y = rms_norm(o, gdn_norm_g) * jax.nn.silu(z)
    return y.reshape(B, L, A_V).astype(a_qkv.dtype), s.astype(s0.dtype), conv_new


def indexer_topk(qi, wi, kidx, q_pos, topk):
    logits = jnp.einsum('bqhd,bkd->bqhk', qi.astype(jnp.float32), kidx.astype(jnp.float32)) * (IDX_DH ** -0.5)
    score = jnp.einsum('bqhk,bqh->bqk', jax.nn.relu(logits), wi.astype(jnp.float32)) * (IDX_HEADS ** -0.5)
    causal = jnp.arange(kidx.shape[1])[None, :] <= q_pos[:, None]
    score = jnp.where(causal[None], score, -jnp.inf)
    _, idx = lax.top_k(score, topk)
    return idx, idx <= q_pos[None, :, None]


def sparse_attention(q, k_sel, v_sel, valid):
    B, Q = q.shape[:2]
    qg = q.reshape(B, Q, ATT_KV_HEADS, ATT_HEADS // ATT_KV_HEADS, ATT_DH)
    s = jnp.einsum('bqngd,bqknd->bqngk', qg, k_sel).astype(jnp.float32) * (ATT_DH ** -0.5)
    s = jnp.where(valid[:, :, None, None, :], s, -jnp.inf)
    p = jax.nn.softmax(s, axis=-1).astype(v_sel.dtype)
    o = jnp.einsum('bqngk,bqknd->bqngd', p, v_sel)
    return o.reshape(B, Q, ATT_HEADS * ATT_DH)


def gather_rows(rows, idx):
    return jax.vmap(lambda r, i: r[i])(rows, idx)


def dsa_prompt(q, k, v, qi, wi, kidx):
    B, L = q.shape[:2]
    topk = min(TOPK_MAX, L // 4)
    qb = Q_BLOCK if L % Q_BLOCK == 0 else L
    nb = L // qb

    def to_blocks(t):
        return jnp.swapaxes(t.reshape(B, nb, qb, *t.shape[2:]), 0, 1)

    def block(args):
        q_b, qi_b, wi_b, pos_b = args
        idx, valid = indexer_topk(qi_b, wi_b, kidx, pos_b, topk)
        return sparse_attention(q_b, gather_rows(k, idx), gather_rows(v, idx), valid)

    o = lax.map(block, (to_blocks(q), to_blocks(qi), to_blocks(wi), jnp.arange(L).reshape(nb, qb)))
    return jnp.swapaxes(o, 0, 1).reshape(B, L, ATT_HEADS * ATT_DH)


def dsa_sample(q, k, v, qi, wi, kidx, cache_k, cache_v, cache_kidx, page_table):
    B, L = q.shape[:2]
    past = page_table.shape[1] * PAGE_SIZE
    topk = min(TOPK_MAX, (past + L) // 4)
    kidx_past = cache_kidx[page_table].reshape(B, past, IDX_DH)
    kidx_all = jnp.concatenate([kidx_past, kidx.astype(kidx_past.dtype)], axis=1)
    idx, valid = indexer_topk(qi, wi, kidx_all, past + jnp.arange(L), topk)
    in_past = (idx < past)[..., None, None]
    p_idx = jnp.minimum(idx, past - 1)
    phys = gather_rows(page_table, p_idx // PAGE_SIZE)
    off = p_idx % PAGE_SIZE
    n_idx = jnp.clip(idx - past, 0, L - 1)
    k_sel = jnp.where(in_past, cache_k[phys, off], gather_rows(k, n_idx).astype(cache_k.dtype))
    v_sel = jnp.where(in_past, cache_v[phys, off], gather_rows(v, n_idx).astype(cache_v.dtype))
    return sparse_attention(q, k_sel, v_sel, valid)


def hybrid_layer(x, c, conv_buf, s0, attend, w_ada, b_ada, pre_norm_g, w_in, conv_w, a_log, dt_bias,
                 gdn_norm_g, w_pa, w_pb, w_out, post_norm_g):
    B, L, _ = x.shape
    mod = jnp.einsum('bc,cd->bd', jax.nn.silu(c), w_ada) + b_ada
    shift, scale, gate = jnp.split(mod, 3, axis=-1)
    h = rms_norm(x, pre_norm_g) * (1.0 + scale[:, None]) + shift[:, None]
    (a_qkv, a_z, a_b, a_a, b_q, b_k, b_v, b_z, i_q, i_k, i_w, g_a, g_b) = split_columns(h @ w_in)
    y_a, s_new, conv_new = gdn_branch(a_qkv, a_z, a_b, a_a, conv_buf, s0, conv_w, a_log, dt_bias, gdn_norm_g)
    q = b_q.reshape(B, L, ATT_HEADS, ATT_DH)
    k = b_k.reshape(B, L, ATT_KV_HEADS, ATT_DH)
    v = b_v.reshape(B, L, ATT_KV_HEADS, ATT_DH)
    qi = i_q.reshape(B, L, IDX_HEADS, IDX_DH)
    y_b = attend(q, k, v, qi, i_w, i_k) * jax.nn.silu(b_z)
    merged = jax.nn.sigmoid(g_a) * (y_a @ w_pa) + jax.nn.sigmoid(g_b) * (y_b @ w_pb)
    out = rms_norm(merged @ w_out, post_norm_g)
    return x + gate[:, None] * out, (k, v, i_k, s_new, conv_new)


def setup_inputs(seed: int = 0) -> dict:
    key = jax.random.key(seed)
    ks = jax.random.split(key, 24)
    f32 = jnp.float32
    n_pages = PAST_LEN // PAGE_SIZE
    n_pool = (DEC_BATCH * n_pages * 5) // 4

    def nrm(k, shape, s):
        return jax.random.normal(k, shape, f32) * s

    page_table = jax.random.permutation(ks[9], n_pool)[:DEC_BATCH * n_pages].reshape(DEC_BATCH, n_pages).astype(jnp.int32)
    return {
        'x_prompt': nrm(ks[0], (BATCH, SEQ, D_MODEL), 1.0),
        'x_sample': nrm(ks[1], (DEC_BATCH, DEC_SEQ, D_MODEL), 1.0),
        'c_prompt': nrm(ks[2], (BATCH, D_MODEL), 1.0),
        'c_sample': nrm(ks[3], (DEC_BATCH, D_MODEL), 1.0),
        'cache_k': nrm(ks[4], (DEPTH, n_pool, PAGE_SIZE, ATT_KV_HEADS, ATT_DH), 1.0),
        'cache_v': nrm(ks[5], (DEPTH, n_pool, PAGE_SIZE, ATT_KV_HEADS, ATT_DH), 1.0),
        'cache_kidx': nrm(ks[6], (DEPTH, n_pool, PAGE_SIZE, IDX_DH), 1.0),
        'state_gdn': nrm(ks[7], (DEPTH, DEC_BATCH, GDN_V_HEADS, GDN_DK, GDN_DV), 0.1),
        'state_conv': nrm(ks[8], (DEPTH, DEC_BATCH, CONV_W - 1, A_CONV_CH), 1.0),
        'page_table': page_table,
        'w_ada': nrm(ks[10], (DEPTH, D_MODEL, 3 * D_MODEL), 0.5 * D_MODEL ** -0.5),
        'b_ada': nrm(ks[11], (DEPTH, 3 * D_MODEL), 0.01),
        'pre_norm_g': 1.0 + nrm(ks[12], (DEPTH, D_MODEL), 0.02),
        'w_in': nrm(ks[13], (DEPTH, D_MODEL, D_IN), D_MODEL ** -0.5),
        'conv_w': nrm(ks[14], (DEPTH, CONV_W, A_CONV_CH), CONV_W ** -0.5),
        'a_log': jnp.log(jax.random.uniform(ks[15], (DEPTH, GDN_V_HEADS), f32, 1.0, 16.0)),
        'dt_bias': nrm(ks[16], (DEPTH, GDN_V_HEADS), 0.1),
        'gdn_norm_g': 1.0 + nrm(ks[17], (DEPTH, GDN_DV), 0.02),
        'w_pa': nrm(ks[18], (DEPTH, A_V, D_MODEL), A_V ** -0.5),
        'w_pb': nrm(ks[19], (DEPTH, B_Q, D_MODEL), B_Q ** -0.5),
        'w_out': nrm(ks[20], (DEPTH, D_MODEL, D_MODEL), D_MODEL ** -0.5),
        'post_norm_g': 1.0 + nrm(ks[21], (DEPTH, D_MODEL), 0.02),
    }


def reference(x_prompt, x_sample, c_prompt, c_sample, cache_k, cache_v, cache_kidx, state_gdn, state_conv,
              page_table, w_ada, b_ada, pre_norm_g, w_in, conv_w, a_log, dt_bias, gdn_norm_g, w_pa, w_pb,
              w_out, post_norm_g):
    y_p, y_s = x_prompt, x_sample
    bp = x_prompt.shape[0]
    new_p, new_s = [], []
    for l in range(DEPTH):
        w = (w_ada[l], b_ada[l], pre_norm_g[l], w_in[l], conv_w[l], a_log[l], dt_bias[l], gdn_norm_g[l],
             w_pa[l], w_pb[l], w_out[l], post_norm_g[l])
        conv0 = jnp.zeros((bp, CONV_W - 1, A_CONV_CH), x_prompt.dtype)
        s0 = jnp.zeros((bp, GDN_V_HEADS, GDN_DK, GDN_DV), state_gdn.dtype)
        y_p, st_p = hybrid_layer(y_p, c_prompt, conv0, s0, dsa_prompt, *w)
        attend_s = functools.partial(dsa_sample, cache_k=cache_k[l], cache_v=cache_v[l],
                                     cache_kidx=cache_kidx[l], page_table=page_table)
        y_s, st_s = hybrid_layer(y_s, c_sample, state_conv[l], state_gdn[l], attend_s, *w)
        new_p.append(st_p)
        new_s.append(st_s)

    def stacked(sts, i):
        return jnp.stack([st[i] for st in sts])

    return (y_p, y_s,
            stacked(new_p, 0), stacked(new_p, 1), stacked(new_p, 2), stacked(new_p, 3), stacked(new_p, 4),
            stacked(new_s, 0), stacked(new_s, 1), stacked(new_s, 2), stacked(new_s, 3), stacked(new_s, 4))
```

```python
import numpy as np
from contextlib import ExitStack
import concourse.bass as bass
import concourse.mybir as mybir
from concourse.bass_utils import run_bass_kernel_spmd

F32 = mybir.dt.float32
BF16 = mybir.dt.bfloat16
I32 = mybir.dt.int32
U32 = mybir.dt.uint32
AF = mybir.ActivationFunctionType
ALU = mybir.AluOpType
AX = mybir.AxisListType

D = 2048
NTOK = 2080
OWN0 = 1024
SMP0 = 2048
D_IN = 23248
BIG = 1.0e30


class Buf:
    __slots__ = ("t", "w", "r", "name", "psum")

    def __init__(self, t, name="", psum=False):
        self.t = t
        self.w = None
        self.r = {}
        self.name = name
        self.psum = psum

    def __getitem__(self, idx):
        return self.t[idx]


class Eng:
    def __init__(self, kb, eng, name, is_pe=False, ndma=0):
        self.kb = kb
        self.eng = eng
        self.name = name
        self.is_pe = is_pe
        self.sem = kb.newsem("s_" + name)
        self.count = 0
        self.known = {}
        self.pool = [[kb.newsem("d_%s%d" % (name, i)), 0] for i in range(ndma)]
        self.next = 0
        self.ninst = 0


class KB:
    def __init__(self):
        self.nc = bass.Bass("TRN2", target_bir_lowering=False)
        self.es = ExitStack()
        self.sems = []
        nc = self.nc
        self.pe = Eng(self, nc.tensor, "pe", is_pe=True)
        self.act = Eng(self, nc.scalar, "act", ndma=6)
        self.dve = Eng(self, nc.vector, "dve")
        self.pool = Eng(self, nc.gpsimd, "pool", ndma=8)
        self.sp = Eng(self, nc.sync, "sp", ndma=12)
        self.engs = [self.pe, self.act, self.dve, self.pool, self.sp]
        self.nbuf = 0

    def newsem(self, name):
        s = self.es.enter_context(self.nc.semaphore(name))
        self.sems.append(s)
        return len(self.sems) - 1

    def sbuf(self, shape, dt, name=None):
        self.nbuf += 1
        name = name or ("sb%d" % self.nbuf)
        t = self.es.enter_context(self.nc.sbuf_tensor(name, list(shape), dt))
        return Buf(t, name)

    def psum(self, shape, dt=F32, name=None):
        self.nbuf += 1
        name = name or ("ps%d" % self.nbuf)
        t = self.es.enter_context(self.nc.psum_tensor(name, list(shape), dt))
        return Buf(t, name, psum=True)

    def dram(self, name, shape, dt, kind="Internal"):
        t = self.nc.dram_tensor(name, list(shape), dt, kind=kind)
        return Buf(t.ap(), name)

    def _deps(self, reads, writes):
        deps = {}
        for b in reads:
            if b.w is not None:
                s, v = b.w
                if deps.get(s, 0) < v:
                    deps[s] = v
            if b.psum:
                for s, v in b.r.items():
                    if deps.get(s, 0) < v:
                        deps[s] = v
        for b in writes:
            if b.w is not None:
                s, v = b.w
                if deps.get(s, 0) < v:
                    deps[s] = v
            for s, v in b.r.items():
                if deps.get(s, 0) < v:
                    deps[s] = v
        return deps

    def _wait(self, E, deps):
        for s, v in deps.items():
            if E.is_pe and s == E.sem:
                continue
            if E.known.get(s, 0) < v:
                E.eng.wait_ge(self.sems[s], v)
                E.known[s] = v

    def _mark(self, tok, reads, writes):
        s, v = tok
        for b in reads:
            if b.r.get(s, 0) < v:
                b.r[s] = v
        for b in writes:
            b.w = tok
            b.r = {}

    def op(self, E, fn, reads=(), writes=(), signal=True):
        self._wait(E, self._deps(reads, writes))
        inst = fn(E.eng)
        E.ninst += 1
        if signal:
            E.count += 1
            inst.then_inc(self.sems[E.sem], 1)
            tok = (E.sem, E.count)
        else:
            tok = (E.sem, E.count + 1)
        self._mark(tok, reads, writes)
        return inst

    def dma(self, Q, out_ap, in_ap, reads=(), writes=(), **kw):
        slot = Q.pool[Q.next % len(Q.pool)]
        Q.next += 1
        deps = self._deps(reads, writes)
        if slot[1] > 0 and deps.get(slot[0], 0) < slot[1]:
            deps[slot[0]] = slot[1]
        self._wait(Q, deps)
        slot[1] += 16
        Q.eng.dma_start(out=out_ap, in_=in_ap, **kw).then_inc(self.sems[slot[0]], 16)
        Q.ninst += 1
        self._mark((slot[0], slot[1]), reads, writes)

    def gather(self, out_ap, table_ap, idx_ap, reads=(), writes=()):
        Q = self.pool
        slot = Q.pool[Q.next % len(Q.pool)]
        Q.next += 1
        deps = self._deps(reads, writes)
        if slot[1] > 0 and deps.get(slot[0], 0) < slot[1]:
            deps[slot[0]] = slot[1]
        self._wait(Q, deps)
        slot[1] += 16
        Q.eng.indirect_dma_start(out=out_ap, out_offset=None, in_=table_ap,
                                 in_offset=bass.IndirectOffsetOnAxis(ap=idx_ap, axis=0)).then_inc(self.sems[slot[0]], 16)
        Q.ninst += 1
        self._mark((slot[0], slot[1]), reads, writes)

    def finish(self, bufs):
        deps = {}
        for b in bufs:
            if b.w is not None:
                s, v = b.w
                if deps.get(s, 0) < v:
                    deps[s] = v
        for E in self.engs:
            for s, v in E.pool:
                if v > 0 and deps.get(s, 0) < v:
                    deps[s] = v
            if E.count > 0:
                deps[E.sem] = max(deps.get(E.sem, 0), E.count)
        self._wait(self.sp, deps)

    def barrier(self):
        deps = {}
        for E in self.engs:
            for s_, v in E.pool:
                if v > 0:
                    deps[s_] = v
            if E.count > 0:
                deps[E.sem] = E.count
        for E in self.engs:
            self._wait(E, dict(deps))

    def close(self):
        self.es.close()

    def mm(self, out, lhsT, rhs, reads, writes, start=True, stop=True, signal=None, **kw):
        if signal is None:
            signal = stop
        return self.op(self.pe, lambda e: e.matmul(out, lhsT, rhs, start=start, stop=stop, **kw),
                       reads, writes, signal=signal)

    def tr(self, out, in_, ident, reads, writes, signal=True):
        return self.op(self.pe, lambda e: e.transpose(out, in_, ident), reads, writes, signal=signal)

    def actf(self, out, in_, func, reads, writes, **kw):
        return self.op(self.act, lambda e: e.activation(out=out, in_=in_, func=func, **kw), reads, writes)

    def ts(self, E, out, in0, s1, s2, op0, op1, reads, writes, **kw):
        if op1 is None:
            return self.op(E, lambda e: e.tensor_scalar(out=out, in0=in0, scalar1=s1, scalar2=None, op0=op0, **kw),
                           reads, writes)
        return self.op(E, lambda e: e.tensor_scalar(out=out, in0=in0, scalar1=s1, scalar2=s2, op0=op0, op1=op1, **kw),
                       reads, writes)

    def tt(self, E, out, in0, in1, op, reads, writes):
        return self.op(E, lambda e: e.tensor_tensor(out=out, in0=in0, in1=in1, op=op), reads, writes)

    def stt(self, out, in0, scalar, in1, op0, op1, reads, writes):
        return self.op(self.dve, lambda e: e.scalar_tensor_tensor(out=out, in0=in0, scalar=scalar, in1=in1,
                                                                  op0=op0, op1=op1), reads, writes)

    def cp(self, E, out, in_, reads, writes):
        if E is self.act:
            return self.op(E, lambda e: e.activation(out=out, in_=in_, func=AF.Copy), reads, writes)
        return self.op(E, lambda e: e.tensor_copy(out, in_), reads, writes)


class Ring:
    def __init__(self, bufs):
        self.bufs = bufs
        self.i = 0

    def get(self):
        b = self.bufs[self.i % len(self.bufs)]
        self.i += 1
        return b


C_Q, C_K, C_V, C_Z = 0, 2048, 4096, 8192
C_BETA, C_DEC = 12288, 12320
C_BQ, C_BK, C_BV, C_BZ = 12352, 14400, 14656, 14912
C_IQ, C_IK, C_IW = 16960, 19008, 19136
C_GA, C_GB = 19152, 21200


def build(debug=(), phases=(0, 1, 2, 3, 4)):
    kb = KB()
    nc = kb.nc
    dbg = set(debug)

    def din(name, shape, dt=F32):
        return kb.dram(name, shape, dt, kind="ExternalInput")

    def dout(name, shape, dt=F32):
        return kb.dram(name, shape, dt, kind="ExternalOutput")

    def dscr(name, shape, dt):
        return kb.dram(name, shape, dt, kind=("ExternalOutput" if name in dbg else "Internal"))

    xp = din("xp", [2048, D])
    xs = din("xs", [32, D])
    cc = din("cc", [5, D])
    flag_d = din("flag", [128, 1])
    w_ada = din("w_ada", [D, 3 * D])
    b_ada = din("b_ada", [1, 3 * D])
    pre_g = din("pre_g", [16, 128])
    w_in = din("w_in", [D, D_IN])
    conv_w = din("conv_w", [4, 8192])
    alog_d = din("a_log", [32, 1])
    dtb_d = din("dt_bias", [32, 1])
    gdn_g = din("gdn_g", [1, 128])
    w_pa = din("w_pa", [4096, D])
    w_pb = din("w_pb", [D, D])
    w_out = din("w_out", [D, D])
    post_g = din("post_g", [1, D])
    sconv = din("sconv", [12, 8192])
    if 2 in phases:
        sgdn = din("sgdn", [128, 128, 128])
    if 3 in phases:
        cache_k = din("cache_k", [2560 * 16, 8 * 256])
        cache_v = din("cache_v", [2560 * 16, 8 * 256])
        cache_ki = din("cache_ki", [2560 * 16, 8 * 128])
        scr_sc = dscr("scr_sc", [4, 8, 16, 520], F32)
        scr_nm = dscr("scr_nm", [4, 8, 16, 520], BF16)
        pt_d = din("pt", [4, 64], I32)

    y_p = dout("y_p", [1024, D])
    y_s = dout("y_s", [32, D])
    k_p = dout("k_p", [1024, 256])
    v_p = dout("v_p", [1024, 256])
    ki_p = dout("ki_p", [1024, 128])
    gdn_p = dout("gdn_p", [32, 128, 128])
    conv_p = dout("conv_p", [3, 8192])
    k_s = dout("k_s", [32, 256])
    v_s = dout("v_s", [32, 256])
    ki_s = dout("ki_s", [32, 128])
    gdn_s = dout("gdn_s", [128, 128, 128])
    conv_s = dout("conv_s", [12, 8192])
    outs = [y_p, y_s, k_p, v_p, ki_p, gdn_p, conv_p, k_s, v_s, ki_s, gdn_s, conv_s]

    qT_s = dscr("qT_s", [16, 128, 1056], BF16)
    kT_s = dscr("kT_s", [16, 128, NTOK], BF16)
    ktok_s = dscr("ktok_s", [NTOK, 16, 128], BF16)
    vtok_s = dscr("vtok_s", [NTOK, 32, 128], BF16)
    zT_s = dscr("zT_s", [32, 128, 1056], BF16)
    QT_s = dscr("QT_s", [16, 128, 1056], BF16)
    KT_s = dscr("KT_s", [2, 128, NTOK], BF16)
    Vtok_s = dscr("Vtok_s", [NTOK, 256], BF16)
    kiT_s = dscr("kiT_s", [128, NTOK], BF16)
    bzT_s = dscr("bzT_s", [16, 128, 1056], BF16)
    qiT_s = dscr("qiT_s", [16, 128, 1056], BF16)
    iw_s = dscr("iw_s", [16, 1056], F32)
    gaT_s = dscr("gaT_s", [16, 128, 1056], BF16)
    gbT_s = dscr("gbT_s", [16, 128, 1056], BF16)
    yaT_s = dscr("yaT_s", [32, 128, 1056], BF16)
    ybT_s = dscr("ybT_s", [16, 128, 1056], BF16)
    hT_dbg = dscr("hT_dbg", [16, 128, NTOK], BF16) if "hT_dbg" in dbg else None
    gb_dbg = dscr("gb_dbg", [33, 64, 64], F32) if "gb_dbg" in dbg else None

    identf = kb.sbuf([128, 128], F32, "identf")
    ident = kb.sbuf([128, 128], BF16, "ident")
    ones_bf = kb.sbuf([128, 128], BF16, "ones_bf")
    flag = kb.sbuf([128, 1], F32, "flag_sb")
    kb.op(kb.pool, lambda e: e.memset(identf[:], 1.0), writes=[identf])
    kb.op(kb.pool, lambda e: e.affine_select(out=identf[:], in_=identf[:], pattern=[[-1, 128]], compare_op=ALU.is_equal,
                                             fill=0.0, base=0, channel_multiplier=1), reads=[identf], writes=[identf])
    kb.cp(kb.dve, ident[:], identf[:], [identf], [ident])
    kb.op(kb.pool, lambda e: e.memset(ones_bf[:], 1.0), writes=[ones_bf])
    kb.dma(kb.sp, flag[:], flag_d[:], [flag_d], [flag])

    gbc = kb.sbuf([64, 32, 64], F32, "gbc")
    gbs = kb.sbuf([8, 4, 64], F32, "gbs")
    hstack = ExitStack()
    hT = Buf(hstack.enter_context(nc.sbuf_tensor("hT", [128, 16, NTOK], BF16)), "hT")
    gg_s = dscr("gg_s", [160, D], F32)

    with ExitStack() as p0:
        def sb(shape, dt, name):
            kb.nbuf += 1
            t = p0.enter_context(nc.sbuf_tensor(name, list(shape), dt))
            return Buf(t, name)

        def ps(shape, dt, name):
            t = p0.enter_context(nc.psum_tensor(name, list(shape), dt))
            return Buf(t, name, psum=True)

        modA = sb([128, 16, 5], F32, "modA")
        modB = sb([128, 16, 5], F32, "modB")
        ggp = sb([128, D], F32, "ggp")
        ggs = sb([32, D], F32, "ggs")

        csb = sb([5, D], F32, "csb")
        scT = sb([128, 16, 8], F32, "scT")
        lhs_p = sb([128, 16, 128], F32, "lhs_p")
        lhs_s = sb([128, 16, 32], F32, "lhs_s")
        bg_bc = sb([128, D], F32, "bg_bc")
        postg_bc = sb([128, D], F32, "postg_bc")
        pregT = sb([128, 16], F32, "pregT")
        badaT = sb([128, 32], F32, "badaT")
        pg16 = sb([16, 128], F32, "pg16")
        ba32 = sb([32, 128], F32, "ba32")
        mt = Ring([sb([5, 256], F32, "mt%d" % i) for i in range(2)])
        pst = ps([128, 512], F32, "p0_t")
        psm = ps([128, 512], F32, "p0_m")
        psg = ps([128, 512], F32, "p0_g")
        psg2 = ps([32, 512], F32, "p0_g2")

        kb.dma(kb.sp, csb[:], cc[:], [cc], [csb])
        kb.dma(kb.sp, bg_bc[:], b_ada[:, 2 * D:3 * D].partition_broadcast(128), [b_ada], [bg_bc])
        kb.dma(kb.sp, postg_bc[:], post_g[:].partition_broadcast(128), [post_g], [postg_bc])
        kb.dma(kb.sp, pg16[:], pre_g[:], [pre_g], [pg16])
        kb.dma(kb.sp, ba32[:], b_ada[:, 0:2 * D].rearrange("o (j p) -> (o j) p", p=128), [b_ada], [ba32])
        kb.actf(csb[:], csb[:], AF.Silu, [csb], [csb])
        for j in range(16):
            kb.tr(pst[:, j * 8:j * 8 + 5], csb[:, j * 128:(j + 1) * 128], identf[0:5, 0:5], [csb, identf], [pst],
                  signal=(j == 15))
        kb.cp(kb.dve, scT[:, :, 0:5], pst[:, 0:128].rearrange("p (a b) -> p a b", b=8)[:, :, 0:5], [pst], [scT])
        kb.cp(kb.dve, lhs_p[:], scT[:, :, 0:1].to_broadcast([128, 16, 128]), [scT], [lhs_p])
        kb.cp(kb.dve, lhs_s[:].rearrange("p a (s i) -> p a s i", i=8),
              scT[:, :, 1:5].unsqueeze(3).to_broadcast([128, 16, 4, 8]), [scT], [lhs_s])
        kb.tr(pst[:, 256:272], pg16[:], identf[0:16, 0:16], [pg16, identf], [pst])
        kb.cp(kb.dve, pregT[:], pst[:, 256:272], [pst], [pregT])
        kb.tr(pst[:, 288:320], ba32[:], identf[0:32, 0:32], [ba32, identf], [pst])
        kb.cp(kb.dve, badaT[:], pst[:, 288:320], [pst], [badaT])

        wst32 = Ring([sb([128, 16, 256], F32, "wa%d" % i) for i in range(2)])
        for t in range(24):
            wt = wst32.get()
            for q in range(4):
                src = w_ada[q * 512:(q + 1) * 512, t * 256:(t + 1) * 256].rearrange("(a p) c -> p a c", p=128)
                kb.dma(kb.sp, wt[:, 4 * q:4 * q + 4, :], src, [w_ada], [wt])
            if t < 16:
                for k in range(16):
                    kb.mm(psm[0:5, 0:256], scT[:, k, 0:5], wt[:, k, :], [scT, wt], [psm], start=(k == 0), stop=(k == 15))
                m_ = mt.get()
                kb.cp(kb.dve, m_[:], psm[0:5, 0:256], [psm], [m_])
                for j in range(2):
                    jj = 2 * t + j
                    kb.tr(pst[:, jj * 8:jj * 8 + 5], m_[:, j * 128:(j + 1) * 128], identf[0:5, 0:5], [m_, identf], [pst],
                          signal=(j == 1))
            else:
                c0 = (t - 16) * 256
                for k in range(16):
                    kb.mm(psg[:, 0:256], lhs_p[:, k, :], wt[:, k, :], [lhs_p, wt], [psg], start=(k == 0), stop=(k == 15))
                for k in range(16):
                    kb.mm(psg2[:, 0:256], lhs_s[:, k, :], wt[:, k, :], [lhs_s, wt], [psg2], start=(k == 0), stop=(k == 15))
                kb.tt(kb.dve, ggp[:, c0:c0 + 256], psg[:, 0:256], bg_bc[:, c0:c0 + 256], ALU.add, [psg, bg_bc], [ggp])
                kb.tt(kb.dve, ggs[:, c0:c0 + 256], psg2[:, 0:256], bg_bc[0:32, c0:c0 + 256], ALU.add, [psg2, bg_bc], [ggs])
            if t == 15:
                pv = pst[:, 0:256].rearrange("p (a b) -> p a b", b=8)
                kb.tt(kb.dve, modB[:], pv[:, 0:16, 0:5], badaT[:, 0:16].unsqueeze(2).to_broadcast([128, 16, 5]), ALU.add,
                      [pst, badaT], [modB])
                kb.tt(kb.dve, modA[:], pv[:, 16:32, 0:5], badaT[:, 16:32].unsqueeze(2).to_broadcast([128, 16, 5]), ALU.add,
                      [pst, badaT], [modA])
                kb.ts(kb.dve, modA[:], modA[:], 1.0, None, ALU.add, None, [modA], [modA])
                kb.tt(kb.dve, modA[:], modA[:], pregT[:].unsqueeze(2).to_broadcast([128, 16, 5]), ALU.mult,
                      [modA, pregT], [modA])
        kb.tt(kb.dve, ggp[:], ggp[:], postg_bc[:], ALU.mult, [ggp, postg_bc], [ggp])
        kb.tt(kb.dve, ggs[:], ggs[:], postg_bc[0:32, :], ALU.mult, [ggs, postg_bc], [ggs])
        kb.dma(kb.sp, gg_s[0:128, :], ggp[:], [ggp], [gg_s])
        kb.dma(kb.sp, gg_s[128:160, :], ggs[:], [ggs], [gg_s])

        xring = Ring([sb([128, D], F32, "xt%d" % i) for i in range(2)])
        xnring = Ring([sb([128, D], BF16, "xn%d" % i) for i in range(2)])
        junk = sb([128, D], BF16, "junk")
        stat = Ring([sb([128, 4], F32, "stat%d" % i) for i in range(2)])
        ptr = Ring([ps([128, 1024], BF16, "p0_tr%d" % i) for i in range(2)])
        for ti in range(17):
            rows = 128 if ti < 16 else 32
            xt = xring.get()
            xn = xnring.get()
            st_ = stat.get()
            src = xp[ti * 128:(ti + 1) * 128, :] if ti < 16 else xs[:, :]
            kb.dma(kb.sp, xt[0:rows, :], src, [xp if ti < 16 else xs], [xt])
            kb.actf(junk[0:rows, :], xt[0:rows, :], AF.Square, [xt], [junk, st_], accum_out=st_[0:rows, 0:1])
            kb.ts(kb.dve, st_[0:rows, 1:2], st_[0:rows, 0:1], 1.0 / D, 1e-6, ALU.mult, ALU.add, [st_], [st_])
            kb.actf(st_[0:rows, 2:3], st_[0:rows, 1:2], AF.Sqrt, [st_], [st_])
            kb.op(kb.dve, lambda e: e.reciprocal(st_[0:rows, 3:4], st_[0:rows, 2:3]), [st_], [st_])
            kb.ts(kb.dve, xn[0:rows, :], xt[0:rows, :], st_[0:rows, 3:4], None, ALU.mult, None, [xt, st_], [xn])
            for hh in range(2):
                pt_ = ptr.get()
                for j in range(8):
                    jj = hh * 8 + j
                    kb.tr(pt_[:, j * 128:j * 128 + rows], xn[0:rows, jj * 128:(jj + 1) * 128], ident[0:rows, 0:rows],
                          [xn, ident], [pt_], signal=(j == 7))
                for j in range(8):
                    jj = hh * 8 + j
                    if ti < 16:
                        kb.actf(hT[:, jj, ti * 128:(ti + 1) * 128], pt_[:, j * 128:(j + 1) * 128], AF.Identity,
                                [pt_, modA, modB], [hT], scale=modA[:, jj, 0:1], bias=modB[:, jj, 0:1])
                    else:
                        for s in range(4):
                            kb.actf(hT[:, jj, SMP0 + 8 * s:SMP0 + 8 * s + 8], pt_[:, j * 128 + 8 * s:j * 128 + 8 * s + 8],
                                    AF.Identity, [pt_, modA, modB], [hT],
                                    scale=modA[:, jj, 1 + s:2 + s], bias=modB[:, jj, 1 + s:2 + s])
        if hT_dbg is not None:
            kb.dma(kb.sp, hT_dbg[:].rearrange("j p t -> p j t"), hT[:], [hT], [hT_dbg])
        kb.barrier()

    if 1 not in phases:
        kb.finish(outs)
        hstack.close()
        kb.close()
        return kb

    BL_ALL = [(0, 512), (512, 512), (1024, 512), (1536, 512), (2048, 32)]
    BL_OWN = [(1024, 512), (1536, 512), (2048, 32)]
    BL_Q = [(1021, 3)] + BL_OWN
    with ExitStack() as p1:
        def sb(shape, dt, name):
            kb.nbuf += 1
            t = p1.enter_context(nc.sbuf_tensor(name, list(shape), dt))
            return Buf(t, name)

        def ps(shape, dt, name):
            t = p1.enter_context(nc.psum_tensor(name, list(shape), dt))
            return Buf(t, name, psum=True)

        stage = Ring([sb([128, 4, 512], F32, "wst%d" % i) for i in range(2)])
        wbf = Ring([sb([128, 16, 512], BF16, "wbf%d" % i) for i in range(2)])
        acc = Ring([ps([128, 512], F32, "p1_acc%d" % i) for i in range(2)])
        pyr = Ring([ps([128, 512], F32, "p1_y%d" % i) for i in range(2)])
        pss = ps([128, 512], F32, "p1_ss")
        ptr = Ring([ps([128, 1024], BF16, "p1_tr%d" % i) for i in range(2)])

        def wload(c0, n):
            bt = wbf.get()
            for q in range(4):
                st = stage.get()
                src = w_in[q * 512:(q + 1) * 512, c0:c0 + n].rearrange("(a p) c -> p a c", p=128)
                kb.dma(kb.sp, st[:, :, 0:n], src, [w_in], [st])
                kb.cp(kb.pool if q == 0 else kb.dve, bt[:, 4 * q:4 * q + 4, 0:n], st[:, :, 0:n], [st], [bt])
            return bt

        def proj_fm(wt, col0, m, t0, n):
            pa = acc.get()
            for k in range(16):
                kb.mm(pa[0:m, 0:n], wt[:, k, col0:col0 + m], hT[:, k, t0:t0 + n], [wt, hT], [pa],
                      start=(k == 0), stop=(k == 15))
            return pa

        def proj_tm(wt, n, t0, rows):
            pa = acc.get()
            for k in range(16):
                kb.mm(pa[0:rows, 0:n], hT[:, k, t0:t0 + rows], wt[:, k, 0:n], [wt, hT], [pa],
                      start=(k == 0), stop=(k == 15))
            return pa

        cw_in = Ring([sb([128, 128], F32, "cw_in%d" % i) for i in range(2)])
        cwT = sb([128, 4, 64], F32, "cwT")
        sc_in = sb([12, 8192], F32, "sc_in") if False else None
        sconvT = sb([128, 64, 12], BF16, "sconvT")
        cwv = conv_w[:, :].rearrange("i (c p) -> (i c) p", p=128)
        for hh in range(2):
            t_ = cw_in.get()
            kb.dma(kb.sp, t_[:], cwv[hh * 128:(hh + 1) * 128, :], [conv_w], [t_])
            pp = pyr.get()
            kb.tr(pp[:, 0:128], t_[:], identf[:], [t_, identf], [pp])
            kb.cp(kb.dve, cwT[:, 2 * hh:2 * hh + 2, :], pp[:, 0:128].rearrange("p (i c) -> p i c", c=64), [pp], [cwT])
        scr = Ring([sb([12, 512], F32, "scr%d" % i) for i in range(2)])
        for qd in range(16):
            t_ = scr.get()
            kb.dma(kb.sp, t_[:], sconv[:, qd * 512:(qd + 1) * 512], [sconv], [t_])
            pp = pyr.get()
            for j in range(4):
                kb.tr(pp[:, j * 16:j * 16 + 12], t_[:, j * 128:(j + 1) * 128], identf[0:12, 0:12], [t_, identf], [pp],
                      signal=(j == 3))
            kb.cp(kb.dve, sconvT[:, qd * 4:(qd + 1) * 4, :],
                  pp[:, 0:64].rearrange("p (c b) -> p c b", b=16)[:, :, 0:12], [pp], [sconvT])

        dgr = Ring([sb([128, 4, 128], BF16, "dg%d" % i) for i in range(2)])
        abr = Ring([sb([128, 516], BF16, "ab%d" % i) for i in range(3)])
        sabr = Ring([sb([128, 4, 11], BF16, "sab%d" % i) for i in range(2)])
        ybr = Ring([sb([128, 512], F32, "yb%d" % i) for i in range(2)])
        sqr = Ring([sb([128, 512], BF16, "sq%d" % i) for i in range(2)])
        rsr = Ring([sb([128, 512], F32, "rs%d" % i) for i in range(2)])
        fmr = Ring([sb([128, NTOK], BF16, "fm%d" % i) for i in range(2)])
        tmr = Ring([sb([128, 17, 512], BF16, "tm%d" % i) for i in range(1)])
        tkf = Ring([sb([128, 512], F32, "tkf%d" % i) for i in range(2)])
        tkb = Ring([sb([128, 256], BF16, "tkb%d" % i) for i in range(2)])
        cvo = Ring([sb([35, 512], F32, "cvo%d" % i) for i in range(2)])
        gbT = sb([64, NTOK], F32, "gbT")
        gpar = sb([64, 4], F32, "gpar")
        iwt = sb([16, 1056], F32, "iwt")

        pipe = []

        def pstep(gen):
            try:
                next(gen)
                return True
            except StopIteration:
                return False

        def tick(newgen):
            alive = pstep(newgen)
            keep = [g_ for g_ in pipe if pstep(g_)]
            pipe[:] = keep
            if alive:
                pipe.append(newgen)

        def flush():
            while pipe:
                keep = [g_ for g_ in pipe if pstep(g_)]
                pipe[:] = keep

        def blk(kind, wt, cj, gch, dg, fm, tm, tmslot, t0, n, state, is_last):
            pa = proj_fm(wt, cj * 128, 128, t0, n)
            sample = t0 >= SMP0
            if not sample:
                ab = abr.get()
                kb.cp(kb.act, ab[:, 3:3 + n], pa[:, 0:n], [pa], [ab])
                if state["prev"] is None:
                    kb.op(kb.pool, lambda e: e.memset(ab[:, 0:3], 0.0), [], [ab])
                else:
                    pab, pn = state["prev"]
                    if t0 == OWN0:
                        kb.ts(kb.dve, ab[:, 0:3], pab[:, pn:pn + 3], flag[:, 0:1], None, ALU.mult, None,
                              [pab, flag], [ab])
                    else:
                        kb.cp(kb.dve, ab[:, 0:3], pab[:, pn:pn + 3], [pab], [ab])
                state["prev"] = (ab, n)
                if kind == "q" and t0 < OWN0:
                    return
            else:
                sab = sabr.get()
                kb.cp(kb.act, sab[:, :, 3:11], pa[:, 0:32].rearrange("p (s i) -> p s i", i=8), [pa], [sab])
                kb.cp(kb.dve, sab[:, :, 0:3], sconvT[:, gch, :].rearrange("p (s i) -> p s i", i=3), [sconvT], [sab])
            yield
            py = pyr.get()
            if not sample:
                for i in range(4):
                    kb.mm(py[:, 0:n], dg[:, i, :], ab[:, i:i + n], [dg, ab], [py], start=(i == 0), stop=(i == 3))
            else:
                for i in range(4):
                    kb.mm(py[:, 0:32], dg[:, i, :], sab[:, :, i:i + 8], [dg, sab], [py], start=(i == 0), stop=(i == 3))
            if kind == "v":
                yv = sqr.get()
                kb.actf(yv[:, 0:n], py[:, 0:n], AF.Silu, [py], [yv])
                src_fm, o0 = yv, 0
            else:
                yb = ybr.get()
                sq = sqr.get()
                kb.actf(yb[:, 0:n], py[:, 0:n], AF.Silu, [py], [yb])
                kb.tt(kb.dve, sq[:, 0:n], yb[:, 0:n], yb[:, 0:n], ALU.mult, [yb], [sq])
            yield
            if kind != "v":
                rs = rsr.get()
                kb.mm(pss[:, 0:n], ones_bf[:], sq[:, 0:n], [ones_bf, sq], [pss])
                kb.actf(rs[:, 0:n], pss[:, 0:n], AF.Sqrt, [pss], [rs], bias=epsb[:, 0:1])
                kb.op(kb.dve, lambda e: e.reciprocal(rs[:, 0:n], rs[:, 0:n]), [rs], [rs])
                if kind == "q":
                    o0 = t0 - OWN0
                    kb.stt(fm[:, o0:o0 + n], yb[:, 0:n], float(128 ** -0.5), rs[:, 0:n], ALU.mult, ALU.mult,
                           [yb, rs], [fm])
                else:
                    o0 = t0
                    if t0 < OWN0:
                        kb.stt(fm[:, o0:o0 + n], yb[:, 0:n], flag[:, 0:1], rs[:, 0:n], ALU.mult, ALU.mult,
                               [yb, rs, flag], [fm])
                    else:
                        kb.tt(kb.dve, fm[:, o0:o0 + n], yb[:, 0:n], rs[:, 0:n], ALU.mult, [yb, rs], [fm])
                src_fm = fm
            yield
            if kind != "q":
                nsub = (n + 127) // 128
                pt_ = ptr.get()
                for sbk in range(nsub):
                    w_ = min(128, n - sbk * 128)
                    so = (o0 + sbk * 128) if kind == "k" else sbk * 128
                    kb.tr(pt_[0:w_, sbk * 128:(sbk + 1) * 128], src_fm[:, so:so + w_], ident[:], [src_fm, ident], [pt_],
                          signal=(sbk == nsub - 1))
                ti0 = t0 // 128
                if n >= 128:
                    kb.cp(kb.dve, tm[:, ti0:ti0 + nsub, tmslot * 128:(tmslot + 1) * 128],
                          pt_[:, 0:nsub * 128].rearrange("p (a d) -> p a d", d=128), [pt_], [tm])
                else:
                    kb.cp(kb.dve, tm[0:n, ti0, tmslot * 128:(tmslot + 1) * 128], pt_[0:n, 0:128], [pt_], [tm])
            if is_last:
                if kind == "q":
                    kb.dma(kb.sp, qT_s[gch], fm[:, 0:1056], [fm], [qT_s])
                elif kind == "k":
                    kb.dma(kb.sp, kT_s[gch - 16], fm[:, :], [fm], [kT_s])

        def aqkv_chunk(wt, cj, gch, tm, tmslot):
            kind = "q" if gch < 16 else ("k" if gch < 32 else "v")
            dg = dgr.get()
            for i in range(4):
                kb.ts(kb.pool, dg[:, i, :], identf[:], cwT[:, i, gch:gch + 1], None, ALU.mult, None, [identf, cwT], [dg])
            fm = fmr.get() if kind != "v" else None
            state = {"prev": None}
            blocks = BL_Q if kind == "q" else BL_ALL
            for bi, (t0, n) in enumerate(blocks):
                tick(blk(kind, wt, cj, gch, dg, fm, tm, tmslot, t0, n, state, bi == len(blocks) - 1))

        def delayed(fn, nticks):
            for _ in range(nticks):
                yield
            fn()

        def simple_chunk(wt, cj, dst, func, scale=1.0):
            fm = fmr.get()
            for (t0, n) in BL_OWN:
                pa = proj_fm(wt, cj * 128, 128, t0, n)
                kb.actf(fm[:, t0 - OWN0:t0 - OWN0 + n], pa[:, 0:n], func, [pa], [fm], scale=scale)
            kb.dma(kb.sp, dst, fm[:, 0:1056], [fm], [dst_buf[0]])

        dst_buf = [None]
        epsb = sb([128, 1], F32, "epsb")
        kb.op(kb.pool, lambda e: e.memset(epsb[:], 1e-6), [], [epsb])

        tiles = []
        for t in range(16):
            tiles.append(("aqkv", t * 512, 512, t))
        for t in range(8):
            tiles.append(("z", C_Z + t * 512, 512, t))
        tiles.append(("bd", C_BETA, 64, 0))
        for t in range(4):
            tiles.append(("bq", C_BQ + t * 512, 512, t))
        tiles.append(("kv", C_BK, 512, 0))
        for t in range(4):
            tiles.append(("bz", C_BZ + t * 512, 512, t))
        for t in range(4):
            tiles.append(("iq", C_IQ + t * 512, 512, t))
        tiles.append(("ik", C_IK, 128, 0))
        tiles.append(("iw", C_IW, 16, 0))
        for t in range(4):
            tiles.append(("ga", C_GA + t * 512, 512, t))
        for t in range(4):
            tiles.append(("gb", C_GB + t * 512, 512, t))
        only = [x for x in debug if isinstance(x, tuple)]
        if only:
            tiles = [tl for tl in tiles if (tl[0] in only[0] or (tl[0], tl[3]) in only[0])]

        nxt = wload(tiles[0][1], tiles[0][2])
        for idx, (kind, c0, n, t) in enumerate(tiles):
            wt = nxt
            if idx + 1 < len(tiles):
                nxt = wload(tiles[idx + 1][1], tiles[idx + 1][2])
            if kind == "aqkv":
                pa = proj_tm(wt, 512, 2045, 35)
                cv = cvo.get()
                kb.cp(kb.dve, cv[:, :], pa[0:35, :], [pa], [cv])
                kb.dma(kb.sp, conv_p[:, c0:c0 + 512], cv[0:3, :], [cv], [conv_p])
                for s in range(4):
                    kb.dma(kb.sp, conv_s[3 * s:3 * s + 3, c0:c0 + 512], cv[3 + 8 * s + 5:3 + 8 * s + 8, :], [cv], [conv_s])
                gch0 = t * 4
                tm = tmr.get() if gch0 >= 16 else None
                for cj in range(4):
                    aqkv_chunk(wt, cj, gch0 + cj, tm, cj)
                if gch0 >= 16:
                    if gch0 < 32:
                        h0 = gch0 - 16
                        dstT, dstB = ktok_s, ktok_s
                    else:
                        h0 = gch0 - 32
                        dstT, dstB = vtok_s, vtok_s
                    def tm_dma(dstT=dstT, dstB=dstB, h0=h0, tm=tm):
                        kb.dma(kb.sp, dstT[0:2048, h0:h0 + 4, :].rearrange("(a p) h d -> p a (h d)", p=128),
                               tm[:, 0:16, :], [tm], [dstB])
                        kb.dma(kb.sp, dstT[2048:2080, h0:h0 + 4, :].rearrange("p h d -> p (h d)"),
                               tm[0:32, 16, :], [tm], [dstB])
                    pipe.append(delayed(tm_dma, 3))
            elif kind in ("z", "bq", "bz", "iq", "ga", "gb"):
                flush()
                dstT = {"z": zT_s, "bq": QT_s, "bz": bzT_s, "iq": qiT_s, "ga": gaT_s, "gb": gbT_s}[kind]
                func = {"z": AF.Silu, "bq": AF.Copy, "bz": AF.Silu, "iq": AF.Copy, "ga": AF.Sigmoid, "gb": AF.Sigmoid}[kind]
                scl = {"bq": float(128 ** -0.5), "iq": float(128 ** -0.5)}.get(kind, 1.0)
                dst_buf[0] = dstT
                for cj in range(4):
                    simple_chunk(wt, cj, dstT[t * 4 + cj], func, scl)
            elif kind == "bd":
                flush()
                kb.dma(kb.sp, gpar[32:64, 0:1], alog_d[:], [alog_d], [gpar])
                kb.dma(kb.sp, gpar[32:64, 1:2], dtb_d[:], [dtb_d], [gpar])
                kb.actf(gpar[32:64, 2:3], gpar[32:64, 0:1], AF.Exp, [gpar], [gpar])
                kb.ts(kb.dve, gpar[32:64, 2:3], gpar[32:64, 2:3], -1.0, None, ALU.mult, None, [gpar], [gpar])
                for (t0, nb) in BL_ALL:
                    pa = proj_fm(wt, 0, 64, t0, nb)
                    kb.actf(gbT[0:32, t0:t0 + nb], pa[0:32, 0:nb], AF.Sigmoid, [pa], [gbT])
                    kb.actf(gbT[32:64, t0:t0 + nb], pa[32:64, 0:nb], AF.Exp, [pa, gpar], [gbT], bias=gpar[32:64, 1:2])
                    kb.actf(gbT[32:64, t0:t0 + nb], gbT[32:64, t0:t0 + nb], AF.Ln, [gbT], [gbT], bias=1.0)
                    kb.ts(kb.dve, gbT[32:64, t0:t0 + nb], gbT[32:64, t0:t0 + nb], gpar[32:64, 2:3], None, ALU.mult, None,
                          [gbT, gpar], [gbT])
                for c8 in range(4):
                    pp = pyr.get()
                    for j in range(8):
                        c = c8 * 8 + j
                        kb.tr(pp[0:64, j * 64:(j + 1) * 64], gbT[:, c * 64:(c + 1) * 64], identf[0:64, 0:64],
                              [gbT, identf], [pp], signal=(j == 7))
                    kb.cp(kb.dve, gbc[:, c8 * 8:(c8 + 1) * 8, :], pp[0:64, :].rearrange("p (a b) -> p a b", b=64),
                          [pp], [gbc])
                pp = pyr.get()
                for s_ in range(4):
                    kb.tr(pp[0:8, s_ * 64:(s_ + 1) * 64], gbT[:, SMP0 + 8 * s_:SMP0 + 8 * s_ + 8], identf[0:64, 0:64],
                          [gbT, identf], [pp], signal=(s_ == 3))
                kb.cp(kb.dve, gbs[:, :, :], pp[0:8, 0:256].rearrange("p (a b) -> p a b", b=64), [pp], [gbs])
                if gb_dbg is not None:
                    kb.dma(kb.sp, gb_dbg[0:32].rearrange("a p c -> p a c"), gbc[:], [gbc], [gb_dbg])
                    kb.dma(kb.sp, gb_dbg[32, 0:32, :].rearrange("(s i) c -> i s c", i=8), gbs[:], [gbs], [gb_dbg])
            elif kind == "kv":
                flush()
                for kvh in range(2):
                    fm = fmr.get()
                    for (t0, nb) in BL_ALL:
                        pa = proj_fm(wt, kvh * 128, 128, t0, nb)
                        kb.cp(kb.act, fm[:, t0:t0 + nb], pa[:, 0:nb], [pa], [fm])
                    kb.dma(kb.sp, KT_s[kvh], fm[:, :], [fm], [KT_s])
                for ti in range(17):
                    rows = 128 if ti < 16 else 32
                    pa = proj_tm(wt, 512, ti * 128, rows)
                    vb = tkb.get()
                    kb.cp(kb.act, vb[0:rows, :], pa[0:rows, 256:512], [pa], [vb])
                    kb.dma(kb.sp, Vtok_s[ti * 128:ti * 128 + rows, :], vb[0:rows, :], [vb], [Vtok_s])
                    if ti >= 8:
                        tf = tkf.get()
                        kb.cp(kb.dve, tf[0:rows, :], pa[0:rows, :], [pa], [tf])
                        _q = kb.sp
                        if ti < 16:
                            r0 = ti * 128 - OWN0
                            kb.dma(_q, k_p[r0:r0 + 128, :], tf[:, 0:256], [tf], [k_p])
                            kb.dma(_q, v_p[r0:r0 + 128, :], tf[:, 256:512], [tf], [v_p])
                        else:
                            kb.dma(_q, k_s[:, :], tf[0:32, 0:256], [tf], [k_s])
                            kb.dma(_q, v_s[:, :], tf[0:32, 256:512], [tf], [v_s])
            elif kind == "ik":
                flush()
                fm = fmr.get()
                for (t0, nb) in BL_ALL:
                    pa = proj_fm(wt, 0, 128, t0, nb)
                    kb.cp(kb.act, fm[:, t0:t0 + nb], pa[:, 0:nb], [pa], [fm])
                kb.dma(kb.sp, kiT_s[:, :], fm[:, :], [fm], [kiT_s])
                for ti in range(8, 17):
                    rows = 128 if ti < 16 else 32
                    pa = proj_tm(wt, 128, ti * 128, rows)
                    tf = tkf.get()
                    kb.cp(kb.dve, tf[0:rows, 0:128], pa[0:rows, 0:128], [pa], [tf])
                    if ti < 16:
                        r0 = ti * 128 - OWN0
                        kb.dma(kb.sp, ki_p[r0:r0 + 128, :], tf[:, 0:128], [tf], [ki_p])
                    else:
                        kb.dma(kb.sp, ki_s[:, :], tf[0:32, 0:128], [tf], [ki_s])
            elif kind == "iw":
                flush()
                for (t0, nb) in BL_OWN:
                    pa = proj_fm(wt, 0, 16, t0, nb)
                    kb.actf(iwt[:, t0 - OWN0:t0 - OWN0 + nb], pa[0:16, 0:nb], AF.Copy, [pa], [iwt], scale=0.25)
                kb.dma(kb.sp, iw_s[:, :], iwt[:, :], [iwt], [iw_s])
        flush()
        kb.barrier()

    hstack.close()
    if 2 in phases:
        gdn_phase(kb, nc, locals())

    if 3 in phases:
        dsa_phase(kb, nc, locals())
    if 4 in phases:
        out_phase(kb, nc, locals())

    kb.finish(outs)
    kb.close()
    return kb


def dsa_phase(kb, nc, L):
    identf, ident, ones_bf, flag = L["identf"], L["ident"], L["ones_bf"], L["flag"]
    kiT_s, KT_s, Vtok_s, qiT_s, QT_s, bzT_s, iw_s, ybT_s = (L[k] for k in (
        "kiT_s", "KT_s", "Vtok_s", "qiT_s", "QT_s", "bzT_s", "iw_s", "ybT_s"))
    with ExitStack() as p3:
        def sb(shape, dt, name):
            kb.nbuf += 1
            t = p3.enter_context(nc.sbuf_tensor(name, list(shape), dt))
            return Buf(t, name)

        def ps(shape, dt, name):
            t = p3.enter_context(nc.psum_tensor(name, list(shape), dt))
            return Buf(t, name, psum=True)

        pLR = Ring([ps([128, 512], F32, "p3_l%d" % i) for i in range(2)])
        pSc = ps([128, 512], F32, "p3_sc")
        ptb = ps([128, 1024], BF16, "p3_tb")
        pSTR = Ring([ps([128, 512], F32, "p3_st%d" % i) for i in range(2)])
        pO = ps([128, 1024], F32, "p3_o")

        kiT = sb([128, NTOK], BF16, "kiT")
        KT = sb([128, 2, NTOK], BF16, "KT")
        Vaug = sb([128, 17, 2, 129], BF16, "Vaug")
        negb = sb([128, 1], F32, "negb")
        Bpat = sb([128, 8], BF16, "Bpat")
        Ind = sb([128, 16, 128], BF16, "Ind")
        kb.dma(kb.sp, kiT[:], kiT_s[:, :], [kiT_s], [kiT])
        kb.dma(kb.sp, KT[:], KT_s[:].rearrange("j p t -> p j t"), [KT_s], [KT])
        kb.op(kb.pool, lambda e: e.memset(Vaug[:], 1.0), [], [Vaug])
        for j in range(2):
            kb.dma(kb.sp, Vaug[:, 0:16, j, 0:128], Vtok_s[0:2048, j * 128:(j + 1) * 128].rearrange("(a p) d -> p a d", p=128),
                   [Vtok_s], [Vaug])
        kb.dma(kb.sp, Vaug[0:32, 16, :, 0:128], Vtok_s[2048:2080, :].rearrange("p (j d) -> p j d", d=128),
               [Vtok_s], [Vaug])
        kb.ts(kb.dve, negb[:], flag[:], BIG, -BIG, ALU.mult, ALU.add, [flag], [negb])
        kb.op(kb.pool, lambda e: e.memset(Bpat[:], 0.0), [], [Bpat])
        for h in range(16):
            kb.op(kb.pool, lambda e, h=h: e.affine_select(out=Bpat[:], in_=Bpat[:], pattern=[[-1, 8]],
                                                          compare_op=ALU.not_equal, fill=1.0, base=-8 * h,
                                                          channel_multiplier=1), [Bpat], [Bpat])
        kb.op(kb.pool, lambda e: e.memset(Ind[:], 0.0), [], [Ind])
        for g in range(16):
            kb.cp(kb.pool, Ind[:, g, 8 * g:8 * g + 8], Bpat[:], [Bpat], [Ind])

        with ExitStack() as pp_:
            def sbp(shape, dt, name):
                kb.nbuf += 1
                t = pp_.enter_context(nc.sbuf_tensor(name, list(shape), dt))
                return Buf(t, name)

            qiR = [sbp([128, 16, 128], BF16, "qi%d" % i) for i in range(2)]
            qigR = [sbp([128, 16, 16, 8], BF16, "qig%d" % i) for i in range(2)]
            Qt = sbp([128, 16, 128], BF16, "Qt")
            bz = sbp([128, 16, 128], BF16, "bz")
            WrR = [sbp([128, 16], F32, "Wr%d" % i) for i in range(2)]
            WselR = [sbp([128, 16, 128], BF16, "Wsel%d" % i) for i in range(2)]
            scR = [sbp([128, 2048], F32, "sc%d" % i) for i in range(2)]
            wk = sbp([128, 2048], F32, "wk")
            m8 = sbp([128, 8], F32, "m8")
            thr = sbp([128, 1], F32, "thr")
            m01 = sbp([128, 2048], BF16, "m01")
            maskT = sbp([128, 16, 128], BF16, "maskT")
            PmA = sbp([128, 16, 4, 128], BF16, "PmA")
            PR = Ring([sbp([128, 512], BF16, "Pp%d" % i) for i in range(2)])
            RR = Ring([sbp([128, 512], BF16, "Rr%d" % i) for i in range(3)])
            otok = sbp([128, 16, 128], BF16, "otok")
            yb = sbp([128, 16, 128], BF16, "yb")
            rc = sbp([128, 4], F32, "rc")
            def indexer(qb):
                q0 = qb * 128
                nkb = 9 + qb
                N = nkb * 128
                qi, qig, Wr, Wsel, sc = qiR[qb % 2], qigR[qb % 2], WrR[qb % 2], WselR[qb % 2], scR[qb % 2]
                kb.dma(kb.sp, qi[:], qiT_s[:, :, q0:q0 + 128].rearrange("h p q -> p h q"), [qiT_s], [qi])
                for h in range(16):
                    kb.dma(kb.sp, Wr[8 * h:8 * h + 8, :], iw_s[h, q0:q0 + 128].rearrange("(g ql) -> ql g", ql=8),
                           [iw_s], [Wr], allow_slow_non_contiguous=True)
                kb.tt(kb.pool, Wsel[:], Ind[:], Wr[:, :].unsqueeze(2).to_broadcast([128, 16, 128]), ALU.mult,
                      [Ind, Wr], [Wsel])
                kb.cp(kb.pool, qig[:], qi[:].rearrange("p h (g ql) -> p g h ql", ql=8), [qi], [qig])
                for kc in range((N + 511) // 512):
                    n = min(512, N - kc * 512)
                    for g in range(16):
                        pL = pLR.get()
                        kb.mm(pL[:, 0:n], qig[:, g].rearrange("p h ql -> p (h ql)"), kiT[:, kc * 512:kc * 512 + n],
                              [qig, kiT], [pL])
                        R = RR.get()
                        kb.actf(R[:, 0:n], pL[:, 0:n], AF.Relu, [pL], [R])
                        kb.mm(pSc[:, 0:n], Wsel[:, g, :], R[:, 0:n], [Wsel, R], [pSc], start=(g == 0), stop=(g == 15))
                    kb.cp(kb.act, sc[:, kc * 512:kc * 512 + n], pSc[:, 0:n], [pSc], [sc])
            def rest(qb):
                q0 = qb * 128
                nkb = 9 + qb
                N = nkb * 128
                sc = scR[qb % 2]
                kb.dma(kb.sp, Qt[:], QT_s[:, :, q0:q0 + 128].rearrange("h p q -> p h q"), [QT_s], [Qt])
                kb.dma(kb.sp, bz[:], bzT_s[:, :, q0:q0 + 128].rearrange("h p q -> p h q"), [bzT_s], [bz])
                c0 = (nkb - 1) * 128
                kb.op(kb.pool, lambda e: e.affine_select(out=sc[:, c0:c0 + 128], in_=sc[:, c0:c0 + 128],
                                                         pattern=[[-1, 128]], compare_op=ALU.is_ge, fill=-BIG, base=0,
                                                         channel_multiplier=1), [sc], [sc])
                kb.ts(kb.dve, sc[:, 0:1024], sc[:, 0:1024], negb[:, 0:1], None, ALU.add, None, [sc, negb], [sc])
                kb.cp(kb.act, wk[:, 0:N], sc[:, 0:N], [sc], [wk])
                for r in range(32):
                    kb.op(kb.dve, lambda e: e.max(out=m8[:, 0:8], in_=wk[:, 0:N]), [wk], [m8])
                    if r < 31:
                        kb.op(kb.dve, lambda e: e.match_replace(out=wk[:, 0:N], in_to_replace=m8[:, 0:8],
                                                                in_values=wk[:, 0:N], imm_value=-3.0e38), [wk, m8], [wk])
                kb.ts(kb.dve, thr[:], m8[:, 7:8], -1.0e29, None, ALU.max, None, [m8], [thr])
                kb.ts(kb.dve, m01[:, 0:N], sc[:, 0:N], thr[:, 0:1], None, ALU.is_ge, None, [sc, thr], [m01])
                for kb0 in range(0, nkb, 8):
                    cnt = min(8, nkb - kb0)
                    for j in range(cnt):
                        kb.tr(ptb[:, j * 128:(j + 1) * 128], m01[:, (kb0 + j) * 128:(kb0 + j + 1) * 128], ident[:],
                              [m01, ident], [ptb], signal=(j == cnt - 1))
                    kb.cp(kb.act, maskT[:, kb0:kb0 + cnt, :], ptb[:, 0:cnt * 128].rearrange("p (a b) -> p a b", b=128),
                          [ptb], [maskT])
                for hg in range(4):
                    kvh = hg // 2
                    for kb_ in range(nkb):
                        pST = pSTR.get()
                        for hh in range(4):
                            kb.mm(pST[:, hh * 128:(hh + 1) * 128], KT[:, kvh, kb_ * 128:(kb_ + 1) * 128],
                                  Qt[:, 4 * hg + hh, :], [KT, Qt], [pST], signal=(hh == 3))
                        P = PR.get()
                        kb.actf(P[:, :], pST[:, :], AF.Exp, [pST], [P])
                        kb.tt(kb.dve, PmA[:, kb_, :, :], P[:, :].rearrange("p (a b) -> p a b", b=128),
                              maskT[:, kb_, :].unsqueeze(1).to_broadcast([128, 4, 128]), ALU.mult, [P, maskT], [PmA])
                    for hh in range(4):
                        for kb_ in range(nkb):
                            kb.mm(pO[:, hh * 256:hh * 256 + 129], PmA[:, kb_, hh, :], Vaug[:, kb_, kvh, :], [PmA, Vaug], [pO],
                                  start=(kb_ == 0), stop=(kb_ == nkb - 1), signal=(kb_ == nkb - 1 and hh == 3))
                    pOv = pO[:, :].rearrange("p (a b) -> p a b", b=256)
                    kb.op(kb.dve, lambda e: e.reciprocal(rc[:, :], pOv[:, :, 128]), [pO], [rc])
                    kb.tt(kb.dve, otok[:, 4 * hg:4 * hg + 4, :], pOv[:, :, 0:128],
                          rc[:, :].unsqueeze(2).to_broadcast([128, 4, 128]), ALU.mult, [pO, rc], [otok])
                for hf in range(2):
                    for j in range(8):
                        kb.tr(ptb[:, j * 128:(j + 1) * 128], otok[:, hf * 8 + j, :], ident[:], [otok, ident], [ptb],
                              signal=(j == 7))
                    kb.tt(kb.dve, yb[:, hf * 8:hf * 8 + 8, :], ptb[:, :].rearrange("p (a b) -> p a b", b=128),
                          bz[:, hf * 8:hf * 8 + 8, :], ALU.mult, [ptb, bz], [yb])
                kb.dma(kb.sp, ybT_s[:, :, q0:q0 + 128].rearrange("h p q -> p h q"), yb[:], [yb], [ybT_s])
            indexer(0)
            for qb in range(8):
                if qb + 1 < 8:
                    indexer(qb + 1)
                rest(qb)
            kb.barrier()
        if "nosample" not in L["dbg"]:
            dsa_sample(kb, nc, L, locals())
        else:
            zt = sb([128, 16, 32], BF16, "zt")
            kb.op(kb.pool, lambda e: e.memset(zt[:], 0.0), [], [zt])
            kb.dma(kb.sp, ybT_s[:, :, 1024:1056].rearrange("h p q -> p h q"), zt[:], [zt], [ybT_s])
        kb.barrier()


def dsa_sample(kb, nc, L, P3):
    identf, ident = L["identf"], L["ident"]
    kiT, KT, Bpat = P3["kiT"], P3["KT"], P3["Bpat"]
    pLR, pSc, ptb, pSTR, pO = P3["pLR"], P3["pSc"], P3["ptb"], P3["pSTR"], P3["pO"]
    qiT_s, QT_s, bzT_s, iw_s, ybT_s, Vtok_s = (L[k] for k in ("qiT_s", "QT_s", "bzT_s", "iw_s", "ybT_s", "Vtok_s"))
    cache_k, cache_v, cache_ki, pt_d, scr_sc, scr_nm = (L[k] for k in ("cache_k", "cache_v", "cache_ki", "pt_d", "scr_sc", "scr_nm"))
    acc = [pLR.bufs[0], pLR.bufs[1]]
    with ExitStack() as ps_:
        def sb(shape, dt, name):
            kb.nbuf += 1
            t = ps_.enter_context(nc.sbuf_tensor(name, list(shape), dt))
            return Buf(t, name)

        pti = sb([128, 1], I32, "pti")
        ptf = sb([128, 1], F32, "ptf")
        ptf8 = sb([128, 8], F32, "ptf8")
        idx8 = [sb([128, 8], I32, "idx8_%d" % i) for i in range(4)]
        tbv = sb([128, 8], F32, "tbv")
        kigR = Ring([sb([128, 8, 128], F32, "kig%d" % i) for i in range(2)])
        kgR = Ring([sb([128, 8, 256], F32, "kg%d" % i) for i in range(2)])
        vgR = Ring([sb([128, 8, 256], F32, "vg%d" % i) for i in range(2)])
        kTgR = Ring([sb([128, 1024], BF16, "kTg%d" % i) for i in range(2)])
        VgbR = Ring([sb([128, 8, 2, 129], BF16, "Vgb%d" % i) for i in range(2)])
        sc_s = sb([8, 8192], F32, "sc_s")
        scn = sb([8, 8], F32, "scn")
        negfill = sb([8, 15, 8], F32, "negfill")
        scb = sb([128, 4, 520], F32, "scb")
        cmpt = sb([128, 4, 520], BF16, "cmpt")
        negm = sb([128, 4, 520], BF16, "negm")
        negm_s = sb([8, 8192], BF16, "negm_s")
        negm_n = sb([8, 8], BF16, "negm_n")
        qis = sb([128, 16, 8], BF16, "qis")
        Wr8 = sb([128, 1], F32, "Wr8")
        Wq = sb([128, 8], BF16, "Wq")
        RR = Ring([sb([128, 512], BF16, "Rs%d" % i) for i in range(2)])
        PR = Ring([sb([128, 512], BF16, "Ps%d" % i) for i in range(2)])
        QTs = sb([128, 2, 8, 8], BF16, "QTs")
        bzs = sb([128, 16, 8], BF16, "bzs")
        ybs = sb([128, 16, 8], BF16, "ybs")
        Vn = sb([8, 2, 129], BF16, "Vn")
        E8 = sb([8, 8, 8], BF16, "E8")
        Pn = sb([8, 64], BF16, "Pn")
        ons = sb([64, 128], BF16, "ons")
        rcs = sb([64, 1], F32, "rcs")
        Bt = sb([8, 128], F32, "Bt")
        G16 = sb([128, 128], F32, "G16")
        lo = sb([128, 4], F32, "lo")
        hi = sb([128, 4], F32, "hi")
        mid = sb([128, 4], F32, "mid")
        cnt = sb([128, 4], F32, "cnt")
        ge = sb([128, 4], F32, "ge")
        d1 = sb([128, 4], F32, "d1")

        kb.op(kb.pool, lambda e: e.memset(negfill[:], -BIG), [], [negfill])
        kb.cp(kb.dve, E8[:], ident[0:8, 0:8].unsqueeze(1).to_broadcast([8, 8, 8]), [ident], [E8])
        for b_ in VgbR.bufs:
            kb.op(kb.pool, lambda e, b_=b_: e.memset(b_[:], 1.0), [], [b_])
        kb.op(kb.pool, lambda e: e.iota(tbv[:], pattern=[[1, 8]], base=0, channel_multiplier=0,
                                        allow_small_or_imprecise_dtypes=True), [], [tbv])
        kb.op(kb.pool, lambda e: e.memset(Bt[:], 1.0), [], [Bt])
        kb.op(kb.pool, lambda e: e.affine_select(out=Bt[:], in_=Bt[:], pattern=[[1, 128]], compare_op=ALU.is_ge, fill=0.0,
                                                 base=0, channel_multiplier=-16), [Bt], [Bt])
        kb.op(kb.pool, lambda e: e.affine_select(out=Bt[:], in_=Bt[:], pattern=[[-1, 128]], compare_op=ALU.is_ge, fill=0.0,
                                                 base=15, channel_multiplier=16), [Bt], [Bt])
        kb.mm(pSc[:, 0:128], Bt[:], Bt[:], [Bt], [pSc])
        kb.cp(kb.dve, G16[:], pSc[:, 0:128], [pSc], [G16])

        def seq_idx(s_):
            kb.dma(kb.sp, pti[0:64, 0:1], pt_d[s_:s_ + 1, :].rearrange("o n -> n o"), [pt_d], [pti])
            kb.dma(kb.sp, pti[64:128, 0:1], pt_d[s_:s_ + 1, :].rearrange("o n -> n o"), [pt_d], [pti])
            kb.cp(kb.dve, ptf[:], pti[:], [pti], [ptf])
            kb.ts(kb.dve, ptf[0:64], ptf[0:64], 16.0, None, ALU.mult, None, [ptf], [ptf])
            kb.ts(kb.dve, ptf[64:128], ptf[64:128], 16.0, 8.0, ALU.mult, ALU.add, [ptf], [ptf])
            kb.ts(kb.dve, ptf8[:], tbv[:], ptf[:, 0:1], None, ALU.add, None, [tbv, ptf], [ptf8])
            kb.cp(kb.dve, idx8[s_][:], ptf8[:], [ptf8], [idx8[s_]])

        for s_ in range(4):
            c0 = 1024 + 8 * s_
            seq_idx(s_)
            kb.dma(kb.sp, qis[:], qiT_s[:, :, c0:c0 + 8].rearrange("h p q -> p h q"), [qiT_s], [qis])
            for h in range(16):
                kb.dma(kb.sp, Wr8[8 * h:8 * h + 8, 0:1], iw_s[h:h + 1, c0:c0 + 8].rearrange("o q -> q o"), [iw_s], [Wr8])
            kb.ts(kb.dve, Wq[:], Bpat[:], Wr8[:, 0:1], None, ALU.mult, None, [Bpat, Wr8], [Wq])
            qis2 = qis[:].rearrange("p h q -> p (h q)")
            for tb in range(8):
                kig = kigR.get()
                kb.gather(kig[:].rearrange("p t d -> p (t d)"), cache_ki[:, :], idx8[s_][:, tb:tb + 1],
                          [cache_ki, idx8[s_]], [kig])
                for j in range(8):
                    kb.tr(pO[:, j * 128:(j + 1) * 128], kig[:, j, :], identf[:], [kig, identf], [pO], signal=(j == 7))
                kTg = kTgR.get()
                kb.cp(kb.act, kTg[:], pO[:, :], [pO], [kTg])
                for hf in range(2):
                    pL = pLR.get()
                    kb.mm(pL[:, :], qis2, kTg[:, hf * 512:(hf + 1) * 512], [qis, kTg], [pL])
                    R = RR.get()
                    kb.actf(R[:, :], pL[:, :], AF.Relu, [pL], [R])
                    kb.mm(pSc[0:8, :], Wq[:], R[:, :], [Wq, R], [pSc])
                    col = (tb * 2 + hf) * 512
                    kb.cp(kb.dve, sc_s[:, col:col + 512], pSc[0:8, :], [pSc], [sc_s])
            pL = pLR.get()
            kb.mm(pL[:, 0:8], qis2, kiT[:, SMP0 + 8 * s_:SMP0 + 8 * s_ + 8], [qis, kiT], [pL])
            R = RR.get()
            kb.actf(R[:, 0:8], pL[:, 0:8], AF.Relu, [pL], [R])
            kb.mm(pSc[0:8, 0:8], Wq[:], R[:, 0:8], [Wq, R], [pSc])
            kb.cp(kb.dve, scn[:], pSc[0:8, 0:8], [pSc], [scn])
            kb.op(kb.pool, lambda e: e.affine_select(out=scn[:], in_=scn[:], pattern=[[-1, 8]], compare_op=ALU.is_ge,
                                                     fill=-BIG, base=0, channel_multiplier=1), [scn], [scn])
            kb.dma(kb.sp, scr_sc[s_, :, :, 0:512], sc_s[:, :].rearrange("q (g c) -> q g c", c=512), [sc_s], [scr_sc])
            kb.dma(kb.sp, scr_sc[s_, :, 0, 512:520], scn[:], [scn], [scr_sc])
            kb.dma(kb.sp, scr_sc[s_, :, 1:16, 512:520], negfill[:], [negfill], [scr_sc])
        for s_ in range(4):
            kb.dma(kb.sp, scb[:, s_, :], scr_sc[s_].rearrange("q g c -> (q g) c"), [scr_sc], [scb])
        kb.op(kb.pool, lambda e: e.memset(lo[:], -1.0e4), [], [lo])
        for it in range(36):
            w_it = float(1.0e4 * (0.5 ** it))
            kb.ts(kb.dve, mid[:], lo[:], w_it, None, ALU.add, None, [lo], [mid])
            kb.tt(kb.dve, cmpt[:], scb[:], mid[:].unsqueeze(2).to_broadcast([128, 4, 520]), ALU.is_ge, [scb, mid], [cmpt])
            kb.op(kb.dve, lambda e: e.tensor_reduce(out=cnt[:], in_=cmpt[:], axis=AX.X, op=ALU.add), [cmpt], [cnt])
            kb.mm(pSc[:, 0:4], G16[:], cnt[:], [G16, cnt], [pSc])
            kb.ts(kb.dve, ge[:], pSc[:, 0:4], 256.0, w_it, ALU.is_ge, ALU.mult, [pSc], [ge])
            kb.tt(kb.dve, lo[:], lo[:], ge[:], ALU.add, [lo, ge], [lo])
        kb.tt(kb.dve, cmpt[:], scb[:], lo[:].unsqueeze(2).to_broadcast([128, 4, 520]), ALU.is_ge, [scb, lo], [cmpt])
        kb.ts(kb.dve, negm[:], cmpt[:], -1.0, 30000.0, ALU.add, ALU.mult, [cmpt], [negm])
        for s_ in range(4):
            kb.dma(kb.sp, scr_nm[s_].rearrange("q g c -> (q g) c"), negm[:, s_, :], [negm], [scr_nm])
        for s_ in range(4):
            c0 = 1024 + 8 * s_
            kb.dma(kb.sp, negm_s[:, :].rearrange("q (g c) -> q g c", c=512), scr_nm[s_, :, :, 0:512], [scr_nm], [negm_s])
            kb.dma(kb.sp, negm_n[:, :], scr_nm[s_, :, 0, 512:520], [scr_nm], [negm_n])
            for k_ in range(2):
                kb.dma(kb.sp, QTs[:, k_], QT_s[8 * k_:8 * k_ + 8, :, c0:c0 + 8].rearrange("h p q -> p h q"), [QT_s], [QTs])
            kb.dma(kb.sp, bzs[:], bzT_s[:, :, c0:c0 + 8].rearrange("h p q -> p h q"), [bzT_s], [bzs])
            kb.op(kb.pool, lambda e: e.memset(Vn[:], 1.0), [], [Vn])
            kb.dma(kb.sp, Vn[:, :, 0:128], Vtok_s[SMP0 + 8 * s_:SMP0 + 8 * s_ + 8, :].rearrange("p (j d) -> p j d", d=128),
                   [Vtok_s], [Vn])
            E8f = E8[:].rearrange("q h r -> q (h r)")
            for tb in range(8):
                kg = kgR.get()
                vg = vgR.get()
                kb.gather(kg[:].rearrange("p t d -> p (t d)"), cache_k[:, :], idx8[s_][:, tb:tb + 1], [cache_k, idx8[s_]], [kg])
                kb.gather(vg[:].rearrange("p t d -> p (t d)"), cache_v[:, :], idx8[s_][:, tb:tb + 1], [cache_v, idx8[s_]], [vg])
                Vgb = VgbR.get()
                kb.cp(kb.dve, Vgb[:, :, :, 0:128], vg[:].rearrange("p t (j d) -> p t j d", d=128), [vg], [Vgb])
                for k_ in range(2):
                    for j in range(8):
                        kb.tr(pO[:, j * 128:(j + 1) * 128], kg[:, j, k_ * 128:(k_ + 1) * 128], identf[:], [kg, identf], [pO],
                              signal=(j == 7))
                    kTg = kTgR.get()
                    kb.cp(kb.act, kTg[:], pO[:, :], [pO], [kTg])
                    pST = pSTR.get()
                    Qf = QTs[:, k_].rearrange("p h q -> p (h q)")
                    for j in range(8):
                        kb.mm(pST[:, j * 64:(j + 1) * 64], kTg[:, j * 128:(j + 1) * 128], Qf, [kTg, QTs], [pST],
                              start=True, stop=False)
                        kcol = (tb * 8 + j) * 128
                        kb.mm(pST[:, j * 64:(j + 1) * 64], negm_s[:, kcol:kcol + 128], E8f, [negm_s, E8], [pST],
                              start=False, stop=True, signal=(j == 7))
                    P = PR.get()
                    kb.actf(P[:, :], pST[:, :], AF.Exp, [pST], [P])
                    for j in range(8):
                        kb.mm(acc[k_][0:64, 0:129], P[:, j * 64:(j + 1) * 64], Vgb[:, j, k_, :], [P, Vgb], [acc[k_]],
                              start=(tb == 0 and j == 0), stop=False, signal=False)
            for k_ in range(2):
                Qf = QTs[:, k_].rearrange("p h q -> p (h q)")
                pST = pSTR.get()
                kb.mm(pST[0:8, 0:64], KT[:, k_, SMP0 + 8 * s_:SMP0 + 8 * s_ + 8], Qf, [KT, QTs], [pST], start=True, stop=False)
                kb.mm(pST[0:8, 0:64], negm_n[:, :], E8f, [negm_n, E8], [pST], start=False, stop=True)
                kb.actf(Pn[:, :], pST[0:8, 0:64], AF.Exp, [pST], [Pn])
                kb.mm(acc[k_][0:64, 0:129], Pn[:, :], Vn[:, k_, :], [Pn, Vn], [acc[k_]], start=False, stop=True)
                kb.op(kb.dve, lambda e: e.reciprocal(rcs[:, :], acc[k_][0:64, 128:129]), [acc[k_]], [rcs])
                kb.ts(kb.dve, ons[:, :], acc[k_][0:64, 0:128], rcs[:, 0:1], None, ALU.mult, None, [acc[k_], rcs], [ons])
                kb.tr(ptb[:, 0:64], ons[:, :], ident[0:64, 0:64], [ons, ident], [ptb])
                kb.tt(kb.dve, ybs[:, 8 * k_:8 * k_ + 8, :], ptb[:, 0:64].rearrange("p (h q) -> p h q", q=8),
                      bzs[:, 8 * k_:8 * k_ + 8, :], ALU.mult, [ptb, bzs], [ybs])
            kb.dma(kb.sp, ybT_s[:, :, c0:c0 + 8].rearrange("h p q -> p h q"), ybs[:], [ybs], [ybT_s])


def out_phase(kb, nc, L):
    yaT_s, ybT_s, gaT_s, gbT_s, gg_s = L["yaT_s"], L["ybT_s"], L["gaT_s"], L["gbT_s"], L["gg_s"]
    w_pa, w_pb, w_out, xp, xs, y_p, y_s = L["w_pa"], L["w_pb"], L["w_out"], L["xp"], L["xs"], L["y_p"], L["y_s"]
    BLK = [(0, 512), (512, 512), (1024, 32)]
    with ExitStack() as p4:
        def sb(shape, dt, name):
            kb.nbuf += 1
            t = p4.enter_context(nc.sbuf_tensor(name, list(shape), dt))
            return Buf(t, name)

        merged = sb([128, 16, 1056], BF16, "merged")
        with ExitStack() as pa_:
            def sba(shape, dt, name):
                kb.nbuf += 1
                t = pa_.enter_context(nc.sbuf_tensor(name, list(shape), dt))
                return Buf(t, name)

            def psa(shape, dt, name):
                t = pa_.enter_context(nc.psum_tensor(name, list(shape), dt))
                return Buf(t, name, psum=True)

            accR = Ring([psa([128, 512], F32, "p4_acc%d" % i) for i in range(3)])
            act_in = sba([128, 32, 1056], BF16, "act_in")
            wstgR = Ring([sba([128, 32, 128], F32, "wstg%d" % i) for i in range(2)])
            wbfR = Ring([sba([128, 32, 128], BF16, "wcb%d" % i) for i in range(2)])
            gtR = Ring([sba([128, 1056], BF16, "gt%d" % i) for i in range(2)])
            tmpR = Ring([sba([128, 512], F32, "tmp%d" % i) for i in range(2)])
            for which in range(2):
                nk = 32 if which == 0 else 16
                W = w_pa if which == 0 else w_pb
                src_act = yaT_s if which == 0 else ybT_s
                gsrc = gaT_s if which == 0 else gbT_s
                kb.dma(kb.sp, act_in[:, 0:nk, :], src_act[:].rearrange("j p t -> p j t"), [src_act], [act_in])
                for cj in range(16):
                    wstg = wstgR.get()
                    kb.dma(kb.sp, wstg[:, 0:nk, :], W[:, cj * 128:(cj + 1) * 128].rearrange("(a p) c -> p a c", p=128),
                           [W], [wstg])
                    wb = wbfR.get()
                    kb.cp(kb.dve, wb[:, 0:nk, :], wstg[:, 0:nk, :], [wstg], [wb])
                    gt = gtR.get()
                    kb.dma(kb.sp, gt[:], gsrc[cj], [gsrc], [gt])
                    for (t0, n) in BLK:
                        pa = accR.get()
                        for k in range(nk):
                            kb.mm(pa[:, 0:n], wb[:, k, :], act_in[:, k, t0:t0 + n], [wb, act_in], [pa],
                                  start=(k == 0), stop=(k == nk - 1))
                        if which == 0:
                            kb.tt(kb.dve, merged[:, cj, t0:t0 + n], pa[:, 0:n], gt[:, t0:t0 + n], ALU.mult,
                                  [pa, gt], [merged])
                        else:
                            tmp = tmpR.get()
                            kb.tt(kb.dve, tmp[:, 0:n], pa[:, 0:n], gt[:, t0:t0 + n], ALU.mult, [pa, gt], [tmp])
                            kb.tt(kb.dve, merged[:, cj, t0:t0 + n], merged[:, cj, t0:t0 + n], tmp[:, 0:n], ALU.add,
                                  [merged, tmp], [merged])
            kb.barrier()
        with ExitStack() as pb_:
            def sbb(shape, dt, name):
                kb.nbuf += 1
                t = pb_.enter_context(nc.sbuf_tensor(name, list(shape), dt))
                return Buf(t, name)

            def psb(shape, dt, name):
                t = pb_.enter_context(nc.psum_tensor(name, list(shape), dt))
                return Buf(t, name, psum=True)

            poR = Ring([psb([128, 2048], F32, "p4_o%d" % i) for i in range(2)])
            wo = sbb([128, 16, 2048], BF16, "wo")
            stgR = Ring([sbb([128, 4, 512], F32, "wos%d" % i) for i in range(2)])
            ggp = sbb([128, 2048], F32, "ggp4")
            ggs = sbb([32, 2048], F32, "ggs4")
            xR = Ring([sbb([128, 2048], F32, "x4_%d" % i) for i in range(2)])
            oR = Ring([sbb([128, 2048], F32, "o4_%d" % i) for i in range(2)])
            junk = sbb([128, 2048], BF16, "junk4")
            stR = Ring([sbb([128, 4], F32, "st4_%d" % i) for i in range(2)])
            for cb in range(4):
                for q in range(4):
                    st = stgR.get()
                    kb.dma(kb.sp, st[:], w_out[q * 512:(q + 1) * 512, cb * 512:(cb + 1) * 512].rearrange(
                        "(a p) c -> p a c", p=128), [w_out], [st])
                    kb.cp(kb.dve, wo[:, 4 * q:4 * q + 4, cb * 512:(cb + 1) * 512], st[:], [st], [wo])
            kb.dma(kb.sp, ggp[:], gg_s[0:128, :], [gg_s], [ggp])
            kb.dma(kb.sp, ggs[:], gg_s[128:160, :], [gg_s], [ggs])
            for ti in range(9):
                rows = 128 if ti < 8 else 32
                t0 = ti * 128
                xt = xR.get()
                kb.dma(kb.sp, xt[0:rows, :], xp[OWN0 + t0:OWN0 + t0 + 128, :] if ti < 8 else xs[:, :],
                       [xp if ti < 8 else xs], [xt])
                po = poR.get()
                for cb in range(4):
                    for k in range(16):
                        kb.mm(po[0:rows, cb * 512:(cb + 1) * 512], merged[:, k, t0:t0 + rows],
                              wo[:, k, cb * 512:(cb + 1) * 512], [merged, wo], [po], start=(k == 0), stop=(k == 15),
                              signal=(k == 15 and cb == 3))
                st = stR.get()
                kb.actf(junk[0:rows, :], po[0:rows, :], AF.Square, [po], [junk, st], accum_out=st[0:rows, 0:1])
                kb.ts(kb.dve, st[0:rows, 1:2], st[0:rows, 0:1], 1.0 / 2048, 1e-6, ALU.mult, ALU.add, [st], [st])
                kb.actf(st[0:rows, 2:3], st[0:rows, 1:2], AF.Sqrt, [st], [st])
                kb.op(kb.dve, lambda e: e.reciprocal(st[0:rows, 3:4], st[0:rows, 2:3]), [st], [st])
                ot = oR.get()
                gg = ggp if ti < 8 else ggs
                kb.stt(ot[0:rows, :], po[0:rows, :], st[0:rows, 3:4], gg[0:rows, :], ALU.mult, ALU.mult,
                       [po, st, gg], [ot])
                kb.tt(kb.dve, ot[0:rows, :], ot[0:rows, :], xt[0:rows, :], ALU.add, [ot, xt], [ot])
                if ti < 8:
                    kb.dma(kb.sp, y_p[t0:t0 + 128, :], ot[:, :], [ot], [y_p])
                else:
                    kb.dma(kb.sp, y_s[:, :], ot[0:32, :], [ot], [y_s])
            kb.barrier()


def gdn_phase(kb, nc, L):
    identf, ident, ones_bf = L["identf"], L["ident"], L["ones_bf"]
    gbc, gbs = L["gbc"], L["gbs"]
    kT_s, qT_s, ktok_s, vtok_s, zT_s, yaT_s = L["kT_s"], L["qT_s"], L["ktok_s"], L["vtok_s"], L["zT_s"], L["yaT_s"]
    sgdn, gdn_p, gdn_s, gdn_g = L["sgdn"], L["gdn_p"], L["gdn_s"], L["gdn_g"]
    NEG = -30000.0
    with ExitStack() as p2:
        def sb(shape, dt, name):
            kb.nbuf += 1
            t = p2.enter_context(nc.sbuf_tensor(name, list(shape), dt))
            return Buf(t, name)

        def ps(shape, dt, name):
            t = p2.enter_context(nc.psum_tensor(name, list(shape), dt))
            return Buf(t, name, psum=True)

        r1 = Ring([ps([128, 512], F32, "p2_a%d" % i) for i in range(3)])
        r2 = Ring([ps([128, 1024], F32, "p2_b%d" % i) for i in range(2)])
        ptb = ps([128, 1024], BF16, "p2_tb")

        U1f = sb([64, 64], F32, "U1f")
        Lsf = sb([64, 64], F32, "Lsf")
        U1b = sb([64, 64], BF16, "U1b")
        Lsb = sb([64, 64], BF16, "Lsb")
        NEG1 = sb([64, 8, 64], BF16, "NEG1")
        NEG2 = sb([64, 8, 64], BF16, "NEG2")
        I8 = sb([64, 8, 64], BF16, "I8")
        onesf = sb([64, 128], F32, "onesf")
        gcol = sb([128, 1], F32, "gcol")
        g1 = sb([1, 128], F32, "g1")
        for t_, base, cm, step in ((U1f, 0, -1, 1), (Lsf, -1, 1, -1)):
            kb.op(kb.pool, lambda e, t_=t_: e.memset(t_[:], 1.0), [], [t_])
            kb.op(kb.pool, lambda e, t_=t_, base=base, cm=cm, step=step: e.affine_select(
                out=t_[:], in_=t_[:], pattern=[[step, 64]], compare_op=ALU.is_ge, fill=0.0, base=base,
                channel_multiplier=cm), [t_], [t_])
        kb.cp(kb.dve, U1b[:], U1f[:], [U1f], [U1b])
        kb.cp(kb.dve, Lsb[:], Lsf[:], [Lsf], [Lsb])
        kb.op(kb.pool, lambda e: e.memset(NEG1[:], NEG), [], [NEG1])
        kb.op(kb.pool, lambda e: e.affine_select(out=NEG1[:], in_=NEG1[:], pattern=[[0, 8], [1, 64]], compare_op=ALU.is_ge,
                                                 fill=0.0, base=0, channel_multiplier=-1), [NEG1], [NEG1])
        kb.op(kb.pool, lambda e: e.memset(NEG2[:], NEG), [], [NEG2])
        kb.op(kb.pool, lambda e: e.affine_select(out=NEG2[:], in_=NEG2[:], pattern=[[0, 8], [-1, 64]], compare_op=ALU.is_ge,
                                                 fill=0.0, base=-1, channel_multiplier=1), [NEG2], [NEG2])
        kb.cp(kb.dve, I8[:], ident[0:64, 0:64].unsqueeze(1).to_broadcast([64, 8, 64]), [ident], [I8])
        kb.op(kb.pool, lambda e: e.memset(onesf[:], 1.0), [], [onesf])
        kb.dma(kb.sp, g1[:], gdn_g[:], [gdn_g], [g1])
        pp = r1.get()
        kb.tr(pp[:, 0:1], g1[:], identf[0:1, 0:1], [g1, identf], [pp])
        kb.cp(kb.dve, gcol[:], pp[:, 0:1], [pp], [gcol])

        kTg = sb([128, 4, NTOK], BF16, "kTg")
        qTg = sb([128, 4, 1056], BF16, "qTg")
        zTg = sb([128, 8, 1056], BF16, "zTg")
        yaT = sb([128, 8, 1056], BF16, "yaT")
        Sf = sb([128, 8, 128], F32, "Sf")
        Sb = sb([128, 8, 128], BF16, "Sb")
        Sd = sb([128, 8, 128], F32, "Sd")
        NSLOT = 3

        class Slot:
            pass

        slots = []
        for i in range(NSLOT):
            S_ = Slot()
            for nm, shape, dt in (("ktk", [64, 4, 128], BF16), ("vtk", [64, 8, 128], BF16), ("rg1", [64, 8, 64], BF16),
                                  ("rg2", [64, 8, 64], BF16), ("lnb", [64, 8], F32), ("lnbb", [64, 8, 64], BF16),
                                  ("E1", [64, 8, 64], BF16), ("ex", [64, 16], F32), ("gl", [128, 8], F32),
                                  ("A_", [64, 8, 64], BF16), ("Xa", [64, 8, 64], BF16), ("Xb", [64, 8, 64], BF16),
                                  ("XTa", [64, 8, 64], BF16), ("XTb", [64, 8, 64], BF16), ("TTa", [64, 8, 64], BF16),
                                  ("TTb", [64, 8, 64], BF16), ("ncbe", [64, 8], F32), ("nTC", [64, 8, 64], BF16),
                                  ("bv", [64, 8, 128], BF16), ("ktl", [64, 8, 128], BF16), ("ub", [64, 8, 128], BF16),
                                  ("deg", [64, 8, 64], BF16), ("qd", [128, 8, 64], BF16), ("E2T", [64, 8, 64], BF16),
                                  ("MT", [64, 8, 64], BF16)):
                setattr(S_, nm, sb(shape, dt, "%s_%d" % (nm, i)))
            slots.append(S_)

        def T2(name, shape, dt, n=2):
            return Ring([sb(shape, dt, "%s%d" % (name, i)) for i in range(n)])

        kSsR = T2("kSs", [64, 8, 128], BF16)
        usR = T2("us", [64, 8, 128], BF16)
        osqR = T2("osq", [64, 8, 128], F32, 1)
        stR = T2("ost", [64, 32], F32)
        onR = T2("on", [64, 8, 128], BF16)

        def gen_T(S_, C, nlev, own, kcs, qcs, gch, bch, gb_b, ctx):
            R = slice(0, C)
            ktk, vtk = S_.ktk[R, :, :], S_.vtk[R, :, :]

            def v3(t):
                return t[R, :, 0:C]

            def bh(ap2, X):
                return ap2.unsqueeze(1).to_broadcast([C, 8, X])

            def bl(ap2, X):
                return ap2.unsqueeze(2).to_broadcast([C, 8, X])

            rg1, lnb, lnbb, E1, ex, gl = S_.rg1, S_.lnb, S_.lnbb, S_.E1, S_.ex, S_.gl
            kb.tt(kb.dve, v3(rg1), bh(Lsf[R, 0:C], C), bl(gch, C), ALU.mult, [Lsf, gb_b], [rg1])
            kb.actf(lnb[R, :], bch, AF.Ln, [gb_b], [lnb])
            kb.cp(kb.act, v3(lnbb), bl(lnb[R, :], C), [lnb], [lnbb])
            pG = r1.get()
            pGv = pG[R, 0:8 * C].rearrange("p (h c) -> p h c", c=C)
            kb.mm(pGv, U1b[R, 0:C], v3(rg1), [U1b, rg1], [pG], start=True, stop=False)
            kb.mm(pGv, ident[R, 0:C], v3(lnbb), [ident, lnbb], [pG], start=False, stop=False)
            kb.mm(pGv, ident[R, 0:C], v3(NEG1), [ident, NEG1], [pG], start=False, stop=True)
            kb.actf(v3(E1), pGv, AF.Exp, [pG], [E1])
            psm = r1.get()
            kb.mm(psm[R, 0:8], U1f[R, 0:C], gch, [U1f, gb_b], [psm], signal=False)
            kb.mm(psm[R, 8:16], Lsf[R, 0:C], gch, [Lsf, gb_b], [psm], signal=False)
            kb.mm(psm[:, 16:24], onesf[R, :], gch, [onesf, gb_b], [psm], signal=True)
            kb.actf(ex[R, :], psm[R, 0:16], AF.Exp, [psm], [ex])
            kb.actf(gl[:, :], psm[:, 16:24], AF.Exp, [psm], [gl])
            eG, etail = ex[R, 0:8], ex[R, 8:16]
            yield
            pK = r1.get()
            pKv = pK[R, 0:8 * C].rearrange("p (h c) -> p h c", c=C)
            for j in range(4):
                kb.mm(pKv[:, j, :], kcs[j], kcs[j], [kTg], [pK], signal=(not own and j == 3))
            if own:
                for j in range(4):
                    kb.mm(pKv[:, 4 + j, :], kcs[j], qcs[j], [kTg, qTg], [pK], signal=(j == 3))
            A_ = S_.A_
            kb.tt(kb.dve, v3(A_).rearrange("p (j t) c -> p j t c", t=2),
                  pKv[:, 0:4, :].unsqueeze(2).to_broadcast([C, 4, 2, C]),
                  v3(E1).rearrange("p (j t) c -> p j t c", t=2), ALU.mult, [pK, E1], [A_])
            if own:
                rg2, E2T, MT = S_.rg2, S_.E2T, S_.MT
                kb.tt(kb.dve, v3(rg2), bh(U1f[R, 0:C], C), bl(gch, C), ALU.mult, [U1f, gb_b], [rg2])
                pG2 = r1.get()
                pG2v = pG2[R, 0:8 * C].rearrange("p (h c) -> p h c", c=C)
                kb.mm(pG2v, Lsb[R, 0:C], v3(rg2), [Lsb, rg2], [pG2], start=True, stop=False)
                kb.mm(pG2v, ident[R, 0:C], v3(NEG2), [ident, NEG2], [pG2], start=False, stop=True)
                kb.actf(v3(E2T), pG2v, AF.Exp, [pG2], [E2T])
                kb.tt(kb.dve, v3(MT).rearrange("p (j t) c -> p j t c", t=2),
                      pKv[:, 4:8, :].unsqueeze(2).to_broadcast([C, 4, 2, C]),
                      v3(E2T).rearrange("p (j t) c -> p j t c", t=2), ALU.mult, [pK, E2T], [MT])
            yield
            ptv = ptb[R, 0:8 * C].rearrange("p (h c) -> p h c", c=C)
            for h in range(8):
                kb.tr(ptv[:, h, :], A_[R, h, 0:C], ident[R, 0:C], [A_, ident], [ptb], signal=(h == 7))
            XT, TT = S_.XTa, S_.TTa
            kb.cp(kb.act, v3(XT), ptv, [ptb], [XT])
            kb.tt(kb.dve, v3(TT), v3(I8), ptv, ALU.subtract, [I8, ptb], [TT])
            X = A_
            yield
            for lev in range(1, nlev + 1):
                pXa = r1.get()
                pXav = pXa[R, 0:8 * C].rearrange("p (h c) -> p h c", c=C)
                for h in range(8):
                    kb.mm(pXav[:, h, :], XT[R, h, 0:C], X[R, h, 0:C], [XT, X], [pXa], signal=(h == 7))
                Xn = S_.Xa if (lev % 2 == 1) else S_.Xb
                kb.cp(kb.act, v3(Xn), pXav, [pXa], [Xn])
                if lev < nlev:
                    pXb = r1.get()
                    pXbv = pXb[R, 0:8 * C].rearrange("p (h c) -> p h c", c=C)
                    for h in range(8):
                        kb.mm(pXbv[:, h, :], X[R, h, 0:C], XT[R, h, 0:C], [XT, X], [pXb], signal=(h == 7))
                    XTn = S_.XTb if (lev % 2 == 1) else S_.XTa
                    kb.cp(kb.act, v3(XTn), pXbv, [pXb], [XTn])
                yield
                pT = r1.get()
                pTv = pT[R, 0:8 * C].rearrange("p (h c) -> p h c", c=C)
                for h in range(8):
                    kb.mm(pTv[:, h, :], Xn[R, h, 0:C], TT[R, h, 0:C], [Xn, TT], [pT], signal=(h == 7))
                TTn = S_.TTb if (lev % 2 == 1) else S_.TTa
                kb.tt(kb.dve, v3(TTn), pTv, v3(TT), ALU.add, [pT, TT], [TTn])
                X, TT = Xn, TTn
                if lev < nlev:
                    XT = XTn
                yield
            ncbe, nTC, bv, ktl, ub = S_.ncbe, S_.nTC, S_.bv, S_.ktl, S_.ub
            kb.stt(ncbe[R, :], eG, -1.0, bch, ALU.mult, ALU.mult, [ex, gb_b], [ncbe])
            kb.tt(kb.dve, v3(nTC), v3(TT), bl(ncbe[R, :], C), ALU.mult, [TT, ncbe], [nTC])
            kb.tt(kb.dve, bv[R, :, :], vtk, bl(bch, 128), ALU.mult, [S_.vtk, gb_b], [bv])
            kb.tt(kb.dve, ktl[R, :, :].rearrange("p (j t) d -> p j t d", t=2),
                  ktk.unsqueeze(2).to_broadcast([C, 4, 2, 128]),
                  etail.unsqueeze(2).to_broadcast([C, 8, 128]).rearrange("p (j t) d -> p j t d", t=2), ALU.mult,
                  [S_.ktk, ex], [ktl])
            pU = r2.get()
            pUv = pU[R, :].rearrange("p (h d) -> p h d", d=128)
            for h in range(8):
                kb.mm(pUv[:, h, :], TT[R, h, 0:C], bv[R, h, :], [TT, bv], [pU], signal=(h == 7))
            kb.cp(kb.act, ub[R, :, :], pUv, [pU], [ub])
            yield
            if own:
                deg, qd = S_.deg, S_.qd
                kb.tt(kb.dve, v3(deg), v3(I8), bl(eG, C), ALU.mult, [I8, ex], [deg])
                pE = r1.get()
                pEv = pE[:, 0:8 * C].rearrange("p (h c) -> p h c", c=C)
                kb.mm(pEv, ones_bf[R, :], v3(deg), [ones_bf, deg], [pE])
                qdv = qd[:, :, 0:C]
                for j in range(4):
                    kb.tt(kb.dve, qdv[:, 2 * j:2 * j + 2, :], pEv[:, 2 * j:2 * j + 2, :],
                          qcs[j].unsqueeze(1).to_broadcast([128, 2, C]), ALU.mult, [pE, qTg], [qd])
                yield

        def gen_scan(S_, C, own, kcs, zTs, yout, pre=None, post=None):
            R = slice(0, C)
            if pre is not None:
                pre()
            nTC, ub, ktl, gl, qd, MT = S_.nTC, S_.ub, S_.ktl, S_.gl, S_.qd, S_.MT
            pS = r2.get()
            pSv = pS[R, :].rearrange("p (h d) -> p h d", d=128)
            for h in range(8):
                kb.mm(pSv[:, h, :], kcs[h // 2], Sb[:, h, :], [kTg, Sb], [pS], signal=(h == 7))
            kSs = kSsR.get()
            kb.cp(kb.act, kSs[R, :, :], pSv, [pS], [kSs])
            yield
            pU2 = r2.get()
            pU2v = pU2[R, :].rearrange("p (h d) -> p h d", d=128)
            for h in range(8):
                kb.mm(pU2v[:, h, :], nTC[R, h, 0:C], kSs[R, h, :], [nTC, kSs], [pU2], signal=(h == 7))
            us = usR.get()
            kb.tt(kb.dve, us[R, :, :], pU2v, ub[R, :, :], ALU.add, [pU2, ub], [us])
            yield
            pD = r2.get()
            pDv = pD[:, :].rearrange("p (h d) -> p h d", d=128)
            for h in range(8):
                kb.mm(pDv[:, h, :], ktl[R, h, :], us[R, h, :], [ktl, us], [pD], signal=(h == 7))
            if own:
                pO = r2.get()
                pOv = pO[R, :].rearrange("p (h d) -> p h d", d=128)
                for h in range(8):
                    kb.mm(pOv[:, h, :], qd[:, h, 0:C], Sb[:, h, :], [qd, Sb], [pO], start=True, stop=False)
                    kb.mm(pOv[:, h, :], MT[R, h, 0:C], us[R, h, :], [MT, us], [pO], start=False, stop=True,
                          signal=(h == 7))
            kb.tt(kb.dve, Sd[:, :, :], Sf[:, :, :], gl[:, :].unsqueeze(2).to_broadcast([128, 8, 128]), ALU.mult,
                  [Sf, gl], [Sd])
            kb.tt(kb.dve, Sf[:, :, :], Sd[:, :, :], pDv, ALU.add, [Sd, pD], [Sf])
            kb.cp(kb.act, Sb[:, :, :], Sf[:, :, :], [Sf], [Sb])
            if own:
                osq = osqR.get()
                st = stR.get()
                kb.actf(osq[R, :, :], pOv, AF.Square, [pO], [osq])
                kb.op(kb.dve, lambda e: e.tensor_reduce(out=st[R, 0:8], in_=osq[R, :, :], axis=AX.X, op=ALU.add),
                      [osq], [st])
                kb.ts(kb.dve, st[R, 8:16], st[R, 0:8], 1.0 / 128, 1e-6, ALU.mult, ALU.add, [st], [st])
                kb.actf(st[R, 16:24], st[R, 8:16], AF.Sqrt, [st], [st])
                kb.op(kb.dve, lambda e: e.reciprocal(st[R, 24:32], st[R, 16:24]), [st], [st])
                on = onR.get()
                kb.tt(kb.dve, on[R, :, :], pOv, st[R, 24:32].unsqueeze(2).to_broadcast([C, 8, 128]), ALU.mult,
                      [pO, st], [on])
            if post is not None:
                post()
            yield
            if own:
                ptv2 = ptb[:, 0:8 * C].rearrange("p (h c) -> p h c", c=C)
                for h in range(8):
                    kb.tr(ptv2[:, h, :], on[R, h, :], ident[R, 0:C], [on, ident], [ptb], signal=(h == 7))
                kb.stt(yout, ptv2, gcol[:, 0:1], zTs, ALU.mult, ALU.mult, [ptb, gcol, zTg], [yaT])
                yield

        def step(gen):
            try:
                next(gen)
                return True
            except StopIteration:
                return False

        for g in range(4):
            kb.dma(kb.sp, kTg[:], kT_s[4 * g:4 * g + 4].rearrange("j p t -> p j t"), [kT_s], [kTg])
            kb.dma(kb.sp, qTg[:], qT_s[4 * g:4 * g + 4].rearrange("j p t -> p j t"), [qT_s], [qTg])
            kb.dma(kb.sp, zTg[:], zT_s[8 * g:8 * g + 8].rearrange("j p t -> p j t"), [zT_s], [zTg])
            kb.op(kb.pool, lambda e: e.memset(Sf[:], 0.0), [], [Sf])
            kb.op(kb.pool, lambda e: e.memset(Sb[:], 0.0), [], [Sb])
            items = []
            for c in range(32):
                own = c >= 16
                o0 = (c - 16) * 64
                items.append(dict(C=64, nlev=5, own=own, tok0=c * 64,
                                  kcs=[kTg[:, j, c * 64:(c + 1) * 64] for j in range(4)],
                                  qcs=[qTg[:, j, o0:o0 + 64] for j in range(4)] if own else None,
                                  gch=gbc[:, c, 32 + 8 * g:40 + 8 * g], bch=gbc[:, c, 8 * g:8 * g + 8], gb=gbc,
                                  zTs=zTg[:, :, o0:o0 + 64] if own else None,
                                  yout=yaT[:, :, o0:o0 + 64] if own else None, pre=None,
                                  post=(lambda g=g: kb.dma(kb.sp, gdn_p[8 * g:8 * g + 8].rearrange("h p d -> p h d"),
                                                           Sf[:], [Sf], [gdn_p])) if c == 31 else None))
            for s_ in range(4):
                t0 = SMP0 + 8 * s_
                o0 = 1024 + 8 * s_

                def pre(s_=s_, g=g):
                    kb.dma(kb.sp, Sf[:], sgdn[s_ * 32 + 8 * g:s_ * 32 + 8 * g + 8].rearrange("h p d -> p h d"), [sgdn], [Sf])
                    kb.cp(kb.act, Sb[:, :, :], Sf[:, :, :], [Sf], [Sb])

                def post(s_=s_, g=g):
                    kb.dma(kb.sp, gdn_s[s_ * 32 + 8 * g:s_ * 32 + 8 * g + 8].rearrange("h p d -> p h d"), Sf[:], [Sf], [gdn_s])

                items.append(dict(C=8, nlev=2, own=True, tok0=t0, kcs=[kTg[:, j, t0:t0 + 8] for j in range(4)],
                                  qcs=[qTg[:, j, o0:o0 + 8] for j in range(4)],
                                  gch=gbs[:, s_, 32 + 8 * g:40 + 8 * g], bch=gbs[:, s_, 8 * g:8 * g + 8], gb=gbs,
                                  zTs=zTg[:, :, o0:o0 + 8], yout=yaT[:, :, o0:o0 + 8], pre=pre, post=post))
            n_items = len(items)

            def make_T(i):
                it = items[i]
                S_ = slots[i % NSLOT]
                C = it["C"]
                kb.dma(kb.sp, S_.ktk[0:C], ktok_s[it["tok0"]:it["tok0"] + C, 4 * g:4 * g + 4, :], [ktok_s], [S_.ktk])
                kb.dma(kb.sp, S_.vtk[0:C], vtok_s[it["tok0"]:it["tok0"] + C, 8 * g:8 * g + 8, :], [vtok_s], [S_.vtk])
                return gen_T(S_, C, it["nlev"], it["own"], it["kcs"], it["qcs"], it["gch"], it["bch"], it["gb"], None)

            tg = {0: make_T(0)}
            while step(tg[0]):
                pass
            if n_items > 1:
                tg[1] = make_T(1)
            for i in range(n_items):
                it = items[i]
                if i + 2 < n_items:
                    tg[i + 2] = make_T(i + 2)
                sg = gen_scan(slots[i % NSLOT], it["C"], it["own"], it["kcs"], it["zTs"], it["yout"], it["pre"], it["post"])
                older = tg.get(i + 1)
                younger = tg.get(i + 2)
                alive_s, alive_o, rnd = True, older is not None, 0
                while alive_s or alive_o:
                    if alive_o:
                        alive_o = step(older)
                    if younger is not None and rnd % 2 == 0:
                        if not step(younger):
                            younger = None
                    if alive_s and rnd % 3 == 0:
                        alive_s = step(sg)
                    if not alive_o and alive_s:
                        alive_s = step(sg)
                    rnd += 1
                tg.pop(i, None)
            kb.dma(kb.sp, yaT_s[8 * g:8 * g + 8].rearrange("j p t -> p j t"), yaT[:], [yaT], [yaT_s])
        kb.barrier()


def make_in_maps(inp, phases=(0, 1, 2, 3, 4)):
    f = np.float32
    xp_all = np.asarray(inp["x_prompt"], f)
    maps = []
    ck = np.ascontiguousarray(np.asarray(inp["cache_k"], f)[0]).reshape(2560 * 16, 8 * 256)
    cv = np.ascontiguousarray(np.asarray(inp["cache_v"], f)[0]).reshape(2560 * 16, 8 * 256)
    cki = np.ascontiguousarray(np.asarray(inp["cache_kidx"], f)[0]).reshape(2560 * 16, 8 * 128)
    shared = {
        "w_ada": np.ascontiguousarray(np.asarray(inp["w_ada"], f)[0]),
        "b_ada": np.ascontiguousarray(np.asarray(inp["b_ada"], f)[0]).reshape(1, 6144),
        "pre_g": np.ascontiguousarray(np.asarray(inp["pre_norm_g"], f)[0]).reshape(16, 128),
        "w_in": np.ascontiguousarray(np.asarray(inp["w_in"], f)[0]),
        "conv_w": np.ascontiguousarray(np.asarray(inp["conv_w"], f)[0]),
        "a_log": np.ascontiguousarray(np.asarray(inp["a_log"], f)[0]).reshape(32, 1),
        "dt_bias": np.ascontiguousarray(np.asarray(inp["dt_bias"], f)[0]).reshape(32, 1),
        "gdn_g": np.ascontiguousarray(np.asarray(inp["gdn_norm_g"], f)[0]).reshape(1, 128),
        "w_pa": np.ascontiguousarray(np.asarray(inp["w_pa"], f)[0]),
        "w_pb": np.ascontiguousarray(np.asarray(inp["w_pb"], f)[0]),
        "w_out": np.ascontiguousarray(np.asarray(inp["w_out"], f)[0]),
        "post_g": np.ascontiguousarray(np.asarray(inp["post_norm_g"], f)[0]).reshape(1, 2048),
    }
    for c in range(8):
        b, half = c // 2, c % 2
        x = xp_all[b]
        xr = x if half == 1 else np.concatenate([x[1024:], x[:1024]], axis=0)
        m = dict(shared)
        m["xp"] = np.ascontiguousarray(xr)
        m["xs"] = np.ascontiguousarray(np.asarray(inp["x_sample"], f)[4 * c:4 * c + 4].reshape(32, 2048))
        m["cc"] = np.ascontiguousarray(np.concatenate([np.asarray(inp["c_prompt"], f)[b:b + 1],
                                                       np.asarray(inp["c_sample"], f)[4 * c:4 * c + 4]], axis=0))
        m["flag"] = np.full((128, 1), float(half), f)
        m["sconv"] = np.ascontiguousarray(np.asarray(inp["state_conv"], f)[0, 4 * c:4 * c + 4].reshape(12, 8192))
        if 2 in phases:
            m["sgdn"] = np.ascontiguousarray(np.asarray(inp["state_gdn"], f)[0, 4 * c:4 * c + 4].reshape(128, 128, 128))
        if 3 in phases:
            m["cache_k"] = ck
            m["cache_v"] = cv
            m["cache_ki"] = cki
            m["pt"] = np.ascontiguousarray(np.asarray(inp["page_table"], np.int32)[4 * c:4 * c + 4])
        maps.append(m)
    return maps


_CACHE = {}


def kernel(x_prompt, x_sample, c_prompt, c_sample, cache_k, cache_v, cache_kidx, state_gdn, state_conv, page_table,
           w_ada, b_ada, pre_norm_g, w_in, conv_w, a_log, dt_bias, gdn_norm_g, w_pa, w_pb, w_out, post_norm_g):
    inp = dict(x_prompt=x_prompt, x_sample=x_sample, c_prompt=c_prompt, c_sample=c_sample, cache_k=cache_k,
               cache_v=cache_v, cache_kidx=cache_kidx, state_gdn=state_gdn, state_conv=state_conv,
               page_table=page_table, w_ada=w_ada, b_ada=b_ada, pre_norm_g=pre_norm_g, w_in=w_in, conv_w=conv_w,
               a_log=a_log, dt_bias=dt_bias, gdn_norm_g=gdn_norm_g, w_pa=w_pa, w_pb=w_pb, w_out=w_out,
               post_norm_g=post_norm_g)
    if "kb" not in _CACHE:
        _CACHE["kb"] = build()
    kbd = _CACHE["kb"]
    maps = make_in_maps(inp)
    res = run_bass_kernel_spmd(kbd.nc, maps, core_ids=list(range(8)))
    R = res.results
    f = np.float32
    y_p = np.zeros((4, 2048, 2048), f)
    k_p = np.zeros((1, 4, 2048, 2, 128), f)
    v_p = np.zeros((1, 4, 2048, 2, 128), f)
    ki_p = np.zeros((1, 4, 2048, 128), f)
    gdn_p = np.zeros((1, 4, 32, 128, 128), f)
    conv_p = np.zeros((1, 4, 3, 8192), f)
    y_s = np.zeros((32, 8, 2048), f)
    k_s = np.zeros((1, 32, 8, 2, 128), f)
    v_s = np.zeros((1, 32, 8, 2, 128), f)
    ki_s = np.zeros((1, 32, 8, 128), f)
    gdn_s = np.zeros((1, 32, 32, 128, 128), f)
    conv_s = np.zeros((1, 32, 3, 8192), f)
    for c in range(8):
        b, half = c // 2, c % 2
        r = R[c]
        own = slice(1024, 2048) if half == 1 else slice(0, 1024)
        y_p[b, own] = np.asarray(r["y_p"], f)
        k_p[0, b, own] = np.asarray(r["k_p"], f).reshape(1024, 2, 128)
        v_p[0, b, own] = np.asarray(r["v_p"], f).reshape(1024, 2, 128)
        ki_p[0, b, own] = np.asarray(r["ki_p"], f)
        if half == 1:
            gdn_p[0, b] = np.asarray(r["gdn_p"], f)
            conv_p[0, b] = np.asarray(r["conv_p"], f)
        sl = slice(4 * c, 4 * c + 4)
        y_s[sl] = np.asarray(r["y_s"], f).reshape(4, 8, 2048)
        k_s[0, sl] = np.asarray(r["k_s"], f).reshape(4, 8, 2, 128)
        v_s[0, sl] = np.asarray(r["v_s"], f).reshape(4, 8, 2, 128)
        ki_s[0, sl] = np.asarray(r["ki_s"], f).reshape(4, 8, 128)
        gdn_s[0, sl] = np.asarray(r["gdn_s"], f).reshape(4, 32, 128, 128)
        conv_s[0, sl] = np.asarray(r["conv_s"], f).reshape(4, 3, 8192)
    return (y_p, y_s, k_p, v_p, ki_p, gdn_p, conv_p, k_s, v_s, ki_s, gdn_s, conv_s)
```

```python
import numpy as np
from contextlib import ExitStack
import concourse.bass as bass
import concourse.mybir as mybir
from concourse.bass_utils import run_bass_kernel_spmd

F32 = mybir.dt.float32
BF16 = mybir.dt.bfloat16
I32 = mybir.dt.int32
U32 = mybir.dt.uint32
AF = mybir.ActivationFunctionType
ALU = mybir.AluOpType
AX = mybir.AxisListType

D = 2048
NTOK = 2080
OWN0 = 1024
SMP0 = 2048
D_IN = 23248
BIG = 1.0e30


class Buf:
    __slots__ = ("t", "w", "r", "name", "psum")

    def __init__(self, t, name="", psum=False):
        self.t = t
        self.w = None
        self.r = {}
        self.name = name
        self.psum = psum

    def __getitem__(self, idx):
        return self.t[idx]


class Eng:
    def __init__(self, kb, eng, name, is_pe=False, ndma=0):
        self.kb = kb
        self.eng = eng
        self.name = name
        self.is_pe = is_pe
        self.sem = kb.newsem("s_" + name)
        self.count = 0
        self.known = {}
        self.pool = [[kb.newsem("d_%s%d" % (name, i)), 0] for i in range(ndma)]
        self.next = 0
        self.ninst = 0


class KB:
    def __init__(self):
        self.nc = bass.Bass("TRN2", target_bir_lowering=False)
        self.es = ExitStack()
        self.sems = []
        nc = self.nc
        self.pe = Eng(self, nc.tensor, "pe", is_pe=True)
        self.act = Eng(self, nc.scalar, "act", ndma=6)
        self.dve = Eng(self, nc.vector, "dve")
        self.pool = Eng(self, nc.gpsimd, "pool", ndma=8)
        self.sp = Eng(self, nc.sync, "sp", ndma=12)
        self.engs = [self.pe, self.act, self.dve, self.pool, self.sp]
        self.nbuf = 0

    def newsem(self, name):
        s = self.es.enter_context(self.nc.semaphore(name))
        self.sems.append(s)
        return len(self.sems) - 1

    def sbuf(self, shape, dt, name=None):
        self.nbuf += 1
        name = name or ("sb%d" % self.nbuf)
        t = self.es.enter_context(self.nc.sbuf_tensor(name, list(shape), dt))
        return Buf(t, name)

    def psum(self, shape, dt=F32, name=None):
        self.nbuf += 1
        name = name or ("ps%d" % self.nbuf)
        t = self.es.enter_context(self.nc.psum_tensor(name, list(shape), dt))
        return Buf(t, name, psum=True)

    def dram(self, name, shape, dt, kind="Internal"):
        t = self.nc.dram_tensor(name, list(shape), dt, kind=kind)
        return Buf(t.ap(), name)

    def _deps(self, reads, writes):
        deps = {}
        for b in reads:
            if b.w is not None:
                s, v = b.w
                if deps.get(s, 0) < v:
                    deps[s] = v
            if b.psum:
                for s, v in b.r.items():
                    if deps.get(s, 0) < v:
                        deps[s] = v
        for b in writes:
            if b.w is not None:
                s, v = b.w
                if deps.get(s, 0) < v:
                    deps[s] = v
            for s, v in b.r.items():
                if deps.get(s, 0) < v:
                    deps[s] = v
        return deps

    def _wait(self, E, deps):
        for s, v in deps.items():
            if E.is_pe and s == E.sem:
                continue
            if E.known.get(s, 0) < v:
                E.eng.wait_ge(self.sems[s], v)
                E.known[s] = v

    def _mark(self, tok, reads, writes):
        s, v = tok
        for b in reads:
            if b.r.get(s, 0) < v:
                b.r[s] = v
        for b in writes:
            b.w = tok
            b.r = {}

    def op(self, E, fn, reads=(), writes=(), signal=True):
        self._wait(E, self._deps(reads, writes))
        inst = fn(E.eng)
        E.ninst += 1
        if signal:
            E.count += 1
            inst.then_inc(self.sems[E.sem], 1)
            tok = (E.sem, E.count)
        else:
            tok = (E.sem, E.count + 1)
        self._mark(tok, reads, writes)
        return inst

    def dma(self, Q, out_ap, in_ap, reads=(), writes=(), **kw):
        slot = Q.pool[Q.next % len(Q.pool)]
        Q.next += 1
        deps = self._deps(reads, writes)
        if slot[1] > 0 and deps.get(slot[0], 0) < slot[1]:
            deps[slot[0]] = slot[1]
        self._wait(Q, deps)
        slot[1] += 16
        Q.eng.dma_start(out=out_ap, in_=in_ap, **kw).then_inc(self.sems[slot[0]], 16)
        Q.ninst += 1
        self._mark((slot[0], slot[1]), reads, writes)

    def gather(self, out_ap, table_ap, idx_ap, reads=(), writes=()):
        Q = self.pool
        slot = Q.pool[Q.next % len(Q.pool)]
        Q.next += 1
        deps = self._deps(reads, writes)
        if slot[1] > 0 and deps.get(slot[0], 0) < slot[1]:
            deps[slot[0]] = slot[1]
        self._wait(Q, deps)
        slot[1] += 16
        Q.eng.indirect_dma_start(out=out_ap, out_offset=None, in_=table_ap,
                                 in_offset=bass.IndirectOffsetOnAxis(ap=idx_ap, axis=0)).then_inc(self.sems[slot[0]], 16)
        Q.ninst += 1
        self._mark((slot[0], slot[1]), reads, writes)

    def finish(self, bufs):
        deps = {}
        for b in bufs:
            if b.w is not None:
                s, v = b.w
                if deps.get(s, 0) < v:
                    deps[s] = v
        for E in self.engs:
            for s, v in E.pool:
                if v > 0 and deps.get(s, 0) < v:
                    deps[s] = v
            if E.count > 0:
                deps[E.sem] = max(deps.get(E.sem, 0), E.count)
        self._wait(self.sp, deps)

    def barrier(self):
        deps = {}
        for E in self.engs:
            for s_, v in E.pool:
                if v > 0:
                    deps[s_] = v
            if E.count > 0:
                deps[E.sem] = E.count
        for E in self.engs:
            self._wait(E, dict(deps))

    def close(self):
        self.es.close()

    def mm(self, out, lhsT, rhs, reads, writes, start=True, stop=True, signal=None, **kw):
        if signal is None:
            signal = stop
        return self.op(self.pe, lambda e: e.matmul(out, lhsT, rhs, start=start, stop=stop, **kw),
                       reads, writes, signal=signal)

    def tr(self, out, in_, ident, reads, writes, signal=True):
        return self.op(self.pe, lambda e: e.transpose(out, in_, ident), reads, writes, signal=signal)

    def actf(self, out, in_, func, reads, writes, **kw):
        return self.op(self.act, lambda e: e.activation(out=out, in_=in_, func=func, **kw), reads, writes)

    def ts(self, E, out, in0, s1, s2, op0, op1, reads, writes, **kw):
        if op1 is None:
            return self.op(E, lambda e: e.tensor_scalar(out=out, in0=in0, scalar1=s1, scalar2=None, op0=op0, **kw),
                           reads, writes)
        return self.op(E, lambda e: e.tensor_scalar(out=out, in0=in0, scalar1=s1, scalar2=s2, op0=op0, op1=op1, **kw),
                       reads, writes)

    def tt(self, E, out, in0, in1, op, reads, writes):
        return self.op(E, lambda e: e.tensor_tensor(out=out, in0=in0, in1=in1, op=op), reads, writes)

    def stt(self, out, in0, scalar, in1, op0, op1, reads, writes):
        return self.op(self.dve, lambda e: e.scalar_tensor_tensor(out=out, in0=in0, scalar=scalar, in1=in1,
                                                                  op0=op0, op1=op1), reads, writes)

    def cp(self, E, out, in_, reads, writes):
        if E is self.act:
            return self.op(E, lambda e: e.activation(out=out, in_=in_, func=AF.Copy), reads, writes)
        return self.op(E, lambda e: e.tensor_copy(out, in_), reads, writes)


class Ring:
    def __init__(self, bufs):
        self.bufs = bufs
        self.i = 0

    def get(self):
        b = self.bufs[self.i % len(self.bufs)]
        self.i += 1
        return b


C_Q, C_K, C_V, C_Z = 0, 2048, 4096, 8192
C_BETA, C_DEC = 12288, 12320
C_BQ, C_BK, C_BV, C_BZ = 12352, 14400, 14656, 14912
C_IQ, C_IK, C_IW = 16960, 19008, 19136
C_GA, C_GB = 19152, 21200


def build(debug=(), phases=(0, 1, 2, 3, 4)):
    kb = KB()
    nc = kb.nc
    dbg = set(debug)

    def din(name, shape, dt=F32):
        return kb.dram(name, shape, dt, kind="ExternalInput")

    def dout(name, shape, dt=F32):
        return kb.dram(name, shape, dt, kind="ExternalOutput")

    def dscr(name, shape, dt):
        return kb.dram(name, shape, dt, kind=("ExternalOutput" if name in dbg else "Internal"))

    xp = din("xp", [2048, D])
    xs = din("xs", [32, D])
    cc = din("cc", [5, D])
    flag_d = din("flag", [128, 1])
    w_ada = din("w_ada", [D, 3 * D])
    b_ada = din("b_ada", [1, 3 * D])
    pre_g = din("pre_g", [16, 128])
    w_in = din("w_in", [D, D_IN])
    conv_w = din("conv_w", [4, 8192])
    alog_d = din("a_log", [32, 1])
    dtb_d = din("dt_bias", [32, 1])
    gdn_g = din("gdn_g", [1, 128])
    w_pa = din("w_pa", [4096, D])
    w_pb = din("w_pb", [D, D])
    w_out = din("w_out", [D, D])
    post_g = din("post_g", [1, D])
    sconv = din("sconv", [12, 8192])
    if 2 in phases:
        sgdn = din("sgdn", [128, 128, 128])
    if 3 in phases:
        cache_k = din("cache_k", [2560 * 16, 8 * 256])
        cache_v = din("cache_v", [2560 * 16, 8 * 256])
        cache_ki = din("cache_ki", [2560 * 16, 8 * 128])
        scr_sc = dscr("scr_sc", [4, 8, 16, 520], F32)
        scr_nm = dscr("scr_nm", [4, 8, 16, 520], BF16)
        pt_d = din("pt", [4, 64], I32)

    y_p = dout("y_p", [1024, D])
    y_s = dout("y_s", [32, D])
    k_p = dout("k_p", [1024, 256])
    v_p = dout("v_p", [1024, 256])
    ki_p = dout("ki_p", [1024, 128])
    gdn_p = dout("gdn_p", [32, 128, 128])
    conv_p = dout("conv_p", [3, 8192])
    k_s = dout("k_s", [32, 256])
    v_s = dout("v_s", [32, 256])
    ki_s = dout("ki_s", [32, 128])
    gdn_s = dout("gdn_s", [128, 128, 128])
    conv_s = dout("conv_s", [12, 8192])
    outs = [y_p, y_s, k_p, v_p, ki_p, gdn_p, conv_p, k_s, v_s, ki_s, gdn_s, conv_s]

    qT_s = dscr("qT_s", [16, 128, 1056], BF16)
    kT_s = dscr("kT_s", [16, 128, NTOK], BF16)
    ktok_s = dscr("ktok_s", [NTOK, 16, 128], BF16)
    vtok_s = dscr("vtok_s", [NTOK, 32, 128], BF16)
    zT_s = dscr("zT_s", [32, 128, 1056], BF16)
    QT_s = dscr("QT_s", [16, 128, 1056], BF16)
    KT_s = dscr("KT_s", [2, 128, NTOK], BF16)
    Vtok_s = dscr("Vtok_s", [NTOK, 256], BF16)
    kiT_s = dscr("kiT_s", [128, NTOK], BF16)
    bzT_s = dscr("bzT_s", [16, 128, 1056], BF16)
    qiT_s = dscr("qiT_s", [16, 128, 1056], BF16)
    iw_s = dscr("iw_s", [16, 1056], F32)
    gaT_s = dscr("gaT_s", [16, 128, 1056], BF16)
    gbT_s = dscr("gbT_s", [16, 128, 1056], BF16)
    yaT_s = dscr("yaT_s", [32, 128, 1056], BF16)
    ybT_s = dscr("ybT_s", [16, 128, 1056], BF16)
    hT_dbg = dscr("hT_dbg", [16, 128, NTOK], BF16) if "hT_dbg" in dbg else None
    gb_dbg = dscr("gb_dbg", [33, 64, 64], F32) if "gb_dbg" in dbg else None

    identf = kb.sbuf([128, 128], F32, "identf")
    ident = kb.sbuf([128, 128], BF16, "ident")
    ones_bf = kb.sbuf([128, 128], BF16, "ones_bf")
    flag = kb.sbuf([128, 1], F32, "flag_sb")
    kb.op(kb.pool, lambda e: e.memset(identf[:], 1.0), writes=[identf])
    kb.op(kb.pool, lambda e: e.affine_select(out=identf[:], in_=identf[:], pattern=[[-1, 128]], compare_op=ALU.is_equal,
                                             fill=0.0, base=0, channel_multiplier=1), reads=[identf], writes=[identf])
    kb.cp(kb.dve, ident[:], identf[:], [identf], [ident])
    kb.op(kb.pool, lambda e: e.memset(ones_bf[:], 1.0), writes=[ones_bf])
    kb.dma(kb.sp, flag[:], flag_d[:], [flag_d], [flag])

    gbc = kb.sbuf([64, 32, 64], F32, "gbc")
    gbs = kb.sbuf([8, 4, 64], F32, "gbs")
    hstack = ExitStack()
    hT = Buf(hstack.enter_context(nc.sbuf_tensor("hT", [128, 16, NTOK], BF16)), "hT")
    gg_s = dscr("gg_s", [160, D], F32)

    with ExitStack() as p0:
        def sb(shape, dt, name):
            kb.nbuf += 1
            t = p0.enter_context(nc.sbuf_tensor(name, list(shape), dt))
            return Buf(t, name)

        def ps(shape, dt, name):
            t = p0.enter_context(nc.psum_tensor(name, list(shape), dt))
            return Buf(t, name, psum=True)

        modA = sb([128, 16, 5], F32, "modA")
        modB = sb([128, 16, 5], F32, "modB")
        ggp = sb([128, D], F32, "ggp")
        ggs = sb([32, D], F32, "ggs")

        csb = sb([5, D], F32, "csb")
        scT = sb([128, 16, 8], F32, "scT")
        lhs_p = sb([128, 16, 128], F32, "lhs_p")
        lhs_s = sb([128, 16, 32], F32, "lhs_s")
        bg_bc = sb([128, D], F32, "bg_bc")
        postg_bc = sb([128, D], F32, "postg_bc")
        pregT = sb([128, 16], F32, "pregT")
        badaT = sb([128, 32], F32, "badaT")
        pg16 = sb([16, 128], F32, "pg16")
        ba32 = sb([32, 128], F32, "ba32")
        mt = Ring([sb([5, 256], F32, "mt%d" % i) for i in range(2)])
        pst = ps([128, 512], F32, "p0_t")
        psm = ps([128, 512], F32, "p0_m")
        psg = ps([128, 512], F32, "p0_g")
        psg2 = ps([32, 512], F32, "p0_g2")

        kb.dma(kb.sp, csb[:], cc[:], [cc], [csb])
        kb.dma(kb.sp, bg_bc[:], b_ada[:, 2 * D:3 * D].partition_broadcast(128), [b_ada], [bg_bc])
        kb.dma(kb.sp, postg_bc[:], post_g[:].partition_broadcast(128), [post_g], [postg_bc])
        kb.dma(kb.sp, pg16[:], pre_g[:], [pre_g], [pg16])
        kb.dma(kb.sp, ba32[:], b_ada[:, 0:2 * D].rearrange("o (j p) -> (o j) p", p=128), [b_ada], [ba32])
        kb.actf(csb[:], csb[:], AF.Silu, [csb], [csb])
        for j in range(16):
            kb.tr(pst[:, j * 8:j * 8 + 5], csb[:, j * 128:(j + 1) * 128], identf[0:5, 0:5], [csb, identf], [pst],
                  signal=(j == 15))
        kb.cp(kb.dve, scT[:, :, 0:5], pst[:, 0:128].rearrange("p (a b) -> p a b", b=8)[:, :, 0:5], [pst], [scT])
        kb.cp(kb.dve, lhs_p[:], scT[:, :, 0:1].to_broadcast([128, 16, 128]), [scT], [lhs_p])
        kb.cp(kb.dve, lhs_s[:].rearrange("p a (s i) -> p a s i", i=8),
              scT[:, :, 1:5].unsqueeze(3).to_broadcast([128, 16, 4, 8]), [scT], [lhs_s])
        kb.tr(pst[:, 256:272], pg16[:], identf[0:16, 0:16], [pg16, identf], [pst])
        kb.cp(kb.dve, pregT[:], pst[:, 256:272], [pst], [pregT])
        kb.tr(pst[:, 288:320], ba32[:], identf[0:32, 0:32], [ba32, identf], [pst])
        kb.cp(kb.dve, badaT[:], pst[:, 288:320], [pst], [badaT])

        wst32 = Ring([sb([128, 16, 256], F32, "wa%d" % i) for i in range(2)])
        for t in range(24):
            wt = wst32.get()
            for q in range(4):
                src = w_ada[q * 512:(q + 1) * 512, t * 256:(t + 1) * 256].rearrange("(a p) c -> p a c", p=128)
                kb.dma(kb.sp, wt[:, 4 * q:4 * q + 4, :], src, [w_ada], [wt])
            if t < 16:
                for k in range(16):
                    kb.mm(psm[0:5, 0:256], scT[:, k, 0:5], wt[:, k, :], [scT, wt], [psm], start=(k == 0), stop=(k == 15))
                m_ = mt.get()
                kb.cp(kb.dve, m_[:], psm[0:5, 0:256], [psm], [m_])
                for j in range(2):
                    jj = 2 * t + j
                    kb.tr(pst[:, jj * 8:jj * 8 + 5], m_[:, j * 128:(j + 1) * 128], identf[0:5, 0:5], [m_, identf], [pst],
                          signal=(j == 1))
            else:
                c0 = (t - 16) * 256
                for k in range(16):
                    kb.mm(psg[:, 0:256], lhs_p[:, k, :], wt[:, k, :], [lhs_p, wt], [psg], start=(k == 0), stop=(k == 15))
                for k in range(16):
                    kb.mm(psg2[:, 0:256], lhs_s[:, k, :], wt[:, k, :], [lhs_s, wt], [psg2], start=(k == 0), stop=(k == 15))
                kb.tt(kb.dve, ggp[:, c0:c0 + 256], psg[:, 0:256], bg_bc[:, c0:c0 + 256], ALU.add, [psg, bg_bc], [ggp])
                kb.tt(kb.dve, ggs[:, c0:c0 + 256], psg2[:, 0:256], bg_bc[0:32, c0:c0 + 256], ALU.add, [psg2, bg_bc], [ggs])
            if t == 15:
                pv = pst[:, 0:256].rearrange("p (a b) -> p a b", b=8)
                kb.tt(kb.dve, modB[:], pv[:, 0:16, 0:5], badaT[:, 0:16].unsqueeze(2).to_broadcast([128, 16, 5]), ALU.add,
                      [pst, badaT], [modB])
                kb.tt(kb.dve, modA[:], pv[:, 16:32, 0:5], badaT[:, 16:32].unsqueeze(2).to_broadcast([128, 16, 5]), ALU.add,
                      [pst, badaT], [modA])
                kb.ts(kb.dve, modA[:], modA[:], 1.0, None, ALU.add, None, [modA], [modA])
                kb.tt(kb.dve, modA[:], modA[:], pregT[:].unsqueeze(2).to_broadcast([128, 16, 5]), ALU.mult,
                      [modA, pregT], [modA])
        kb.tt(kb.dve, ggp[:], ggp[:], postg_bc[:], ALU.mult, [ggp, postg_bc], [ggp])
        kb.tt(kb.dve, ggs[:], ggs[:], postg_bc[0:32, :], ALU.mult, [ggs, postg_bc], [ggs])
        kb.dma(kb.sp, gg_s[0:128, :], ggp[:], [ggp], [gg_s])
        kb.dma(kb.sp, gg_s[128:160, :], ggs[:], [ggs], [gg_s])

        xring = Ring([sb([128, D], F32, "xt%d" % i) for i in range(2)])
        xnring = Ring([sb([128, D], BF16, "xn%d" % i) for i in range(2)])
        junk = sb([128, D], BF16, "junk")
        stat = Ring([sb([128, 4], F32, "stat%d" % i) for i in range(2)])
        ptr = Ring([ps([128, 1024], BF16, "p0_tr%d" % i) for i in range(2)])
        for ti in range(17):
            rows = 128 if ti < 16 else 32
            xt = xring.get()
            xn = xnring.get()
            st_ = stat.get()
            src = xp[ti * 128:(ti + 1) * 128, :] if ti < 16 else xs[:, :]
            kb.dma(kb.sp, xt[0:rows, :], src, [xp if ti < 16 else xs], [xt])
            kb.actf(junk[0:rows, :], xt[0:rows, :], AF.Square, [xt], [junk, st_], accum_out=st_[0:rows, 0:1])
            kb.ts(kb.dve, st_[0:rows, 1:2], st_[0:rows, 0:1], 1.0 / D, 1e-6, ALU.mult, ALU.add, [st_], [st_])
            kb.actf(st_[0:rows, 2:3], st_[0:rows, 1:2], AF.Sqrt, [st_], [st_])
            kb.op(kb.dve, lambda e: e.reciprocal(st_[0:rows, 3:4], st_[0:rows, 2:3]), [st_], [st_])
            kb.ts(kb.dve, xn[0:rows, :], xt[0:rows, :], st_[0:rows, 3:4], None, ALU.mult, None, [xt, st_], [xn])
            for hh in range(2):
                pt_ = ptr.get()
                for j in range(8):
                    jj = hh * 8 + j
                    kb.tr(pt_[:, j * 128:j * 128 + rows], xn[0:rows, jj * 128:(jj + 1) * 128], ident[0:rows, 0:rows],
                          [xn, ident], [pt_], signal=(j == 7))
                for j in range(8):
                    jj = hh * 8 + j
                    if ti < 16:
                        kb.actf(hT[:, jj, ti * 128:(ti + 1) * 128], pt_[:, j * 128:(j + 1) * 128], AF.Identity,
                                [pt_, modA, modB], [hT], scale=modA[:, jj, 0:1], bias=modB[:, jj, 0:1])
                    else:
                        for s in range(4):
                            kb.actf(hT[:, jj, SMP0 + 8 * s:SMP0 + 8 * s + 8], pt_[:, j * 128 + 8 * s:j * 128 + 8 * s + 8],
                                    AF.Identity, [pt_, modA, modB], [hT],
                                    scale=modA[:, jj, 1 + s:2 + s], bias=modB[:, jj, 1 + s:2 + s])
        if hT_dbg is not None:
            kb.dma(kb.sp, hT_dbg[:].rearrange("j p t -> p j t"), hT[:], [hT], [hT_dbg])
        kb.barrier()

    if 1 not in phases:
        kb.finish(outs)
        hstack.close()
        kb.close()
        return kb

    BL_ALL = [(0, 512), (512, 512), (1024, 512), (1536, 512), (2048, 32)]
    BL_OWN = [(1024, 512), (1536, 512), (2048, 32)]
    BL_Q = [(1021, 3)] + BL_OWN
    with ExitStack() as p1:
        def sb(shape, dt, name):
            kb.nbuf += 1
            t = p1.enter_context(nc.sbuf_tensor(name, list(shape), dt))
            return Buf(t, name)

        def ps(shape, dt, name):
            t = p1.enter_context(nc.psum_tensor(name, list(shape), dt))
            return Buf(t, name, psum=True)

        stage = Ring([sb([128, 4, 512], F32, "wst%d" % i) for i in range(2)])
        wbf = Ring([sb([128, 16, 512], BF16, "wbf%d" % i) for i in range(2)])
        acc = Ring([ps([128, 512], F32, "p1_acc%d" % i) for i in range(2)])
        pyr = Ring([ps([128, 512], F32, "p1_y%d" % i) for i in range(2)])
        pss = ps([128, 512], F32, "p1_ss")
        ptr = Ring([ps([128, 1024], BF16, "p1_tr%d" % i) for i in range(2)])

        def wload(c0, n):
            bt = wbf.get()
            for q in range(4):
                st = stage.get()
                src = w_in[q * 512:(q + 1) * 512, c0:c0 + n].rearrange("(a p) c -> p a c", p=128)
                kb.dma(kb.sp, st[:, :, 0:n], src, [w_in], [st])
                kb.cp(kb.pool if q == 0 else kb.dve, bt[:, 4 * q:4 * q + 4, 0:n], st[:, :, 0:n], [st], [bt])
            return bt

        def proj_fm(wt, col0, m, t0, n):
            pa = acc.get()
            for k in range(16):
                kb.mm(pa[0:m, 0:n], wt[:, k, col0:col0 + m], hT[:, k, t0:t0 + n], [wt, hT], [pa],
                      start=(k == 0), stop=(k == 15))
            return pa

        def proj_tm(wt, n, t0, rows):
            pa = acc.get()
            for k in range(16):
                kb.mm(pa[0:rows, 0:n], hT[:, k, t0:t0 + rows], wt[:, k, 0:n], [wt, hT], [pa],
                      start=(k == 0), stop=(k == 15))
            return pa

        cw_in = Ring([sb([128, 128], F32, "cw_in%d" % i) for i in range(2)])
        cwT = sb([128, 4, 64], F32, "cwT")
        sc_in = sb([12, 8192], F32, "sc_in") if False else None
        sconvT = sb([128, 64, 12], BF16, "sconvT")
        cwv = conv_w[:, :].rearrange("i (c p) -> (i c) p", p=128)
        for hh in range(2):
            t_ = cw_in.get()
            kb.dma(kb.sp, t_[:], cwv[hh * 128:(hh + 1) * 128, :], [conv_w], [t_])
            pp = pyr.get()
            kb.tr(pp[:, 0:128], t_[:], identf[:], [t_, identf], [pp])
            kb.cp(kb.dve, cwT[:, 2 * hh:2 * hh + 2, :], pp[:, 0:128].rearrange("p (i c) -> p i c", c=64), [pp], [cwT])
        scr = Ring([sb([12, 512], F32, "scr%d" % i) for i in range(2)])
        for qd in range(16):
            t_ = scr.get()
            kb.dma(kb.sp, t_[:], sconv[:, qd * 512:(qd + 1) * 512], [sconv], [t_])
            pp = pyr.get()
            for j in range(4):
                kb.tr(pp[:, j * 16:j * 16 + 12], t_[:, j * 128:(j + 1) * 128], identf[0:12, 0:12], [t_, identf], [pp],
                      signal=(j == 3))
            kb.cp(kb.dve, sconvT[:, qd * 4:(qd + 1) * 4, :],
                  pp[:, 0:64].rearrange("p (c b) -> p c b", b=16)[:, :, 0:12], [pp], [sconvT])

        dgr = Ring([sb([128, 4, 128], BF16, "dg%d" % i) for i in range(2)])
        abr = Ring([sb([128, 516], BF16, "ab%d" % i) for i in range(3)])
        sabr = Ring([sb([128, 4, 11], BF16, "sab%d" % i) for i in range(2)])
        ybr = Ring([sb([128, 512], F32, "yb%d" % i) for i in range(2)])
        sqr = Ring([sb([128, 512], BF16, "sq%d" % i) for i in range(2)])
        rsr = Ring([sb([128, 512], F32, "rs%d" % i) for i in range(2)])
        fmr = Ring([sb([128, NTOK], BF16, "fm%d" % i) for i in range(2)])
        tmr = Ring([sb([128, 17, 512], BF16, "tm%d" % i) for i in range(1)])
        tkf = Ring([sb([128, 512], F32, "tkf%d" % i) for i in range(2)])
        tkb = Ring([sb([128, 256], BF16, "tkb%d" % i) for i in range(2)])
        cvo = Ring([sb([35, 512], F32, "cvo%d" % i) for i in range(2)])
        gbT = sb([64, NTOK], F32, "gbT")
        gpar = sb([64, 4], F32, "gpar")
        iwt = sb([16, 1056], F32, "iwt")

        pipe = []

        def pstep(gen):
            try:
                next(gen)
                return True
            except StopIteration:
                return False

        def tick(newgen):
            alive = pstep(newgen)
            keep = [g_ for g_ in pipe if pstep(g_)]
            pipe[:] = keep
            if alive:
                pipe.append(newgen)

        def flush():
            while pipe:
                keep = [g_ for g_ in pipe if pstep(g_)]
                pipe[:] = keep

        def blk(kind, wt, cj, gch, dg, fm, tm, tmslot, t0, n, state, is_last):
            pa = proj_fm(wt, cj * 128, 128, t0, n)
            sample = t0 >= SMP0
            if not sample:
                ab = abr.get()
                kb.cp(kb.act, ab[:, 3:3 + n], pa[:, 0:n], [pa], [ab])
                if state["prev"] is None:
                    kb.op(kb.pool, lambda e: e.memset(ab[:, 0:3], 0.0), [], [ab])
                else:
                    pab, pn = state["prev"]
                    if t0 == OWN0:
                        kb.ts(kb.dve, ab[:, 0:3], pab[:, pn:pn + 3], flag[:, 0:1], None, ALU.mult, None,
                              [pab, flag], [ab])
                    else:
                        kb.cp(kb.dve, ab[:, 0:3], pab[:, pn:pn + 3], [pab], [ab])
                state["prev"] = (ab, n)
                if kind == "q" and t0 < OWN0:
                    return
            else:
                sab = sabr.get()
                kb.cp(kb.act, sab[:, :, 3:11], pa[:, 0:32].rearrange("p (s i) -> p s i", i=8), [pa], [sab])
                kb.cp(kb.dve, sab[:, :, 0:3], sconvT[:, gch, :].rearrange("p (s i) -> p s i", i=3), [sconvT], [sab])
            yield
            py = pyr.get()
            if not sample:
                for i in range(4):
                    kb.mm(py[:, 0:n], dg[:, i, :], ab[:, i:i + n], [dg, ab], [py], start=(i == 0), stop=(i == 3))
            else:
                for i in range(4):
                    kb.mm(py[:, 0:32], dg[:, i, :], sab[:, :, i:i + 8], [dg, sab], [py], start=(i == 0), stop=(i == 3))
            if kind == "v":
                yv = sqr.get()
                kb.actf(yv[:, 0:n], py[:, 0:n], AF.Silu, [py], [yv])
                src_fm, o0 = yv, 0
            else:
                yb = ybr.get()
                sq = sqr.get()
                kb.actf(yb[:, 0:n], py[:, 0:n], AF.Silu, [py], [yb])
                kb.tt(kb.dve, sq[:, 0:n], yb[:, 0:n], yb[:, 0:n], ALU.mult, [yb], [sq])
            yield
            if kind != "v":
                rs = rsr.get()
                kb.mm(pss[:, 0:n], ones_bf[:], sq[:, 0:n], [ones_bf, sq], [pss])
                kb.actf(rs[:, 0:n], pss[:, 0:n], AF.Sqrt, [pss], [rs], bias=epsb[:, 0:1])
                kb.op(kb.dve, lambda e: e.reciprocal(rs[:, 0:n], rs[:, 0:n]), [rs], [rs])
                if kind == "q":
                    o0 = t0 - OWN0
                    kb.stt(fm[:, o0:o0 + n], yb[:, 0:n], float(128 ** -0.5), rs[:, 0:n], ALU.mult, ALU.mult,
                           [yb, rs], [fm])
                else:
                    o0 = t0
                    if t0 < OWN0:
                        kb.stt(fm[:, o0:o0 + n], yb[:, 0:n], flag[:, 0:1], rs[:, 0:n], ALU.mult, ALU.mult,
                               [yb, rs, flag], [fm])
                    else:
                        kb.tt(kb.dve, fm[:, o0:o0 + n], yb[:, 0:n], rs[:, 0:n], ALU.mult, [yb, rs], [fm])
                src_fm = fm
            yield
            if kind != "q":
                nsub = (n + 127) // 128
                pt_ = ptr.get()
                for sbk in range(nsub):
                    w_ = min(128, n - sbk * 128)
                    so = (o0 + sbk * 128) if kind == "k" else sbk * 128
                    kb.tr(pt_[0:w_, sbk * 128:(sbk + 1) * 128], src_fm[:, so:so + w_], ident[:], [src_fm, ident], [pt_],
                          signal=(sbk == nsub - 1))
                ti0 = t0 // 128
                if n >= 128:
                    kb.cp(kb.dve, tm[:, ti0:ti0 + nsub, tmslot * 128:(tmslot + 1) * 128],
                          pt_[:, 0:nsub * 128].rearrange("p (a d) -> p a d", d=128), [pt_], [tm])
                else:
                    kb.cp(kb.dve, tm[0:n, ti0, tmslot * 128:(tmslot + 1) * 128], pt_[0:n, 0:128], [pt_], [tm])
            if is_last:
                if kind == "q":
                    kb.dma(kb.sp, qT_s[gch], fm[:, 0:1056], [fm], [qT_s])
                elif kind == "k":
                    kb.dma(kb.sp, kT_s[gch - 16], fm[:, :], [fm], [kT_s])

        def aqkv_chunk(wt, cj, gch, tm, tmslot):
            kind = "q" if gch < 16 else ("k" if gch < 32 else "v")
            dg = dgr.get()
            for i in range(4):
                kb.ts(kb.pool, dg[:, i, :], identf[:], cwT[:, i, gch:gch + 1], None, ALU.mult, None, [identf, cwT], [dg])
            fm = fmr.get() if kind != "v" else None
            state = {"prev": None}
            blocks = BL_Q if kind == "q" else BL_ALL
            for bi, (t0, n) in enumerate(blocks):
                tick(blk(kind, wt, cj, gch, dg, fm, tm, tmslot, t0, n, state, bi == len(blocks) - 1))

        def delayed(fn, nticks):
            for _ in range(nticks):
                yield
            fn()

        def simple_chunk(wt, cj, dst, func, scale=1.0):
            fm = fmr.get()
            for (t0, n) in BL_OWN:
                pa = proj_fm(wt, cj * 128, 128, t0, n)
                kb.actf(fm[:, t0 - OWN0:t0 - OWN0 + n], pa[:, 0:n], func, [pa], [fm], scale=scale)
            kb.dma(kb.sp, dst, fm[:, 0:1056], [fm], [dst_buf[0]])

        dst_buf = [None]
        epsb = sb([128, 1], F32, "epsb")
        kb.op(kb.pool, lambda e: e.memset(epsb[:], 1e-6), [], [epsb])

        tiles = []
        for t in range(16):
            tiles.append(("aqkv", t * 512, 512, t))
        for t in range(8):
            tiles.append(("z", C_Z + t * 512, 512, t))
        tiles.append(("bd", C_BETA, 64, 0))
        for t in range(4):
            tiles.append(("bq", C_BQ + t * 512, 512, t))
        tiles.append(("kv", C_BK, 512, 0))
        for t in range(4):
            tiles.append(("bz", C_BZ + t * 512, 512, t))
        for t in range(4):
            tiles.append(("iq", C_IQ + t * 512, 512, t))
        tiles.append(("ik", C_IK, 128, 0))
        tiles.append(("iw", C_IW, 16, 0))
        for t in range(4):
            tiles.append(("ga", C_GA + t * 512, 512, t))
        for t in range(4):
            tiles.append(("gb", C_GB + t * 512, 512, t))
        only = [x for x in debug if isinstance(x, tuple)]
        if only:
            tiles = [tl for tl in tiles if (tl[0] in only[0] or (tl[0], tl[3]) in only[0])]

        nxt = wload(tiles[0][1], tiles[0][2])
        for idx, (kind, c0, n, t) in enumerate(tiles):
            wt = nxt
            if idx + 1 < len(tiles):
                nxt = wload(tiles[idx + 1][1], tiles[idx + 1][2])
            if kind == "aqkv":
                pa = proj_tm(wt, 512, 2045, 35)
                cv = cvo.get()
                kb.cp(kb.dve, cv[:, :], pa[0:35, :], [pa], [cv])
                kb.dma(kb.sp, conv_p[:, c0:c0 + 512], cv[0:3, :], [cv], [conv_p])
                for s in range(4):
                    kb.dma(kb.sp, conv_s[3 * s:3 * s + 3, c0:c0 + 512], cv[3 + 8 * s + 5:3 + 8 * s + 8, :], [cv], [conv_s])
                gch0 = t * 4
                tm = tmr.get() if gch0 >= 16 else None
                for cj in range(4):
                    aqkv_chunk(wt, cj, gch0 + cj, tm, cj)
                if gch0 >= 16:
                    if gch0 < 32:
                        h0 = gch0 - 16
                        dstT, dstB = ktok_s, ktok_s
                    else:
                        h0 = gch0 - 32
                        dstT, dstB = vtok_s, vtok_s
                    def tm_dma(dstT=dstT, dstB=dstB, h0=h0, tm=tm):
                        kb.dma(kb.sp, dstT[0:2048, h0:h0 + 4, :].rearrange("(a p) h d -> p a (h d)", p=128),
                               tm[:, 0:16, :], [tm], [dstB])
                        kb.dma(kb.sp, dstT[2048:2080, h0:h0 + 4, :].rearrange("p h d -> p (h d)"),
                               tm[0:32, 16, :], [tm], [dstB])
                    pipe.append(delayed(tm_dma, 3))
            elif kind in ("z", "bq", "bz", "iq", "ga", "gb"):
                flush()
                dstT = {"z": zT_s, "bq": QT_s, "bz": bzT_s, "iq": qiT_s, "ga": gaT_s, "gb": gbT_s}[kind]
                func = {"z": AF.Silu, "bq": AF.Copy, "bz": AF.Silu, "iq": AF.Copy, "ga": AF.Sigmoid, "gb": AF.Sigmoid}[kind]
                scl = {"bq": float(128 ** -0.5), "iq": float(128 ** -0.5)}.get(kind, 1.0)
                dst_buf[0] = dstT
                for cj in range(4):
                    simple_chunk(wt, cj, dstT[t * 4 + cj], func, scl)
            elif kind == "bd":
                flush()
                kb.dma(kb.sp, gpar[32:64, 0:1], alog_d[:], [alog_d], [gpar])
                kb.dma(kb.sp, gpar[32:64, 1:2], dtb_d[:], [dtb_d], [gpar])
                kb.actf(gpar[32:64, 2:3], gpar[32:64, 0:1], AF.Exp, [gpar], [gpar])
                kb.ts(kb.dve, gpar[32:64, 2:3], gpar[32:64, 2:3], -1.0, None, ALU.mult, None, [gpar], [gpar])
                for (t0, nb) in BL_ALL:
                    pa = proj_fm(wt, 0, 64, t0, nb)
                    kb.actf(gbT[0:32, t0:t0 + nb], pa[0:32, 0:nb], AF.Sigmoid, [pa], [gbT])
                    kb.actf(gbT[32:64, t0:t0 + nb], pa[32:64, 0:nb], AF.Exp, [pa, gpar], [gbT], bias=gpar[32:64, 1:2])
                    kb.actf(gbT[32:64, t0:t0 + nb], gbT[32:64, t0:t0 + nb], AF.Ln, [gbT], [gbT], bias=1.0)
                    kb.ts(kb.dve, gbT[32:64, t0:t0 + nb], gbT[32:64, t0:t0 + nb], gpar[32:64, 2:3], None, ALU.mult, None,
                          [gbT, gpar], [gbT])
                for c8 in range(4):
                    pp = pyr.get()
                    for j in range(8):
                        c = c8 * 8 + j
                        kb.tr(pp[0:64, j * 64:(j + 1) * 64], gbT[:, c * 64:(c + 1) * 64], identf[0:64, 0:64],
                              [gbT, identf], [pp], signal=(j == 7))
                    kb.cp(kb.dve, gbc[:, c8 * 8:(c8 + 1) * 8, :], pp[0:64, :].rearrange("p (a b) -> p a b", b=64),
                          [pp], [gbc])
                pp = pyr.get()
                for s_ in range(4):
                    kb.tr(pp[0:8, s_ * 64:(s_ + 1) * 64], gbT[:, SMP0 + 8 * s_:SMP0 + 8 * s_ + 8], identf[0:64, 0:64],
                          [gbT, identf], [pp], signal=(s_ == 3))
                kb.cp(kb.dve, gbs[:, :, :], pp[0:8, 0:256].rearrange("p (a b) -> p a b", b=64), [pp], [gbs])
                if gb_dbg is not None:
                    kb.dma(kb.sp, gb_dbg[0:32].rearrange("a p c -> p a c"), gbc[:], [gbc], [gb_dbg])
                    kb.dma(kb.sp, gb_dbg[32, 0:32, :].rearrange("(s i) c -> i s c", i=8), gbs[:], [gbs], [gb_dbg])
            elif kind == "kv":
                flush()
                for kvh in range(2):
                    fm = fmr.get()
                    for (t0, nb) in BL_ALL:
                        pa = proj_fm(wt, kvh * 128, 128, t0, nb)
                        kb.cp(kb.act, fm[:, t0:t0 + nb], pa[:, 0:nb], [pa], [fm])
                    kb.dma(kb.sp, KT_s[kvh], fm[:, :], [fm], [KT_s])
                for ti in range(17):
                    rows = 128 if ti < 16 else 32
                    pa = proj_tm(wt, 512, ti * 128, rows)
                    vb = tkb.get()
                    kb.cp(kb.act, vb[0:rows, :], pa[0:rows, 256:512], [pa], [vb])
                    kb.dma(kb.sp, Vtok_s[ti * 128:ti * 128 + rows, :], vb[0:rows, :], [vb], [Vtok_s])
                    if ti >= 8:
                        tf = tkf.get()
                        kb.cp(kb.dve, tf[0:rows, :], pa[0:rows, :], [pa], [tf])
                        _q = kb.sp
                        if ti < 16:
                            r0 = ti * 128 - OWN0
                            kb.dma(_q, k_p[r0:r0 + 128, :], tf[:, 0:256], [tf], [k_p])
                            kb.dma(_q, v_p[r0:r0 + 128, :], tf[:, 256:512], [tf], [v_p])
                        else:
                            kb.dma(_q, k_s[:, :], tf[0:32, 0:256], [tf], [k_s])
                            kb.dma(_q, v_s[:, :], tf[0:32, 256:512], [tf], [v_s])
            elif kind == "ik":
                flush()
                fm = fmr.get()
                for (t0, nb) in BL_ALL:
                    pa = proj_fm(wt, 0, 128, t0, nb)
                    kb.cp(kb.act, fm[:, t0:t0 + nb], pa[:, 0:nb], [pa], [fm])
                kb.dma(kb.sp, kiT_s[:, :], fm[:, :], [fm], [kiT_s])
                for ti in range(8, 17):
                    rows = 128 if ti < 16 else 32
                    pa = proj_tm(wt, 128, ti * 128, rows)
                    tf = tkf.get()
                    kb.cp(kb.dve, tf[0:rows, 0:128], pa[0:rows, 0:128], [pa], [tf])
                    if ti < 16:
                        r0 = ti * 128 - OWN0
                        kb.dma(kb.sp, ki_p[r0:r0 + 128, :], tf[:, 0:128], [tf], [ki_p])
                    else:
                        kb.dma(kb.sp, ki_s[:, :], tf[0:32, 0:128], [tf], [ki_s])
            elif kind == "iw":
                flush()
                for (t0, nb) in BL_OWN:
                    pa = proj_fm(wt, 0, 16, t0, nb)
                    kb.actf(iwt[:, t0 - OWN0:t0 - OWN0 + nb], pa[0:16, 0:nb], AF.Copy, [pa], [iwt], scale=0.25)
                kb.dma(kb.sp, iw_s[:, :], iwt[:, :], [iwt], [iw_s])
        flush()
        kb.barrier()

    hstack.close()
    if 2 in phases:
        gdn_phase(kb, nc, locals())

    if 3 in phases:
        dsa_phase(kb, nc, locals())
    if 4 in phases:
        out_phase(kb, nc, locals())

    kb.finish(outs)
    kb.close()
    return kb


def dsa_phase(kb, nc, L):
    identf, ident, ones_bf, flag = L["identf"], L["ident"], L["ones_bf"], L["flag"]
    kiT_s, KT_s, Vtok_s, qiT_s, QT_s, bzT_s, iw_s, ybT_s = (L[k] for k in (
        "kiT_s", "KT_s", "Vtok_s", "qiT_s", "QT_s", "bzT_s", "iw_s", "ybT_s"))
    with ExitStack() as p3:
        def sb(shape, dt, name):
            kb.nbuf += 1
            t = p3.enter_context(nc.sbuf_tensor(name, list(shape), dt))
            return Buf(t, name)

        def ps(shape, dt, name):
            t = p3.enter_context(nc.psum_tensor(name, list(shape), dt))
            return Buf(t, name, psum=True)

        pLR = Ring([ps([128, 512], F32, "p3_l%d" % i) for i in range(2)])
        pSc = ps([128, 512], F32, "p3_sc")
        ptb = ps([128, 1024], BF16, "p3_tb")
        pSTR = Ring([ps([128, 512], F32, "p3_st%d" % i) for i in range(2)])
        pO = ps([128, 1024], F32, "p3_o")

        kiT = sb([128, NTOK], BF16, "kiT")
        KT = sb([128, 2, NTOK], BF16, "KT")
        Vaug = sb([128, 17, 2, 129], BF16, "Vaug")
        negb = sb([128, 1], F32, "negb")
        Bpat = sb([128, 8], BF16, "Bpat")
        Ind = sb([128, 16, 128], BF16, "Ind")
        kb.dma(kb.sp, kiT[:], kiT_s[:, :], [kiT_s], [kiT])
        kb.dma(kb.sp, KT[:], KT_s[:].rearrange("j p t -> p j t"), [KT_s], [KT])
        kb.op(kb.pool, lambda e: e.memset(Vaug[:], 1.0), [], [Vaug])
        for j in range(2):
            kb.dma(kb.sp, Vaug[:, 0:16, j, 0:128], Vtok_s[0:2048, j * 128:(j + 1) * 128].rearrange("(a p) d -> p a d", p=128),
                   [Vtok_s], [Vaug])
        kb.dma(kb.sp, Vaug[0:32, 16, :, 0:128], Vtok_s[2048:2080, :].rearrange("p (j d) -> p j d", d=128),
               [Vtok_s], [Vaug])
        kb.ts(kb.dve, negb[:], flag[:], BIG, -BIG, ALU.mult, ALU.add, [flag], [negb])
        kb.op(kb.pool, lambda e: e.memset(Bpat[:], 0.0), [], [Bpat])
        for h in range(16):
            kb.op(kb.pool, lambda e, h=h: e.affine_select(out=Bpat[:], in_=Bpat[:], pattern=[[-1, 8]],
                                                          compare_op=ALU.not_equal, fill=1.0, base=-8 * h,
                                                          channel_multiplier=1), [Bpat], [Bpat])
        kb.op(kb.pool, lambda e: e.memset(Ind[:], 0.0), [], [Ind])
        for g in range(16):
            kb.cp(kb.pool, Ind[:, g, 8 * g:8 * g + 8], Bpat[:], [Bpat], [Ind])

        with ExitStack() as pp_:
            def sbp(shape, dt, name):
                kb.nbuf += 1
                t = pp_.enter_context(nc.sbuf_tensor(name, list(shape), dt))
                return Buf(t, name)

            qiR = [sbp([128, 16, 128], BF16, "qi%d" % i) for i in range(2)]
            qigR = [sbp([128, 16, 16, 8], BF16, "qig%d" % i) for i in range(2)]
            Qt = sbp([128, 16, 128], BF16, "Qt")
            bz = sbp([128, 16, 128], BF16, "bz")
            WrR = [sbp([128, 16], F32, "Wr%d" % i) for i in range(2)]
            WselR = [sbp([128, 16, 128], BF16, "Wsel%d" % i) for i in range(2)]
            scR = [sbp([128, 2048], F32, "sc%d" % i) for i in range(2)]
            wk = sbp([128, 2048], F32, "wk")
            m8 = sbp([128, 8], F32, "m8")
            thr = sbp([128, 1], F32, "thr")
            m01 = sbp([128, 2048], BF16, "m01")
            maskT = sbp([128, 16, 128], BF16, "maskT")
            PmA = sbp([128, 16, 4, 128], BF16, "PmA")
            PR = Ring([sbp([128, 512], BF16, "Pp%d" % i) for i in range(2)])
            RR = Ring([sbp([128, 512], BF16, "Rr%d" % i) for i in range(3)])
            otok = sbp([128, 16, 128], BF16, "otok")
            yb = sbp([128, 16, 128], BF16, "yb")
            rc = sbp([128, 4], F32, "rc")
            def indexer(qb):
                q0 = qb * 128
                nkb = 9 + qb
                N = nkb * 128
                qi, qig, Wr, Wsel, sc = qiR[qb % 2], qigR[qb % 2], WrR[qb % 2], WselR[qb % 2], scR[qb % 2]
                kb.dma(kb.sp, qi[:], qiT_s[:, :, q0:q0 + 128].rearrange("h p q -> p h q"), [qiT_s], [qi])
                for h in range(16):
                    kb.dma(kb.sp, Wr[8 * h:8 * h + 8, :], iw_s[h, q0:q0 + 128].rearrange("(g ql) -> ql g", ql=8),
                           [iw_s], [Wr], allow_slow_non_contiguous=True)
                kb.tt(kb.pool, Wsel[:], Ind[:], Wr[:, :].unsqueeze(2).to_broadcast([128, 16, 128]), ALU.mult,
                      [Ind, Wr], [Wsel])
                kb.cp(kb.pool, qig[:], qi[:].rearrange("p h (g ql) -> p g h ql", ql=8), [qi], [qig])
                for kc in range((N + 511) // 512):
                    n = min(512, N - kc * 512)
                    for g in range(16):
                        pL = pLR.get()
                        kb.mm(pL[:, 0:n], qig[:, g].rearrange("p h ql -> p (h ql)"), kiT[:, kc * 512:kc * 512 + n],
                              [qig, kiT], [pL])
                        R = RR.get()
                        kb.actf(R[:, 0:n], pL[:, 0:n], AF.Relu, [pL], [R])
                        kb.mm(pSc[:, 0:n], Wsel[:, g, :], R[:, 0:n], [Wsel, R], [pSc], start=(g == 0), stop=(g == 15))
                    kb.cp(kb.act, sc[:, kc * 512:kc * 512 + n], pSc[:, 0:n], [pSc], [sc])
            def rest(qb):
                q0 = qb * 128
                nkb = 9 + qb
                N = nkb * 128
                sc = scR[qb % 2]
                kb.dma(kb.sp, Qt[:], QT_s[:, :, q0:q0 + 128].rearrange("h p q -> p h q"), [QT_s], [Qt])
                kb.dma(kb.sp, bz[:], bzT_s[:, :, q0:q0 + 128].rearrange("h p q -> p h q"), [bzT_s], [bz])
                c0 = (nkb - 1) * 128
                kb.op(kb.pool, lambda e: e.affine_select(out=sc[:, c0:c0 + 128], in_=sc[:, c0:c0 + 128],
                                                         pattern=[[-1, 128]], compare_op=ALU.is_ge, fill=-BIG, base=0,
                                                         channel_multiplier=1), [sc], [sc])
                kb.ts(kb.dve, sc[:, 0:1024], sc[:, 0:1024], negb[:, 0:1], None, ALU.add, None, [sc, negb], [sc])
                kb.cp(kb.act, wk[:, 0:N], sc[:, 0:N], [sc], [wk])
                for r in range(32):
                    kb.op(kb.dve, lambda e: e.max(out=m8[:, 0:8], in_=wk[:, 0:N]), [wk], [m8])
                    if r < 31:
                        kb.op(kb.dve, lambda e: e.match_replace(out=wk[:, 0:N], in_to_replace=m8[:, 0:8],
                                                                in_values=wk[:, 0:N], imm_value=-3.0e38), [wk, m8], [wk])
                kb.ts(kb.dve, thr[:], m8[:, 7:8], -1.0e29, None, ALU.max, None, [m8], [thr])
                kb.ts(kb.dve, m01[:, 0:N], sc[:, 0:N], thr[:, 0:1], None, ALU.is_ge, None, [sc, thr], [m01])
                for kb0 in range(0, nkb, 8):
                    cnt = min(8, nkb - kb0)
                    for j in range(cnt):
                        kb.tr(ptb[:, j * 128:(j + 1) * 128], m01[:, (kb0 + j) * 128:(kb0 + j + 1) * 128], ident[:],
                              [m01, ident], [ptb], signal=(j == cnt - 1))
                    kb.cp(kb.act, maskT[:, kb0:kb0 + cnt, :], ptb[:, 0:cnt * 128].rearrange("p (a b) -> p a b", b=128),
                          [ptb], [maskT])
                for hg in range(4):
                    kvh = hg // 2
                    for kb_ in range(nkb):
                        pST = pSTR.get()
                        for hh in range(4):
                            kb.mm(pST[:, hh * 128:(hh + 1) * 128], KT[:, kvh, kb_ * 128:(kb_ + 1) * 128],
                                  Qt[:, 4 * hg + hh, :], [KT, Qt], [pST], signal=(hh == 3))
                        P = PR.get()
                        kb.actf(P[:, :], pST[:, :], AF.Exp, [pST], [P])
                        kb.tt(kb.dve, PmA[:, kb_, :, :], P[:, :].rearrange("p (a b) -> p a b", b=128),
                              maskT[:, kb_, :].unsqueeze(1).to_broadcast([128, 4, 128]), ALU.mult, [P, maskT], [PmA])
                    for hh in range(4):
                        for kb_ in range(nkb):
                            kb.mm(pO[:, hh * 256:hh * 256 + 129], PmA[:, kb_, hh, :], Vaug[:, kb_, kvh, :], [PmA, Vaug], [pO],
                                  start=(kb_ == 0), stop=(kb_ == nkb - 1), signal=(kb_ == nkb - 1 and hh == 3))
                    pOv = pO[:, :].rearrange("p (a b) -> p a b", b=256)
                    kb.op(kb.dve, lambda e: e.reciprocal(rc[:, :], pOv[:, :, 128]), [pO], [rc])
                    kb.tt(kb.dve, otok[:, 4 * hg:4 * hg + 4, :], pOv[:, :, 0:128],
                          rc[:, :].unsqueeze(2).to_broadcast([128, 4, 128]), ALU.mult, [pO, rc], [otok])
                for hf in range(2):
                    for j in range(8):
                        kb.tr(ptb[:, j * 128:(j + 1) * 128], otok[:, hf * 8 + j, :], ident[:], [otok, ident], [ptb],
                              signal=(j == 7))
                    kb.tt(kb.dve, yb[:, hf * 8:hf * 8 + 8, :], ptb[:, :].rearrange("p (a b) -> p a b", b=128),
                          bz[:, hf * 8:hf * 8 + 8, :], ALU.mult, [ptb, bz], [yb])
                kb.dma(kb.sp, ybT_s[:, :, q0:q0 + 128].rearrange("h p q -> p h q"), yb[:], [yb], [ybT_s])
            indexer(0)
            for qb in range(8):
                if qb + 1 < 8:
                    indexer(qb + 1)
                rest(qb)
            kb.barrier()
        if "nosample" not in L["dbg"]:
            dsa_sample(kb, nc, L, locals())
        else:
            zt = sb([128, 16, 32], BF16, "zt")
            kb.op(kb.pool, lambda e: e.memset(zt[:], 0.0), [], [zt])
            kb.dma(kb.sp, ybT_s[:, :, 1024:1056].rearrange("h p q -> p h q"), zt[:], [zt], [ybT_s])
        kb.barrier()


def dsa_sample(kb, nc, L, P3):
    identf, ident = L["identf"], L["ident"]
    kiT, KT, Bpat = P3["kiT"], P3["KT"], P3["Bpat"]
    pLR, pSc, ptb, pSTR, pO = P3["pLR"], P3["pSc"], P3["ptb"], P3["pSTR"], P3["pO"]
    qiT_s, QT_s, bzT_s, iw_s, ybT_s, Vtok_s = (L[k] for k in ("qiT_s", "QT_s", "bzT_s", "iw_s", "ybT_s", "Vtok_s"))
    cache_k, cache_v, cache_ki, pt_d, scr_sc, scr_nm = (L[k] for k in ("cache_k", "cache_v", "cache_ki", "pt_d", "scr_sc", "scr_nm"))
    acc = [pLR.bufs[0], pLR.bufs[1]]
    with ExitStack() as ps_:
        def sb(shape, dt, name):
            kb.nbuf += 1
            t = ps_.enter_context(nc.sbuf_tensor(name, list(shape), dt))
            return Buf(t, name)

        pti = sb([128, 1], I32, "pti")
        ptf = sb([128, 1], F32, "ptf")
        ptf8 = sb([128, 8], F32, "ptf8")
        idx8 = [sb([128, 8], I32, "idx8_%d" % i) for i in range(4)]
        tbv = sb([128, 8], F32, "tbv")
        kigR = Ring([sb([128, 8, 128], F32, "kig%d" % i) for i in range(3)])
        kgR = Ring([sb([128, 8, 256], F32, "kg%d" % i) for i in range(3)])
        vgR = Ring([sb([128, 8, 256], F32, "vg%d" % i) for i in range(3)])
        kTgR = Ring([sb([128, 1024], BF16, "kTg%d" % i) for i in range(2)])
        VgbR = Ring([sb([128, 8, 2, 129], BF16, "Vgb%d" % i) for i in range(2)])
        sc_s = sb([8, 8192], F32, "sc_s")
        scn = sb([8, 8], F32, "scn")
        negfill = sb([8, 15, 8], F32, "negfill")
        scb = sb([128, 4, 520], F32, "scb")
        cmpt = sb([128, 4, 520], BF16, "cmpt")
        negm = sb([128, 4, 520], BF16, "negm")
        negm_s = sb([8, 8192], BF16, "negm_s")
        negm_n = sb([8, 8], BF16, "negm_n")
        qis = sb([128, 16, 8], BF16, "qis")
        Wr8 = sb([128, 1], F32, "Wr8")
        Wq = sb([128, 8], BF16, "Wq")
        RR = Ring([sb([128, 512], BF16, "Rs%d" % i) for i in range(2)])
        PR = Ring([sb([128, 512], BF16, "Ps%d" % i) for i in range(2)])
        QTs = sb([128, 2, 8, 8], BF16, "QTs")
        bzs = sb([128, 16, 8], BF16, "bzs")
        ybs = sb([128, 16, 8], BF16, "ybs")
        Vn = sb([8, 2, 129], BF16, "Vn")
        E8 = sb([8, 8, 8], BF16, "E8")
        Pn = sb([8, 64], BF16, "Pn")
        ons = sb([64, 128], BF16, "ons")
        rcs = sb([64, 1], F32, "rcs")
        Bt = sb([8, 128], F32, "Bt")
        G16 = sb([128, 128], F32, "G16")
        lo = sb([128, 4], F32, "lo")
        hi = sb([128, 4], F32, "hi")
        mid = sb([128, 4], F32, "mid")
        cnt = sb([128, 4], F32, "cnt")
        ge = sb([128, 4], F32, "ge")
        d1 = sb([128, 4], F32, "d1")

        kb.op(kb.pool, lambda e: e.memset(negfill[:], -BIG), [], [negfill])
        kb.cp(kb.dve, E8[:], ident[0:8, 0:8].unsqueeze(1).to_broadcast([8, 8, 8]), [ident], [E8])
        for b_ in VgbR.bufs:
            kb.op(kb.pool, lambda e, b_=b_: e.memset(b_[:], 1.0), [], [b_])
        kb.op(kb.pool, lambda e: e.iota(tbv[:], pattern=[[1, 8]], base=0, channel_multiplier=0,
                                        allow_small_or_imprecise_dtypes=True), [], [tbv])
        kb.op(kb.pool, lambda e: e.memset(Bt[:], 1.0), [], [Bt])
        kb.op(kb.pool, lambda e: e.affine_select(out=Bt[:], in_=Bt[:], pattern=[[1, 128]], compare_op=ALU.is_ge, fill=0.0,
                                                 base=0, channel_multiplier=-16), [Bt], [Bt])
        kb.op(kb.pool, lambda e: e.affine_select(out=Bt[:], in_=Bt[:], pattern=[[-1, 128]], compare_op=ALU.is_ge, fill=0.0,
                                                 base=15, channel_multiplier=16), [Bt], [Bt])
        kb.mm(pSc[:, 0:128], Bt[:], Bt[:], [Bt], [pSc])
        kb.cp(kb.dve, G16[:], pSc[:, 0:128], [pSc], [G16])

        def seq_idx(s_):
            kb.dma(kb.sp, pti[0:64, 0:1], pt_d[s_:s_ + 1, :].rearrange("o n -> n o"), [pt_d], [pti])
            kb.dma(kb.sp, pti[64:128, 0:1], pt_d[s_:s_ + 1, :].rearrange("o n -> n o"), [pt_d], [pti])
            kb.cp(kb.dve, ptf[:], pti[:], [pti], [ptf])
            kb.ts(kb.dve, ptf[0:64], ptf[0:64], 16.0, None, ALU.mult, None, [ptf], [ptf])
            kb.ts(kb.dve, ptf[64:128], ptf[64:128], 16.0, 8.0, ALU.mult, ALU.add, [ptf], [ptf])
            kb.ts(kb.dve, ptf8[:], tbv[:], ptf[:, 0:1], None, ALU.add, None, [tbv, ptf], [ptf8])
            kb.cp(kb.dve, idx8[s_][:], ptf8[:], [ptf8], [idx8[s_]])

        for s_ in range(4):
            c0 = 1024 + 8 * s_
            seq_idx(s_)
            kb.dma(kb.sp, qis[:], qiT_s[:, :, c0:c0 + 8].rearrange("h p q -> p h q"), [qiT_s], [qis])
            for h in range(16):
                kb.dma(kb.sp, Wr8[8 * h:8 * h + 8, 0:1], iw_s[h:h + 1, c0:c0 + 8].rearrange("o q -> q o"), [iw_s], [Wr8])
            kb.ts(kb.dve, Wq[:], Bpat[:], Wr8[:, 0:1], None, ALU.mult, None, [Bpat, Wr8], [Wq])
            qis2 = qis[:].rearrange("p h q -> p (h q)")
            for tb in range(8):
                kig = kigR.get()
                kb.gather(kig[:].rearrange("p t d -> p (t d)"), cache_ki[:, :], idx8[s_][:, tb:tb + 1],
                          [cache_ki, idx8[s_]], [kig])
                for j in range(8):
                    kb.tr(pO[:, j * 128:(j + 1) * 128], kig[:, j, :], identf[:], [kig, identf], [pO], signal=(j == 7))
                kTg = kTgR.get()
                kb.cp(kb.act, kTg[:], pO[:, :], [pO], [kTg])
                for hf in range(2):
                    pL = pLR.get()
                    kb.mm(pL[:, :], qis2, kTg[:, hf * 512:(hf + 1) * 512], [qis, kTg], [pL])
                    R = RR.get()
                    kb.actf(R[:, :], pL[:, :], AF.Relu, [pL], [R])
                    kb.mm(pSc[0:8, :], Wq[:], R[:, :], [Wq, R], [pSc])
                    col = (tb * 2 + hf) * 512
                    kb.cp(kb.dve, sc_s[:, col:col + 512], pSc[0:8, :], [pSc], [sc_s])
            pL = pLR.get()
            kb.mm(pL[:, 0:8], qis2, kiT[:, SMP0 + 8 * s_:SMP0 + 8 * s_ + 8], [qis, kiT], [pL])
            R = RR.get()
            kb.actf(R[:, 0:8], pL[:, 0:8], AF.Relu, [pL], [R])
            kb.mm(pSc[0:8, 0:8], Wq[:], R[:, 0:8], [Wq, R], [pSc])
            kb.cp(kb.dve, scn[:], pSc[0:8, 0:8], [pSc], [scn])
            kb.op(kb.pool, lambda e: e.affine_select(out=scn[:], in_=scn[:], pattern=[[-1, 8]], compare_op=ALU.is_ge,
                                                     fill=-BIG, base=0, channel_multiplier=1), [scn], [scn])
            kb.dma(kb.sp, scr_sc[s_, :, :, 0:512], sc_s[:, :].rearrange("q (g c) -> q g c", c=512), [sc_s], [scr_sc])
            kb.dma(kb.sp, scr_sc[s_, :, 0, 512:520], scn[:], [scn], [scr_sc])
            kb.dma(kb.sp, scr_sc[s_, :, 1:16, 512:520], negfill[:], [negfill], [scr_sc])
        for s_ in range(4):
            kb.dma(kb.sp, scb[:, s_, :], scr_sc[s_].rearrange("q g c -> (q g) c"), [scr_sc], [scb])
        kb.op(kb.pool, lambda e: e.memset(lo[:], -1.0e4), [], [lo])
        for it in range(36):
            w_it = float(1.0e4 * (0.5 ** it))
            kb.ts(kb.dve, mid[:], lo[:], w_it, None, ALU.add, None, [lo], [mid])
            kb.tt(kb.dve, cmpt[:], scb[:], mid[:].unsqueeze(2).to_broadcast([128, 4, 520]), ALU.is_ge, [scb, mid], [cmpt])
            kb.op(kb.dve, lambda e: e.tensor_reduce(out=cnt[:], in_=cmpt[:], axis=AX.X, op=ALU.add), [cmpt], [cnt])
            kb.mm(pSc[:, 0:4], G16[:], cnt[:], [G16, cnt], [pSc])
            kb.ts(kb.dve, ge[:], pSc[:, 0:4], 256.0, w_it, ALU.is_ge, ALU.mult, [pSc], [ge])
            kb.tt(kb.dve, lo[:], lo[:], ge[:], ALU.add, [lo, ge], [lo])
        kb.tt(kb.dve, cmpt[:], scb[:], lo[:].unsqueeze(2).to_broadcast([128, 4, 520]), ALU.is_ge, [scb, lo], [cmpt])
        kb.ts(kb.dve, negm[:], cmpt[:], -1.0, 30000.0, ALU.add, ALU.mult, [cmpt], [negm])
        for s_ in range(4):
            kb.dma(kb.sp, scr_nm[s_].rearrange("q g c -> (q g) c"), negm[:, s_, :], [negm], [scr_nm])
        for s_ in range(4):
            c0 = 1024 + 8 * s_
            kb.dma(kb.sp, negm_s[:, :].rearrange("q (g c) -> q g c", c=512), scr_nm[s_, :, :, 0:512], [scr_nm], [negm_s])
            kb.dma(kb.sp, negm_n[:, :], scr_nm[s_, :, 0, 512:520], [scr_nm], [negm_n])
            for k_ in range(2):
                kb.dma(kb.sp, QTs[:, k_], QT_s[8 * k_:8 * k_ + 8, :, c0:c0 + 8].rearrange("h p q -> p h q"), [QT_s], [QTs])
            kb.dma(kb.sp, bzs[:], bzT_s[:, :, c0:c0 + 8].rearrange("h p q -> p h q"), [bzT_s], [bzs])
            kb.op(kb.pool, lambda e: e.memset(Vn[:], 1.0), [], [Vn])
            kb.dma(kb.sp, Vn[:, :, 0:128], Vtok_s[SMP0 + 8 * s_:SMP0 + 8 * s_ + 8, :].rearrange("p (j d) -> p j d", d=128),
                   [Vtok_s], [Vn])
            E8f = E8[:].rearrange("q h r -> q (h r)")
            for tb in range(8):
                kg = kgR.get()
                vg = vgR.get()
                kb.gather(kg[:].rearrange("p t d -> p (t d)"), cache_k[:, :], idx8[s_][:, tb:tb + 1], [cache_k, idx8[s_]], [kg])
                kb.gather(vg[:].rearrange("p t d -> p (t d)"), cache_v[:, :], idx8[s_][:, tb:tb + 1], [cache_v, idx8[s_]], [vg])
                Vgb = VgbR.get()
                kb.cp(kb.dve, Vgb[:, :, :, 0:128], vg[:].rearrange("p t (j d) -> p t j d", d=128), [vg], [Vgb])
                for k_ in range(2):
                    for j in range(8):
                        kb.tr(pO[:, j * 128:(j + 1) * 128], kg[:, j, k_ * 128:(k_ + 1) * 128], identf[:], [kg, identf], [pO],
                              signal=(j == 7))
                    kTg = kTgR.get()
                    kb.cp(kb.act, kTg[:], pO[:, :], [pO], [kTg])
                    pST = pSTR.get()
                    Qf = QTs[:, k_].rearrange("p h q -> p (h q)")
                    for j in range(8):
                        kb.mm(pST[:, j * 64:(j + 1) * 64], kTg[:, j * 128:(j + 1) * 128], Qf, [kTg, QTs], [pST],
                              start=True, stop=False)
                        kcol = (tb * 8 + j) * 128
                        kb.mm(pST[:, j * 64:(j + 1) * 64], negm_s[:, kcol:kcol + 128], E8f, [negm_s, E8], [pST],
                              start=False, stop=True, signal=(j == 7))
                    P = PR.get()
                    kb.actf(P[:, :], pST[:, :], AF.Exp, [pST], [P])
                    for j in range(8):
                        kb.mm(acc[k_][0:64, 0:129], P[:, j * 64:(j + 1) * 64], Vgb[:, j, k_, :], [P, Vgb], [acc[k_]],
                              start=(tb == 0 and j == 0), stop=False, signal=False)
            for k_ in range(2):
                Qf = QTs[:, k_].rearrange("p h q -> p (h q)")
                pST = pSTR.get()
                kb.mm(pST[0:8, 0:64], KT[:, k_, SMP0 + 8 * s_:SMP0 + 8 * s_ + 8], Qf, [KT, QTs], [pST], start=True, stop=False)
                kb.mm(pST[0:8, 0:64], negm_n[:, :], E8f, [negm_n, E8], [pST], start=False, stop=True)
                kb.actf(Pn[:, :], pST[0:8, 0:64], AF.Exp, [pST], [Pn])
                kb.mm(acc[k_][0:64, 0:129], Pn[:, :], Vn[:, k_, :], [Pn, Vn], [acc[k_]], start=False, stop=True)
                kb.op(kb.dve, lambda e: e.reciprocal(rcs[:, :], acc[k_][0:64, 128:129]), [acc[k_]], [rcs])
                kb.ts(kb.dve, ons[:, :], acc[k_][0:64, 0:128], rcs[:, 0:1], None, ALU.mult, None, [acc[k_], rcs], [ons])
                kb.tr(ptb[:, 0:64], ons[:, :], ident[0:64, 0:64], [ons, ident], [ptb])
                kb.tt(kb.dve, ybs[:, 8 * k_:8 * k_ + 8, :], ptb[:, 0:64].rearrange("p (h q) -> p h q", q=8),
                      bzs[:, 8 * k_:8 * k_ + 8, :], ALU.mult, [ptb, bzs], [ybs])
            kb.dma(kb.sp, ybT_s[:, :, c0:c0 + 8].rearrange("h p q -> p h q"), ybs[:], [ybs], [ybT_s])


def out_phase(kb, nc, L):
    yaT_s, ybT_s, gaT_s, gbT_s, gg_s = L["yaT_s"], L["ybT_s"], L["gaT_s"], L["gbT_s"], L["gg_s"]
    w_pa, w_pb, w_out, xp, xs, y_p, y_s = L["w_pa"], L["w_pb"], L["w_out"], L["xp"], L["xs"], L["y_p"], L["y_s"]
    BLK = [(0, 512), (512, 512), (1024, 32)]
    with ExitStack() as p4:
        def sb(shape, dt, name):
            kb.nbuf += 1
            t = p4.enter_context(nc.sbuf_tensor(name, list(shape), dt))
            return Buf(t, name)

        merged = sb([128, 16, 1056], BF16, "merged")
        with ExitStack() as pa_:
            def sba(shape, dt, name):
                kb.nbuf += 1
                t = pa_.enter_context(nc.sbuf_tensor(name, list(shape), dt))
                return Buf(t, name)

            def psa(shape, dt, name):
                t = pa_.enter_context(nc.psum_tensor(name, list(shape), dt))
                return Buf(t, name, psum=True)

            accR = Ring([psa([128, 512], F32, "p4_acc%d" % i) for i in range(3)])
            act_in = sba([128, 32, 1056], BF16, "act_in")
            wstgR = Ring([sba([128, 32, 128], F32, "wstg%d" % i) for i in range(2)])
            wbfR = Ring([sba([128, 32, 128], BF16, "wcb%d" % i) for i in range(2)])
            gtR = Ring([sba([128, 1056], BF16, "gt%d" % i) for i in range(2)])
            tmpR = Ring([sba([128, 512], F32, "tmp%d" % i) for i in range(2)])
            for which in range(2):
                nk = 32 if which == 0 else 16
                W = w_pa if which == 0 else w_pb
                src_act = yaT_s if which == 0 else ybT_s
                gsrc = gaT_s if which == 0 else gbT_s
                kb.dma(kb.sp, act_in[:, 0:nk, :], src_act[:].rearrange("j p t -> p j t"), [src_act], [act_in])
                for cj in range(16):
                    wstg = wstgR.get()
                    kb.dma(kb.sp, wstg[:, 0:nk, :], W[:, cj * 128:(cj + 1) * 128].rearrange("(a p) c -> p a c", p=128),
                           [W], [wstg])
                    wb = wbfR.get()
                    kb.cp(kb.dve, wb[:, 0:nk, :], wstg[:, 0:nk, :], [wstg], [wb])
                    gt = gtR.get()
                    kb.dma(kb.sp, gt[:], gsrc[cj], [gsrc], [gt])
                    for (t0, n) in BLK:
                        pa = accR.get()
                        for k in range(nk):
                            kb.mm(pa[:, 0:n], wb[:, k, :], act_in[:, k, t0:t0 + n], [wb, act_in], [pa],
                                  start=(k == 0), stop=(k == nk - 1))
                        if which == 0:
                            kb.tt(kb.dve, merged[:, cj, t0:t0 + n], pa[:, 0:n], gt[:, t0:t0 + n], ALU.mult,
                                  [pa, gt], [merged])
                        else:
                            tmp = tmpR.get()
                            kb.tt(kb.dve, tmp[:, 0:n], pa[:, 0:n], gt[:, t0:t0 + n], ALU.mult, [pa, gt], [tmp])
                            kb.tt(kb.dve, merged[:, cj, t0:t0 + n], merged[:, cj, t0:t0 + n], tmp[:, 0:n], ALU.add,
                                  [merged, tmp], [merged])
            kb.barrier()
        with ExitStack() as pb_:
            def sbb(shape, dt, name):
                kb.nbuf += 1
                t = pb_.enter_context(nc.sbuf_tensor(name, list(shape), dt))
                return Buf(t, name)

            def psb(shape, dt, name):
                t = pb_.enter_context(nc.psum_tensor(name, list(shape), dt))
                return Buf(t, name, psum=True)

            poR = Ring([psb([128, 2048], F32, "p4_o%d" % i) for i in range(2)])
            wo = sbb([128, 16, 2048], BF16, "wo")
            stgR = Ring([sbb([128, 4, 512], F32, "wos%d" % i) for i in range(2)])
            ggp = sbb([128, 2048], F32, "ggp4")
            ggs = sbb([32, 2048], F32, "ggs4")
            xR = Ring([sbb([128, 2048], F32, "x4_%d" % i) for i in range(2)])
            oR = Ring([sbb([128, 2048], F32, "o4_%d" % i) for i in range(2)])
            junk = sbb([128, 2048], BF16, "junk4")
            stR = Ring([sbb([128, 4], F32, "st4_%d" % i) for i in range(2)])
            for cb in range(4):
                for q in range(4):
                    st = stgR.get()
                    kb.dma(kb.sp, st[:], w_out[q * 512:(q + 1) * 512, cb * 512:(cb + 1) * 512].rearrange(
                        "(a p) c -> p a c", p=128), [w_out], [st])
                    kb.cp(kb.dve, wo[:, 4 * q:4 * q + 4, cb * 512:(cb + 1) * 512], st[:], [st], [wo])
            kb.dma(kb.sp, ggp[:], gg_s[0:128, :], [gg_s], [ggp])
            kb.dma(kb.sp, ggs[:], gg_s[128:160, :], [gg_s], [ggs])
            for ti in range(9):
                rows = 128 if ti < 8 else 32
                t0 = ti * 128
                xt = xR.get()
                kb.dma(kb.sp, xt[0:rows, :], xp[OWN0 + t0:OWN0 + t0 + 128, :] if ti < 8 else xs[:, :],
                       [xp if ti < 8 else xs], [xt])
                po = poR.get()
                for cb in range(4):
                    for k in range(16):
                        kb.mm(po[0:rows, cb * 512:(cb + 1) * 512], merged[:, k, t0:t0 + rows],
                              wo[:, k, cb * 512:(cb + 1) * 512], [merged, wo], [po], start=(k == 0), stop=(k == 15),
                              signal=(k == 15 and cb == 3))
                st = stR.get()
                kb.actf(junk[0:rows, :], po[0:rows, :], AF.Square, [po], [junk, st], accum_out=st[0:rows, 0:1])
                kb.ts(kb.dve, st[0:rows, 1:2], st[0:rows, 0:1], 1.0 / 2048, 1e-6, ALU.mult, ALU.add, [st], [st])
                kb.actf(st[0:rows, 2:3], st[0:rows, 1:2], AF.Sqrt, [st], [st])
                kb.op(kb.dve, lambda e: e.reciprocal(st[0:rows, 3:4], st[0:rows, 2:3]), [st], [st])
                ot = oR.get()
                gg = ggp if ti < 8 else ggs
                kb.stt(ot[0:rows, :], po[0:rows, :], st[0:rows, 3:4], gg[0:rows, :], ALU.mult, ALU.mult,
                       [po, st, gg], [ot])
                kb.tt(kb.dve, ot[0:rows, :], ot[0:rows, :], xt[0:rows, :], ALU.add, [ot, xt], [ot])
                if ti < 8:
                    kb.dma(kb.sp, y_p[t0:t0 + 128, :], ot[:, :], [ot], [y_p])
                else:
                    kb.dma(kb.sp, y_s[:, :], ot[0:32, :], [ot], [y_s])
            kb.barrier()


def gdn_phase(kb, nc, L):
    identf, ident, ones_bf = L["identf"], L["ident"], L["ones_bf"]
    gbc, gbs = L["gbc"], L["gbs"]
    kT_s, qT_s, ktok_s, vtok_s, zT_s, yaT_s = L["kT_s"], L["qT_s"], L["ktok_s"], L["vtok_s"], L["zT_s"], L["yaT_s"]
    sgdn, gdn_p, gdn_s, gdn_g = L["sgdn"], L["gdn_p"], L["gdn_s"], L["gdn_g"]
    NEG = -30000.0
    with ExitStack() as p2:
        def sb(shape, dt, name):
            kb.nbuf += 1
            t = p2.enter_context(nc.sbuf_tensor(name, list(shape), dt))
            return Buf(t, name)

        def ps(shape, dt, name):
            t = p2.enter_context(nc.psum_tensor(name, list(shape), dt))
            return Buf(t, name, psum=True)

        r1 = Ring([ps([128, 512], F32, "p2_a%d" % i) for i in range(3)])
        r2 = Ring([ps([128, 1024], F32, "p2_b%d" % i) for i in range(2)])
        ptb = ps([128, 1024], BF16, "p2_tb")

        U1f = sb([64, 64], F32, "U1f")
        Lsf = sb([64, 64], F32, "Lsf")
        U1b = sb([64, 64], BF16, "U1b")
        Lsb = sb([64, 64], BF16, "Lsb")
        NEG1 = sb([64, 8, 64], BF16, "NEG1")
        NEG2 = sb([64, 8, 64], BF16, "NEG2")
        I8 = sb([64, 8, 64], BF16, "I8")
        onesf = sb([64, 128], F32, "onesf")
        gcol = sb([128, 1], F32, "gcol")
        g1 = sb([1, 128], F32, "g1")
        for t_, base, cm, step in ((U1f, 0, -1, 1), (Lsf, -1, 1, -1)):
            kb.op(kb.pool, lambda e, t_=t_: e.memset(t_[:], 1.0), [], [t_])
            kb.op(kb.pool, lambda e, t_=t_, base=base, cm=cm, step=step: e.affine_select(
                out=t_[:], in_=t_[:], pattern=[[step, 64]], compare_op=ALU.is_ge, fill=0.0, base=base,
                channel_multiplier=cm), [t_], [t_])
        kb.cp(kb.dve, U1b[:], U1f[:], [U1f], [U1b])
        kb.cp(kb.dve, Lsb[:], Lsf[:], [Lsf], [Lsb])
        kb.op(kb.pool, lambda e: e.memset(NEG1[:], NEG), [], [NEG1])
        kb.op(kb.pool, lambda e: e.affine_select(out=NEG1[:], in_=NEG1[:], pattern=[[0, 8], [1, 64]], compare_op=ALU.is_ge,
                                                 fill=0.0, base=0, channel_multiplier=-1), [NEG1], [NEG1])
        kb.op(kb.pool, lambda e: e.memset(NEG2[:], NEG), [], [NEG2])
        kb.op(kb.pool, lambda e: e.affine_select(out=NEG2[:], in_=NEG2[:], pattern=[[0, 8], [-1, 64]], compare_op=ALU.is_ge,
                                                 fill=0.0, base=-1, channel_multiplier=1), [NEG2], [NEG2])
        kb.cp(kb.dve, I8[:], ident[0:64, 0:64].unsqueeze(1).to_broadcast([64, 8, 64]), [ident], [I8])
        kb.op(kb.pool, lambda e: e.memset(onesf[:], 1.0), [], [onesf])
        kb.dma(kb.sp, g1[:], gdn_g[:], [gdn_g], [g1])
        pp = r1.get()
        kb.tr(pp[:, 0:1], g1[:], identf[0:1, 0:1], [g1, identf], [pp])
        kb.cp(kb.dve, gcol[:], pp[:, 0:1], [pp], [gcol])

        kTg = sb([128, 4, NTOK], BF16, "kTg")
        qTg = sb([128, 4, 1056], BF16, "qTg")
        zTg = sb([128, 8, 1056], BF16, "zTg")
        yaT = sb([128, 8, 1056], BF16, "yaT")
        Sf = sb([128, 8, 128], F32, "Sf")
        Sb = sb([128, 8, 128], BF16, "Sb")
        Sd = sb([128, 8, 128], F32, "Sd")
        NSLOT = 3

        class Slot:
            pass

        slots = []
        for i in range(NSLOT):
            S_ = Slot()
            for nm, shape, dt in (("ktk", [64, 4, 128], BF16), ("vtk", [64, 8, 128], BF16), ("rg1", [64, 8, 64], BF16),
                                  ("rg2", [64, 8, 64], BF16), ("lnb", [64, 8], F32), ("lnbb", [64, 8, 64], BF16),
                                  ("E1", [64, 8, 64], BF16), ("ex", [64, 16], F32), ("gl", [128, 8], F32),
                                  ("A_", [64, 8, 64], BF16), ("Xa", [64, 8, 64], BF16), ("Xb", [64, 8, 64], BF16),
                                  ("XTa", [64, 8, 64], BF16), ("XTb", [64, 8, 64], BF16), ("TTa", [64, 8, 64], BF16),
                                  ("TTb", [64, 8, 64], BF16), ("ncbe", [64, 8], F32), ("nTC", [64, 8, 64], BF16),
                                  ("bv", [64, 8, 128], BF16), ("ktl", [64, 8, 128], BF16), ("ub", [64, 8, 128], BF16),
                                  ("deg", [64, 8, 64], BF16), ("qd", [128, 8, 64], BF16), ("E2T", [64, 8, 64], BF16),
                                  ("MT", [64, 8, 64], BF16)):
                setattr(S_, nm, sb(shape, dt, "%s_%d" % (nm, i)))
            slots.append(S_)

        def T2(name, shape, dt, n=2):
            return Ring([sb(shape, dt, "%s%d" % (name, i)) for i in range(n)])

        kSsR = T2("kSs", [64, 8, 128], BF16)
        usR = T2("us", [64, 8, 128], BF16)
        osqR = T2("osq", [64, 8, 128], F32, 1)
        stR = T2("ost", [64, 32], F32)
        onR = T2("on", [64, 8, 128], BF16)

        def gen_T(S_, C, nlev, own, kcs, qcs, gch, bch, gb_b, ctx):
            R = slice(0, C)
            ktk, vtk = S_.ktk[R, :, :], S_.vtk[R, :, :]

            def v3(t):
                return t[R, :, 0:C]

            def bh(ap2, X):
                return ap2.unsqueeze(1).to_broadcast([C, 8, X])

            def bl(ap2, X):
                return ap2.unsqueeze(2).to_broadcast([C, 8, X])

            rg1, lnb, lnbb, E1, ex, gl = S_.rg1, S_.lnb, S_.lnbb, S_.E1, S_.ex, S_.gl
            kb.tt(kb.dve, v3(rg1), bh(Lsf[R, 0:C], C), bl(gch, C), ALU.mult, [Lsf, gb_b], [rg1])
            kb.actf(lnb[R, :], bch, AF.Ln, [gb_b], [lnb])
            kb.cp(kb.act, v3(lnbb), bl(lnb[R, :], C), [lnb], [lnbb])
            pG = r1.get()
            pGv = pG[R, 0:8 * C].rearrange("p (h c) -> p h c", c=C)
            kb.mm(pGv, U1b[R, 0:C], v3(rg1), [U1b, rg1], [pG], start=True, stop=False)
            kb.mm(pGv, ident[R, 0:C], v3(lnbb), [ident, lnbb], [pG], start=False, stop=False)
            kb.mm(pGv, ident[R, 0:C], v3(NEG1), [ident, NEG1], [pG], start=False, stop=True)
            kb.actf(v3(E1), pGv, AF.Exp, [pG], [E1])
            psm = r1.get()
            kb.mm(psm[R, 0:8], U1f[R, 0:C], gch, [U1f, gb_b], [psm], signal=False)
            kb.mm(psm[R, 8:16], Lsf[R, 0:C], gch, [Lsf, gb_b], [psm], signal=False)
            kb.mm(psm[:, 16:24], onesf[R, :], gch, [onesf, gb_b], [psm], signal=True)
            kb.actf(ex[R, :], psm[R, 0:16], AF.Exp, [psm], [ex])
            kb.actf(gl[:, :], psm[:, 16:24], AF.Exp, [psm], [gl])
            eG, etail = ex[R, 0:8], ex[R, 8:16]
            yield
            pK = r1.get()
            pKv = pK[R, 0:8 * C].rearrange("p (h c) -> p h c", c=C)
            for j in range(4):
                kb.mm(pKv[:, j, :], kcs[j], kcs[j], [kTg], [pK], signal=(not own and j == 3))
            if own:
                for j in range(4):
                    kb.mm(pKv[:, 4 + j, :], kcs[j], qcs[j], [kTg, qTg], [pK], signal=(j == 3))
            A_ = S_.A_
            kb.tt(kb.dve, v3(A_).rearrange("p (j t) c -> p j t c", t=2),
                  pKv[:, 0:4, :].unsqueeze(2).to_broadcast([C, 4, 2, C]),
                  v3(E1).rearrange("p (j t) c -> p j t c", t=2), ALU.mult, [pK, E1], [A_])
            if own:
                rg2, E2T, MT = S_.rg2, S_.E2T, S_.MT
                kb.tt(kb.dve, v3(rg2), bh(U1f[R, 0:C], C), bl(gch, C), ALU.mult, [U1f, gb_b], [rg2])
                pG2 = r1.get()
                pG2v = pG2[R, 0:8 * C].rearrange("p (h c) -> p h c", c=C)
                kb.mm(pG2v, Lsb[R, 0:C], v3(rg2), [Lsb, rg2], [pG2], start=True, stop=False)
                kb.mm(pG2v, ident[R, 0:C], v3(NEG2), [ident, NEG2], [pG2], start=False, stop=True)
                kb.actf(v3(E2T), pG2v, AF.Exp, [pG2], [E2T])
                kb.tt(kb.dve, v3(MT).rearrange("p (j t) c -> p j t c", t=2),
                      pKv[:, 4:8, :].unsqueeze(2).to_broadcast([C, 4, 2, C]),
                      v3(E2T).rearrange("p (j t) c -> p j t c", t=2), ALU.mult, [pK, E2T], [MT])
            yield
            ptv = ptb[R, 0:8 * C].rearrange("p (h c) -> p h c", c=C)
            for h in range(8):
                kb.tr(ptv[:, h, :], A_[R, h, 0:C], ident[R, 0:C], [A_, ident], [ptb], signal=(h == 7))
            XT, TT = S_.XTa, S_.TTa
            kb.cp(kb.act, v3(XT), ptv, [ptb], [XT])
            kb.tt(kb.dve, v3(TT), v3(I8), ptv, ALU.subtract, [I8, ptb], [TT])
            X = A_
            yield
            for lev in range(1, nlev + 1):
                pXa = r1.get()
                pXav = pXa[R, 0:8 * C].rearrange("p (h c) -> p h c", c=C)
                for h in range(8):
                    kb.mm(pXav[:, h, :], XT[R, h, 0:C], X[R, h, 0:C], [XT, X], [pXa], signal=(h == 7))
                Xn = S_.Xa if (lev % 2 == 1) else S_.Xb
                kb.cp(kb.act, v3(Xn), pXav, [pXa], [Xn])
                if lev < nlev:
                    pXb = r1.get()
                    pXbv = pXb[R, 0:8 * C].rearrange("p (h c) -> p h c", c=C)
                    for h in range(8):
                        kb.mm(pXbv[:, h, :], X[R, h, 0:C], XT[R, h, 0:C], [XT, X], [pXb], signal=(h == 7))
                    XTn = S_.XTb if (lev % 2 == 1) else S_.XTa
                    kb.cp(kb.act, v3(XTn), pXbv, [pXb], [XTn])
                yield
                pT = r1.get()
                pTv = pT[R, 0:8 * C].rearrange("p (h c) -> p h c", c=C)
                for h in range(8):
                    kb.mm(pTv[:, h, :], Xn[R, h, 0:C], TT[R, h, 0:C], [Xn, TT], [pT], signal=(h == 7))
                TTn = S_.TTb if (lev % 2 == 1) else S_.TTa
                kb.tt(kb.dve, v3(TTn), pTv, v3(TT), ALU.add, [pT, TT], [TTn])
                X, TT = Xn, TTn
                if lev < nlev:
                    XT = XTn
                yield
            ncbe, nTC, bv, ktl, ub = S_.ncbe, S_.nTC, S_.bv, S_.ktl, S_.ub
            kb.stt(ncbe[R, :], eG, -1.0, bch, ALU.mult, ALU.mult, [ex, gb_b], [ncbe])
            kb.tt(kb.dve, v3(nTC), v3(TT), bl(ncbe[R, :], C), ALU.mult, [TT, ncbe], [nTC])
            kb.tt(kb.dve, bv[R, :, :], vtk, bl(bch, 128), ALU.mult, [S_.vtk, gb_b], [bv])
            kb.tt(kb.dve, ktl[R, :, :].rearrange("p (j t) d -> p j t d", t=2),
                  ktk.unsqueeze(2).to_broadcast([C, 4, 2, 128]),
                  etail.unsqueeze(2).to_broadcast([C, 8, 128]).rearrange("p (j t) d -> p j t d", t=2), ALU.mult,
                  [S_.ktk, ex], [ktl])
            pU = r2.get()
            pUv = pU[R, :].rearrange("p (h d) -> p h d", d=128)
            for h in range(8):
                kb.mm(pUv[:, h, :], TT[R, h, 0:C], bv[R, h, :], [TT, bv], [pU], signal=(h == 7))
            kb.cp(kb.act, ub[R, :, :], pUv, [pU], [ub])
            yield
            if own:
                deg, qd = S_.deg, S_.qd
                kb.tt(kb.dve, v3(deg), v3(I8), bl(eG, C), ALU.mult, [I8, ex], [deg])
                pE = r1.get()
                pEv = pE[:, 0:8 * C].rearrange("p (h c) -> p h c", c=C)
                kb.mm(pEv, ones_bf[R, :], v3(deg), [ones_bf, deg], [pE])
                qdv = qd[:, :, 0:C]
                for j in range(4):
                    kb.tt(kb.dve, qdv[:, 2 * j:2 * j + 2, :], pEv[:, 2 * j:2 * j + 2, :],
                          qcs[j].unsqueeze(1).to_broadcast([128, 2, C]), ALU.mult, [pE, qTg], [qd])
                yield

        def gen_scan(S_, C, own, kcs, zTs, yout, pre=None, post=None):
            R = slice(0, C)
            if pre is not None:
                pre()
            nTC, ub, ktl, gl, qd, MT = S_.nTC, S_.ub, S_.ktl, S_.gl, S_.qd, S_.MT
            pS = r2.get()
            pSv = pS[R, :].rearrange("p (h d) -> p h d", d=128)
            for h in range(8):
                kb.mm(pSv[:, h, :], kcs[h // 2], Sb[:, h, :], [kTg, Sb], [pS], signal=(h == 7))
            kSs = kSsR.get()
            kb.cp(kb.act, kSs[R, :, :], pSv, [pS], [kSs])
            yield
            pU2 = r2.get()
            pU2v = pU2[R, :].rearrange("p (h d) -> p h d", d=128)
            for h in range(8):
                kb.mm(pU2v[:, h, :], nTC[R, h, 0:C], kSs[R, h, :], [nTC, kSs], [pU2], signal=(h == 7))
            us = usR.get()
            kb.tt(kb.dve, us[R, :, :], pU2v, ub[R, :, :], ALU.add, [pU2, ub], [us])
            yield
            pD = r2.get()
            pDv = pD[:, :].rearrange("p (h d) -> p h d", d=128)
            for h in range(8):
                kb.mm(pDv[:, h, :], ktl[R, h, :], us[R, h, :], [ktl, us], [pD], signal=(h == 7))
            if own:
                pO = r2.get()
                pOv = pO[R, :].rearrange("p (h d) -> p h d", d=128)
                for h in range(8):
                    kb.mm(pOv[:, h, :], qd[:, h, 0:C], Sb[:, h, :], [qd, Sb], [pO], start=True, stop=False)
                    kb.mm(pOv[:, h, :], MT[R, h, 0:C], us[R, h, :], [MT, us], [pO], start=False, stop=True,
                          signal=(h == 7))
            kb.tt(kb.dve, Sd[:, :, :], Sf[:, :, :], gl[:, :].unsqueeze(2).to_broadcast([128, 8, 128]), ALU.mult,
                  [Sf, gl], [Sd])
            kb.tt(kb.dve, Sf[:, :, :], Sd[:, :, :], pDv, ALU.add, [Sd, pD], [Sf])
            kb.cp(kb.act, Sb[:, :, :], Sf[:, :, :], [Sf], [Sb])
            if own:
                osq = osqR.get()
                st = stR.get()
                kb.actf(osq[R, :, :], pOv, AF.Square, [pO], [osq])
                kb.op(kb.dve, lambda e: e.tensor_reduce(out=st[R, 0:8], in_=osq[R, :, :], axis=AX.X, op=ALU.add),
                      [osq], [st])
                kb.ts(kb.dve, st[R, 8:16], st[R, 0:8], 1.0 / 128, 1e-6, ALU.mult, ALU.add, [st], [st])
                kb.actf(st[R, 16:24], st[R, 8:16], AF.Sqrt, [st], [st])
                kb.op(kb.dve, lambda e: e.reciprocal(st[R, 24:32], st[R, 16:24]), [st], [st])
                on = onR.get()
                kb.tt(kb.dve, on[R, :, :], pOv, st[R, 24:32].unsqueeze(2).to_broadcast([C, 8, 128]), ALU.mult,
                      [pO, st], [on])
            if post is not None:
                post()
            yield
            if own:
                ptv2 = ptb[:, 0:8 * C].rearrange("p (h c) -> p h c", c=C)
                for h in range(8):
                    kb.tr(ptv2[:, h, :], on[R, h, :], ident[R, 0:C], [on, ident], [ptb], signal=(h == 7))
                kb.stt(yout, ptv2, gcol[:, 0:1], zTs, ALU.mult, ALU.mult, [ptb, gcol, zTg], [yaT])
                yield

        def step(gen):
            try:
                next(gen)
                return True
            except StopIteration:
                return False

        for g in range(4):
            kb.dma(kb.sp, kTg[:], kT_s[4 * g:4 * g + 4].rearrange("j p t -> p j t"), [kT_s], [kTg])
            kb.dma(kb.sp, qTg[:], qT_s[4 * g:4 * g + 4].rearrange("j p t -> p j t"), [qT_s], [qTg])
            kb.dma(kb.sp, zTg[:], zT_s[8 * g:8 * g + 8].rearrange("j p t -> p j t"), [zT_s], [zTg])
            kb.op(kb.pool, lambda e: e.memset(Sf[:], 0.0), [], [Sf])
            kb.op(kb.pool, lambda e: e.memset(Sb[:], 0.0), [], [Sb])
            items = []
            for c in range(32):
                own = c >= 16
                o0 = (c - 16) * 64
                items.append(dict(C=64, nlev=5, own=own, tok0=c * 64,
                                  kcs=[kTg[:, j, c * 64:(c + 1) * 64] for j in range(4)],
                                  qcs=[qTg[:, j, o0:o0 + 64] for j in range(4)] if own else None,
                                  gch=gbc[:, c, 32 + 8 * g:40 + 8 * g], bch=gbc[:, c, 8 * g:8 * g + 8], gb=gbc,
                                  zTs=zTg[:, :, o0:o0 + 64] if own else None,
                                  yout=yaT[:, :, o0:o0 + 64] if own else None, pre=None,
                                  post=(lambda g=g: kb.dma(kb.sp, gdn_p[8 * g:8 * g + 8].rearrange("h p d -> p h d"),
                                                           Sf[:], [Sf], [gdn_p])) if c == 31 else None))
            for s_ in range(4):
                t0 = SMP0 + 8 * s_
                o0 = 1024 + 8 * s_

                def pre(s_=s_, g=g):
                    kb.dma(kb.sp, Sf[:], sgdn[s_ * 32 + 8 * g:s_ * 32 + 8 * g + 8].rearrange("h p d -> p h d"), [sgdn], [Sf])
                    kb.cp(kb.act, Sb[:, :, :], Sf[:, :, :], [Sf], [Sb])

                def post(s_=s_, g=g):
                    kb.dma(kb.sp, gdn_s[s_ * 32 + 8 * g:s_ * 32 + 8 * g + 8].rearrange("h p d -> p h d"), Sf[:], [Sf], [gdn_s])

                items.append(dict(C=8, nlev=2, own=True, tok0=t0, kcs=[kTg[:, j, t0:t0 + 8] for j in range(4)],
                                  qcs=[qTg[:, j, o0:o0 + 8] for j in range(4)],
                                  gch=gbs[:, s_, 32 + 8 * g:40 + 8 * g], bch=gbs[:, s_, 8 * g:8 * g + 8], gb=gbs,
                                  zTs=zTg[:, :, o0:o0 + 8], yout=yaT[:, :, o0:o0 + 8], pre=pre, post=post))
            n_items = len(items)

            def make_T(i):
                it = items[i]
                S_ = slots[i % NSLOT]
                C = it["C"]
                kb.dma(kb.sp, S_.ktk[0:C], ktok_s[it["tok0"]:it["tok0"] + C, 4 * g:4 * g + 4, :], [ktok_s], [S_.ktk])
                kb.dma(kb.sp, S_.vtk[0:C], vtok_s[it["tok0"]:it["tok0"] + C, 8 * g:8 * g + 8, :], [vtok_s], [S_.vtk])
                return gen_T(S_, C, it["nlev"], it["own"], it["kcs"], it["qcs"], it["gch"], it["bch"], it["gb"], None)

            tg = {0: make_T(0)}
            while step(tg[0]):
                pass
            if n_items > 1:
                tg[1] = make_T(1)
            for i in range(n_items):
                it = items[i]
                if i + 2 < n_items:
                    tg[i + 2] = make_T(i + 2)
                sg = gen_scan(slots[i % NSLOT], it["C"], it["own"], it["kcs"], it["zTs"], it["yout"], it["pre"], it["post"])
                older = tg.get(i + 1)
                younger = tg.get(i + 2)
                alive_s, alive_o, rnd = True, older is not None, 0
                while alive_s or alive_o:
                    if alive_o:
                        alive_o = step(older)
                    if younger is not None and rnd % 2 == 0:
                        if not step(younger):
                            younger = None
                    if alive_s and rnd % 3 == 0:
                        alive_s = step(sg)
                    if not alive_o and alive_s:
                        alive_s = step(sg)
                    rnd += 1
                tg.pop(i, None)
            kb.dma(kb.sp, yaT_s[8 * g:8 * g + 8].rearrange("j p t -> p j t"), yaT[:], [yaT], [yaT_s])
        kb.barrier()


def make_in_maps(inp, phases=(0, 1, 2, 3, 4)):
    f = np.float32
    xp_all = np.asarray(inp["x_prompt"], f)
    maps = []
    ck = np.ascontiguousarray(np.asarray(inp["cache_k"], f)[0]).reshape(2560 * 16, 8 * 256)
    cv = np.ascontiguousarray(np.asarray(inp["cache_v"], f)[0]).reshape(2560 * 16, 8 * 256)
    cki = np.ascontiguousarray(np.asarray(inp["cache_kidx"], f)[0]).reshape(2560 * 16, 8 * 128)
    shared = {
        "w_ada": np.ascontiguousarray(np.asarray(inp["w_ada"], f)[0]),
        "b_ada": np.ascontiguousarray(np.asarray(inp["b_ada"], f)[0]).reshape(1, 6144),
        "pre_g": np.ascontiguousarray(np.asarray(inp["pre_norm_g"], f)[0]).reshape(16, 128),
        "w_in": np.ascontiguousarray(np.asarray(inp["w_in"], f)[0]),
        "conv_w": np.ascontiguousarray(np.asarray(inp["conv_w"], f)[0]),
        "a_log": np.ascontiguousarray(np.asarray(inp["a_log"], f)[0]).reshape(32, 1),
        "dt_bias": np.ascontiguousarray(np.asarray(inp["dt_bias"], f)[0]).reshape(32, 1),
        "gdn_g": np.ascontiguousarray(np.asarray(inp["gdn_norm_g"], f)[0]).reshape(1, 128),
        "w_pa": np.ascontiguousarray(np.asarray(inp["w_pa"], f)[0]),
        "w_pb": np.ascontiguousarray(np.asarray(inp["w_pb"], f)[0]),
        "w_out": np.ascontiguousarray(np.asarray(inp["w_out"], f)[0]),
        "post_g": np.ascontiguousarray(np.asarray(inp["post_norm_g"], f)[0]).reshape(1, 2048),
    }
    for c in range(8):
        b, half = c // 2, c % 2
        x = xp_all[b]
        xr = x if half == 1 else np.concatenate([x[1024:], x[:1024]], axis=0)
        m = dict(shared)
        m["xp"] = np.ascontiguousarray(xr)
        m["xs"] = np.ascontiguousarray(np.asarray(inp["x_sample"], f)[4 * c:4 * c + 4].reshape(32, 2048))
        m["cc"] = np.ascontiguousarray(np.concatenate([np.asarray(inp["c_prompt"], f)[b:b + 1],
                                                       np.asarray(inp["c_sample"], f)[4 * c:4 * c + 4]], axis=0))
        m["flag"] = np.full((128, 1), float(half), f)
        m["sconv"] = np.ascontiguousarray(np.asarray(inp["state_conv"], f)[0, 4 * c:4 * c + 4].reshape(12, 8192))
        if 2 in phases:
            m["sgdn"] = np.ascontiguousarray(np.asarray(inp["state_gdn"], f)[0, 4 * c:4 * c + 4].reshape(128, 128, 128))
        if 3 in phases:
            m["cache_k"] = ck
            m["cache_v"] = cv
            m["cache_ki"] = cki
            m["pt"] = np.ascontiguousarray(np.asarray(inp["page_table"], np.int32)[4 * c:4 * c + 4])
        maps.append(m)
    return maps


_CACHE = {}


def kernel(x_prompt, x_sample, c_prompt, c_sample, cache_k, cache_v, cache_kidx, state_gdn, state_conv, page_table,
           w_ada, b_ada, pre_norm_g, w_in, conv_w, a_log, dt_bias, gdn_norm_g, w_pa, w_pb, w_out, post_norm_g):
    inp = dict(x_prompt=x_prompt, x_sample=x_sample, c_prompt=c_prompt, c_sample=c_sample, cache_k=cache_k,
               cache_v=cache_v, cache_kidx=cache_kidx, state_gdn=state_gdn, state_conv=state_conv,
               page_table=page_table, w_ada=w_ada, b_ada=b_ada, pre_norm_g=pre_norm_g, w_in=w_in, conv_w=conv_w,
               a_log=a_log, dt_bias=dt_bias, gdn_norm_g=gdn_norm_g, w_pa=w_pa, w_pb=w_pb, w_out=w_out,
               post_norm_g=post_norm_g)
    if "kb" not in _CACHE:
        _CACHE["kb"] = build()
    kbd = _CACHE["kb"]
    maps = make_in_maps(inp)
    res = run_bass_kernel_spmd(kbd.nc, maps, core_ids=list(range(8)))
    R = res.results
    f = np.float32
    y_p = np.zeros((4, 2048, 2048), f)
    k_p = np.zeros((1, 4, 2048, 2, 128), f)
    v_p = np.zeros((1, 4, 2048, 2, 128), f)
    ki_p = np.zeros((1, 4, 2048, 128), f)
    gdn_p = np.zeros((1, 4, 32, 128, 128), f)
    conv_p = np.zeros((1, 4, 3, 8192), f)
    y_s = np.zeros((32, 8, 2048), f)
    k_s = np.zeros((1, 32, 8, 2, 128), f)
    v_s = np.zeros((1, 32, 8, 2, 128), f)
    ki_s = np.zeros((1, 32, 8, 128), f)
    gdn_s = np.zeros((1, 32, 32, 128, 128), f)
    conv_s = np.zeros((1, 32, 3, 8192), f)
    for c in range(8):
        b, half = c // 2, c % 2
        r = R[c]
        own = slice(1024, 2048) if half == 1 else slice(0, 1024)
        y_p[b, own] = np.asarray(r["y_p"], f)
        k_p[0, b, own] = np.asarray(r["k_p"], f).reshape(1024, 2, 128)
        v_p[0, b, own] = np.asarray(r["v_p"], f).reshape(1024, 2, 128)
        ki_p[0, b, own] = np.asarray(r["ki_p"], f)
        if half == 1:
            gdn_p[0, b] = np.asarray(r["gdn_p"], f)
            conv_p[0, b] = np.asarray(r["conv_p"], f)
        sl = slice(4 * c, 4 * c + 4)
        y_s[sl] = np.asarray(r["y_s"], f).reshape(4, 8, 2048)
        k_s[0, sl] = np.asarray(r["k_s"], f).reshape(4, 8, 2, 128)
        v_s[0, sl] = np.asarray(r["v_s"], f).reshape(4, 8, 2, 128)
        ki_s[0, sl] = np.asarray(r["ki_s"], f).reshape(4, 8, 128)
        gdn_s[0, sl] = np.asarray(r["gdn_s"], f).reshape(4, 32, 128, 128)
        conv_s[0, sl] = np.asarray(r["conv_s"], f).reshape(4, 3, 8192)
    return (y_p, y_s, k_p, v_p, ki_p, gdn_p, conv_p, k_s, v_s, ki_s, gdn_s, conv_s)
```
